# Optimizing a Trainium2 kernel written in Bass

```python
import jax, jax.numpy as jnp
from jax import lax
import numpy as np

D_MODEL = 1024
BATCH = 8
SEQ = 4096
DEPTH = 1

GRID_W = 64
Q_BLOCK = 128
ROPE_THETA = 10000.0
EPS = 1e-6

MLA_HEADS = 8
MLA_NOPE_DIM = 64
MLA_ROPE_DIM = 32
MLA_V_DIM = 64
Q_LORA_RANK = 256
KV_LORA_RANK = 128

GQA_HEADS = 8
GQA_KV_HEADS = 2
GQA_HEAD_DIM = 64

IN_COLS = (Q_LORA_RANK, KV_LORA_RANK, MLA_ROPE_DIM,
           GQA_HEADS * GQA_HEAD_DIM, GQA_KV_HEADS * GQA_HEAD_DIM, GQA_KV_HEADS * GQA_HEAD_DIM)
D_IN = sum(IN_COLS)
MLA_OUT = MLA_HEADS * MLA_V_DIM
GQA_OUT = GQA_HEADS * GQA_HEAD_DIM
D_MIX = MLA_OUT + GQA_OUT

D_FF = 2816
N_MOD = 9

kernel_name = "hybrid_mla_gqa_macaron_adaln_encoder"


def rms_norm(x, g):
    xf = x.astype(jnp.float32)
    y = xf * lax.rsqrt(jnp.mean(xf * xf, axis=-1, keepdims=True) + EPS)
    return (y * g.astype(jnp.float32)).astype(x.dtype)


def modulate(h, shift, scale):
    return h * (1 + scale[:, None, :]) + shift[:, None, :]


def swiglu(h, w_gu, w_down):
    a, b = jnp.split(h @ w_gu, 2, axis=-1)
    return (jax.nn.silu(a) * b) @ w_down


def axial_angles(seq_len, dim):
    rows = seq_len // GRID_W
    row = jnp.repeat(jnp.arange(rows), GRID_W).astype(jnp.float32)
    col = jnp.tile(jnp.arange(GRID_W), rows).astype(jnp.float32)
    axis_dim = dim // 2
    inv_freq = ROPE_THETA ** (-(jnp.arange(axis_dim // 2, dtype=jnp.float32) * 2.0 / axis_dim))
    return row[:, None] * inv_freq[None, :], col[:, None] * inv_freq[None, :]


def rotate(x, ang):
    xf = x.astype(jnp.float32)
    x1, x2 = jnp.split(xf, 2, axis=-1)
    cos = jnp.cos(ang)[None, :, None, :]
    sin = jnp.sin(ang)[None, :, None, :]
    return jnp.concatenate([x1 * cos - x2 * sin, x1 * sin + x2 * cos], axis=-1).astype(x.dtype)


def axial_rope(x):
    seq_len, dim = x.shape[1], x.shape[-1]
    ang_row, ang_col = axial_angles(seq_len, dim)
    half = dim // 2
    return jnp.concatenate([rotate(x[..., :half], ang_row), rotate(x[..., half:], ang_col)], axis=-1)


def blocked_attention(q, k, v, scale):
    B, S, Hk, G, dk = q.shape
    nb = S // Q_BLOCK
    qb = q.reshape(B, nb, Q_BLOCK, Hk, G, dk).transpose(1, 0, 2, 3, 4, 5)

    def one_block(qi):
        s = jnp.einsum('bqhgd,bshd->bhgqs', qi, k).astype(jnp.float32) * scale
        p = jax.nn.softmax(s, axis=-1).astype(v.dtype)
        return jnp.einsum('bhgqs,bshe->bqhge', p, v)

    o = lax.map(one_block, qb)
    return o.transpose(1, 0, 2, 3, 4, 5).reshape(B, S, Hk * G * v.shape[-1])


def mla_group(q_lat, kv_lat, k_rope, g_q_lat, w_uq, g_kv_lat, w_ukv):
    B, S, _ = q_lat.shape
    q = (rms_norm(q_lat, g_q_lat) @ w_uq).reshape(B, S, MLA_HEADS, MLA_NOPE_DIM + MLA_ROPE_DIM)
    q_nope, q_pe = q[..., :MLA_NOPE_DIM], q[..., MLA_NOPE_DIM:]
    kv = (rms_norm(kv_lat, g_kv_lat) @ w_ukv).reshape(B, S, MLA_HEADS, MLA_NOPE_DIM + MLA_V_DIM)
    k_nope, v = kv[..., :MLA_NOPE_DIM], kv[..., MLA_NOPE_DIM:]
    q_pe = axial_rope(q_pe)
    k_pe = axial_rope(k_rope.reshape(B, S, 1, MLA_ROPE_DIM))
    q_full = jnp.concatenate([q_nope, q_pe], axis=-1)[:, :, :, None, :]
    k_full = jnp.concatenate([k_nope, jnp.broadcast_to(k_pe, (B, S, MLA_HEADS, MLA_ROPE_DIM))], axis=-1)
    return blocked_attention(q_full, k_full, v, (MLA_NOPE_DIM + MLA_ROPE_DIM) ** -0.5)


def gqa_group(q_in, k_in, v_in, g_qhead, g_khead):
    B, S, _ = q_in.shape
    q = axial_rope(rms_norm(q_in.reshape(B, S, GQA_HEADS, GQA_HEAD_DIM), g_qhead))
    k = axial_rope(rms_norm(k_in.reshape(B, S, GQA_KV_HEADS, GQA_HEAD_DIM), g_khead))
    v = v_in.reshape(B, S, GQA_KV_HEADS, GQA_HEAD_DIM)
    q = q.reshape(B, S, GQA_KV_HEADS, GQA_HEADS // GQA_KV_HEADS, GQA_HEAD_DIM)
    return blocked_attention(q, k, v, GQA_HEAD_DIM ** -0.5)


def setup_inputs(seed: int = 0) -> dict:
    key = jax.random.key(seed)
    ks = jax.random.split(key, 24)
    D, L = D_MODEL, DEPTH

    def w(k, shape, fan_in, mult=1.0):
        return jax.random.normal(k, shape, jnp.float32) * (fan_in ** -0.5) * mult

    def gain(k, shape):
        return 1.0 + 0.02 * jax.random.normal(k, shape, jnp.float32)

    return {
        "x": jax.random.normal(ks[0], (BATCH, SEQ, D), jnp.float32),
        "c": jax.random.normal(ks[1], (BATCH, D), jnp.float32),
        "w_ada": w(ks[2], (L, D, N_MOD * D), D, 0.5),
        "b_ada": 0.02 * jax.random.normal(ks[3], (L, N_MOD * D), jnp.float32),
        "g_ffn1": gain(ks[4], (L, D)),
        "w1_gu": w(ks[5], (L, D, 2 * D_FF), D),
        "w1_down": w(ks[6], (L, D_FF, D), D_FF),
        "g_mix": gain(ks[7], (L, D)),
        "w_in": w(ks[8], (L, D, D_IN), D),
        "g_q_lat": gain(ks[9], (L, Q_LORA_RANK)),
        "w_uq": w(ks[10], (L, Q_LORA_RANK, MLA_HEADS * (MLA_NOPE_DIM + MLA_ROPE_DIM)), Q_LORA_RANK),
        "g_kv_lat": gain(ks[11], (L, KV_LORA_RANK)),
        "w_ukv": w(ks[12], (L, KV_LORA_RANK, MLA_HEADS * (MLA_NOPE_DIM + MLA_V_DIM)), KV_LORA_RANK),
        "g_qhead": gain(ks[13], (L, GQA_HEAD_DIM)),
        "g_khead": gain(ks[14], (L, GQA_HEAD_DIM)),
        "g_out_mla": gain(ks[15], (L, MLA_OUT)),
        "g_out_gqa": gain(ks[16], (L, GQA_OUT)),
        "w_out": w(ks[17], (L, D_MIX, D), D_MIX),
        "g_ffn2": gain(ks[18], (L, D)),
        "w2_gu": w(ks[19], (L, D, 2 * D_FF), D),
        "w2_down": w(ks[20], (L, D_FF, D), D_FF),
        "g_final": gain(ks[21], (D,)),
    }


def reference(x, c, w_ada, b_ada, g_ffn1, w1_gu, w1_down, g_mix, w_in, g_q_lat, w_uq,
              g_kv_lat, w_ukv, g_qhead, g_khead, g_out_mla, g_out_gqa, w_out,
              g_ffn2, w2_gu, w2_down, g_final):
    offs = [int(o) for o in np.cumsum(IN_COLS)[:-1]]
    c_act = jax.nn.silu(c)
    for l in range(DEPTH):
        mod = c_act @ w_ada[l] + b_ada[l]
        (sh_f1, sc_f1, gt_f1, sh_m, sc_m, gt_m, sh_f2, sc_f2, gt_f2) = jnp.split(mod, N_MOD, axis=-1)

        h = modulate(rms_norm(x, g_ffn1[l]), sh_f1, sc_f1)
        x = x + 0.5 * gt_f1[:, None, :] * swiglu(h, w1_gu[l], w1_down[l])

        h = modulate(rms_norm(x, g_mix[l]), sh_m, sc_m)
        z = h @ w_in[l]
        q_lat, kv_lat, k_rope, q_g, k_g, v_g = jnp.split(z, offs, axis=-1)
        o_mla = mla_group(q_lat, kv_lat, k_rope, g_q_lat[l], w_uq[l], g_kv_lat[l], w_ukv[l])
        o_gqa = gqa_group(q_g, k_g, v_g, g_qhead[l], g_khead[l])
        o = jnp.concatenate([rms_norm(o_mla, g_out_mla[l]), rms_norm(o_gqa, g_out_gqa[l])], axis=-1)
        x = x + gt_m[:, None, :] * (o @ w_out[l])

        h = modulate(rms_norm(x, g_ffn2[l]), sh_f2, sc_f2)
        x = x + 0.5 * gt_f2[:, None, :] * swiglu(h, w2_gu[l], w2_down[l])
    return rms_norm(x, g_final)
```

```python
import math
from contextlib import ExitStack

import numpy as np
import concourse.bass as bass
import concourse.mybir as mybir
from concourse.bass_utils import run_bass_kernel_spmd

F32 = mybir.dt.float32
BF16 = mybir.dt.bfloat16
AF = mybir.ActivationFunctionType
ALU = mybir.AluOpType
AX = mybir.AxisListType

D = 1024
DFF = 2816
NJ = DFF // 128
NMOD = 9
GRID_W = 64
THETA = 10000.0
EPS = 1e-6
HM, HG = 8, 8
ARENA_BYTES = 212736


class Buf:
    __slots__ = ("name", "lw", "rd", "dsem", "dcnt", "dseen", "excl")

    def __init__(self, name):
        self.name = name
        self.excl = False
        self.lw = {}
        self.rd = {}
        self.dsem = None
        self.dcnt = 0
        self.dseen = 0


class Op:
    __slots__ = ("eng", "fn", "deps", "dwaits", "signal", "tick", "dbuf")

    def __init__(self, eng, fn):
        self.eng = eng
        self.fn = fn
        self.deps = []
        self.dwaits = []
        self.signal = False
        self.tick = 0
        self.dbuf = None


COMPUTE = ("act", "pool", "dve", "pe")
ENGS = ("sp", "act", "pool", "dve", "pe")


class Sched:
    def __init__(self):
        self.ops = {e: [] for e in ENGS}
        self.bufs = []

    def buf(self, name, excl=False):
        b = Buf(name)
        b.excl = excl
        self.bufs.append(b)
        return b

    def _gather(self, op, r, w):
        for b in r:
            for o in b.lw.values():
                op.deps.append(o)
            if b.excl:
                for e2, o in b.rd.items():
                    if e2 != op.eng:
                        op.deps.append(o)
            if b.dcnt:
                op.dwaits.append((b, b.dcnt))
        for b in w:
            for o in b.lw.values():
                op.deps.append(o)
            for o in b.rd.values():
                op.deps.append(o)
            if b.dcnt:
                op.dwaits.append((b, b.dcnt))

    def op(self, eng, fn, r=(), w=()):
        op = Op(eng, fn)
        self._gather(op, r, w)
        for b in r:
            b.rd[eng] = op
        for b in w:
            b.lw[eng] = op
            b.rd = {}
        self.ops[eng].append(op)
        return op

    def dma(self, eng, fn, track, r=(), w=()):
        op = Op(eng, fn)
        self._gather(op, r, w)
        op.dbuf = track
        track.dcnt += 1
        if track in w:
            track.lw = {}
            track.rd = {}
        self.ops[eng].append(op)
        return op

    def barrier(self):
        last = {}
        for e in COMPUTE:
            last[e] = None
            for o in reversed(self.ops[e]):
                if o.fn is None:
                    break
                if o.dbuf is None:
                    last[e] = o
                    break
        dl = [(b, b.dcnt) for b in self.bufs if b.dcnt > b.dseen]
        for b in self.bufs:
            b.dseen = b.dcnt
            b.lw = {}
            b.rd = {}
        for e in ENGS:
            op = Op(e, None)
            for e2 in COMPUTE:
                o = last[e2]
                while o is not None and o.fn is None:
                    o = None
                if o is not None:
                    op.deps.append(o)
            op.dwaits = list(dl)
            self.ops[e].append(op)

    def emit(self, nc, stack):
        for e in ENGS:
            for op in self.ops[e]:
                for d in op.deps:
                    if d.fn is None:
                        continue
                    if d.eng == "pe" and op.eng == "pe":
                        continue
                    d.signal = True
        esem = {e: stack.enter_context(nc.semaphore("s_" + e)) for e in COMPUTE}
        for e in COMPUTE:
            t = 0
            for op in self.ops[e]:
                if op.signal:
                    t += 1
                    op.tick = t
        for b in self.bufs:
            if b.dcnt:
                b.dsem = stack.enter_context(nc.semaphore("d_" + b.name))
        block = stack.enter_context(nc.Block())
        sched = self

        def run(eng_name):
            def body(eng):
                seen = {}
                for op in sched.ops[eng_name]:
                    need = {}
                    for d in op.deps:
                        if d.fn is None:
                            continue
                        if d.eng == "pe" and eng_name == "pe":
                            continue
                        s = esem[d.eng]
                        if d.tick > need.get(s, (0,))[0]:
                            need[s] = (d.tick,)
                    for b, c in op.dwaits:
                        s = b.dsem
                        if 16 * c > need.get(s, (0,))[0]:
                            need[s] = (16 * c,)
                    for s, (v,) in need.items():
                        if seen.get(s, 0) < v:
                            eng.wait_ge(s, v)
                            seen[s] = v
                    if op.fn is None:
                        continue
                    ins = op.fn(eng)
                    if op.dbuf is not None:
                        ins.then_inc(op.dbuf.dsem, 16)
                    elif op.signal:
                        ins.then_inc(esem[eng_name], 1)
            return body

        block.sync(run("sp"))
        block.scalar(run("act"))
        block.gpsimd(run("pool"))
        block.vector(run("dve"))
        block.tensor(run("pe"))


def build_program(S, debug=False):
    NT = S // 128
    GT = 256
    TPG = GT // 128
    NG = S // GT
    QG = 512
    NQG = S // QG
    nc = bass.Bass("TRN2", target_bir_lowering=False)

    def din(name, shape, dt=F32):
        return nc.dram_tensor(name, list(shape), dt, kind="ExternalInput").ap()

    x_d = din("x", [S, D])
    ccol_d = din("c_col", [128, 8])
    wada_d = din("w_ada", [D, NMOD * D])
    bada_d = din("b_ada", [1, NMOD * D])
    gvec_d = din("gvecs", [4, D])
    w1gu_d = din("w1_gu", [D, 2 * DFF])
    w1d_d = din("w1_down", [DFF, D])
    w2gu_d = din("w2_gu", [D, 2 * DFF])
    w2d_d = din("w2_down", [DFF, D])
    win_d = din("w_in", [D, 1184])
    wuq_d = din("w_uq", [256, 768])
    wuqr_d = din("w_uq_rot", [256, 768])
    wuk_d = din("w_ukv_k", [128, 512])
    wuv_d = din("w_ukv_v", [128, 512])
    gql_d = din("g_q_lat_col", [128, 2])
    gkvl_d = din("g_kv_lat_col", [128, 1])
    gqk_d = din("g_qk_row", [1, 640])
    gout_d = din("g_out_col", [128, 8])
    wout_d = din("w_out", [D, D])
    cosm_d = din("cosm", [S, 32])
    sinm_d = din("sinm", [S, 32])
    cosg_d = din("cosg", [S, 64])
    sing_d = din("sing", [S, 64])
    cosf_d = din("cosf", [32, S])
    sinf_d = din("sinf", [32, S])
    y_d = nc.dram_tensor("y", [S, D], F32, kind="ExternalOutput").ap()
    skind = "ExternalOutput" if debug else "Internal"
    modscr = nc.dram_tensor("modscr", [NMOD, 128, D], F32, kind=skind).ap()
    x1scr = nc.dram_tensor("x1scr", [S, D], F32, kind=skind).ap()
    otscr = nc.dram_tensor("otscr", [D, S], BF16, kind=skind).ap()

    S_ = Sched()
    stack = ExitStack()
    with stack:
        big = stack.enter_context(nc.sbuf_tensor("arena", [128, ARENA_BYTES // 4], F32))
        pall = stack.enter_context(nc.psum_tensor("psum", [128, 4096], F32))

        def V(off, dt, shape):
            esz = 4 if dt == F32 else 2
            n = 1
            for s in shape[1:]:
                n *= s
            nb = n * esz
            assert off % 4 == 0 and nb % 4 == 0, (off, nb)
            assert off + nb <= ARENA_BYTES, (off, nb)
            ap = big[:, off // 4:(off + nb) // 4]
            if dt != F32:
                ap = ap.bitcast(dt)
            if len(shape) == 3:
                ap = ap.rearrange("p (a b) -> p a b", a=shape[1], b=shape[2])
            elif len(shape) == 4:
                ap = ap.rearrange("p (a b c) -> p a b c", a=shape[1], b=shape[2], c=shape[3])
            return ap

        def PB(bank, dt=F32):
            ap = pall[:, bank * 512:(bank + 1) * 512]
            if dt != F32:
                ap = ap.bitcast(dt)
            return ap

        class Arena:
            def __init__(self, base):
                self.o = base

            def take(self, nbytes):
                o = self.o
                self.o += (nbytes + 63) // 64 * 64
                assert self.o <= ARENA_BYTES, self.o
                return o

        A0 = Arena(0)
        ident = V(A0.take(256), BF16, [128, 128])
        identf = V(A0.take(512), F32, [128, 128])
        onesf = V(A0.take(256), F32, [128, 64])
        epsT = V(A0.take(64), F32, [128, 1])
        onesb = V(A0.take(128), BF16, [128, 64])
        b_const = S_.buf("const")
        S_.op("pool", lambda e: e.memset(identf, 0.0), w=[b_const])
        S_.op("pool", lambda e: e.affine_select(out=identf, in_=identf, pattern=[[-1, 128]],
                                                compare_op=ALU.not_equal, fill=1.0, base=0,
                                                channel_multiplier=1), w=[b_const])
        S_.op("pool", lambda e: e.tensor_copy(out=ident, in_=identf), r=[b_const], w=[b_const])
        S_.op("pool", lambda e: e.memset(onesf, 1.0), w=[b_const])
        S_.op("pool", lambda e: e.memset(epsT, EPS), w=[b_const])
        S_.op("pool", lambda e: e.memset(onesb, 1.0), w=[b_const])
        PERS = A0.o
        S_.barrier()

        def rstd_ops(ms_ap, out_ap, bufs_r, bufs_w, tmp_ap):
            S_.op("act", lambda e: e.activation(out=tmp_ap, in_=ms_ap, func=AF.Sqrt, bias=epsT[:, 0:1], scale=1.0),
                  r=bufs_r, w=bufs_w)
            S_.op("dve", lambda e: e.reciprocal(out=out_ap, in_=tmp_ap), r=bufs_w, w=bufs_w)

        W_BYTES = 8 * 2 * DFF * 2 + NJ * D * 2

        def ffn_weight_views(base):
            WGU = V(base, BF16, [128, 8, 2 * DFF])
            WD = V(base + 8 * 2 * DFF * 2, BF16, [128, NJ, D])
            return WGU, WD

        def ffn_weight_loads(tag, wgu_d, wd_d, WGU, WD):
            bufs = [S_.buf(tag + "w%d" % i) for i in range(4)]
            wgu_v = wgu_d.rearrange("(k p) n -> p k n", p=128)
            wd_v = wd_d.rearrange("(j p) n -> p j n", p=128)
            steps = []
            n = 0
            for cg in range(2 * DFF // 512):
                def st(cg=cg, b=bufs[n % 4]):
                    S_.dma("pool", lambda e: e.dma_start(out=WGU[:, :, cg * 512:(cg + 1) * 512],
                                                         in_=wgu_v[:, :, cg * 512:(cg + 1) * 512]), b, w=[b])
                steps.append(st)
                n += 1
            for j0 in range(0, NJ, 4):
                nj = min(4, NJ - j0)
                for c0 in (0, 512):
                    def st(j0=j0, nj=nj, c0=c0, b=bufs[n % 4]):
                        S_.dma("pool", lambda e: e.dma_start(out=WD[:, j0:j0 + nj, c0:c0 + 512],
                                                             in_=wd_v[:, j0:j0 + nj, c0:c0 + 512]), b, w=[b])
                    steps.append(st)
                    n += 1
            return steps, bufs

        W1GU, W1D = ffn_weight_views(PERS)
        w1steps, w1bufs = ffn_weight_loads("f1", w1gu_d, w1d_d, W1GU, W1D)
        n_gu_steps = 2 * DFF // 512
        w1gu_steps = [w1steps.pop(0) for _ in range(n_gu_steps)]
        A = Arena(PERS + 8 * 2 * DFF * 2)
        cact = V(A.take(32), F32, [128, 8])
        chb = V(A.take(16), BF16, [128, 8])
        clb = V(A.take(16), BF16, [128, 8])
        creph = V(A.take(2048), BF16, [128, 8, 128])
        crepl = V(A.take(2048), BF16, [128, 8, 128])
        gb = [V(A.take(4096), F32, [128, D]) for _ in range(3)]
        bb = [V(A.take(4096), F32, [128, D]) for _ in range(2)]
        mo = [V(A.take(4096), F32, [128, D]) for _ in range(2)]
        wst = [V(A.take(16384), F32, [128, 8, 512]) for _ in range(2)]
        whb = [V(A.take(8192), BF16, [128, 8, 512]) for _ in range(2)]
        wlb = [V(A.take(8192), BF16, [128, 8, 512]) for _ in range(2)]
        b_c = S_.buf("cact")
        b_gb = [S_.buf("gb%d" % i) for i in range(3)]
        b_bb = [S_.buf("bb%d" % i) for i in range(2)]
        b_mo = [S_.buf("mo%d" % i) for i in range(2)]
        b_wst = [S_.buf("wst%d" % i) for i in range(2)]
        b_wh = [S_.buf("wh%d" % i) for i in range(2)]
        b_wl = [S_.buf("wl%d" % i) for i in range(2)]
        b_pm = [S_.buf("pm%d" % i) for i in range(4)]
        S_.dma("sp", lambda e: e.dma_start(out=cact, in_=ccol_d), b_c, w=[b_c])
        for i in range(3):
            S_.dma("sp", lambda e, i=i: e.dma_start(out=gb[i], in_=gvec_d[i:i + 1, :].partition_broadcast(128)),
                   b_gb[i], w=[b_gb[i]])
        S_.op("act", lambda e: e.activation(out=cact, in_=cact, func=AF.Silu), r=[b_c], w=[b_c])
        S_.op("dve", lambda e: e.tensor_copy(out=chb, in_=cact), r=[b_c], w=[b_c])
        S_.op("dve", lambda e: e.tensor_tensor(out=clb, in0=cact, in1=chb, op=ALU.subtract), r=[b_c], w=[b_c])
        for k in range(8):
            S_.op("dve", lambda e, k=k: e.tensor_copy(out=creph[:, k, :], in_=chb[:, k:k + 1].to_broadcast([128, 128])),
                  r=[b_c], w=[b_c])
            S_.op("dve", lambda e, k=k: e.tensor_copy(out=crepl[:, k, :], in_=clb[:, k:k + 1].to_broadcast([128, 128])),
                  r=[b_c], w=[b_c])
        wada_v = wada_d.rearrange("(k p) n -> p k n", p=128)
        def p0_load(jh):
            j, hf = jh // 2, jh % 2
            ws = jh % 2
            S_.dma("sp", lambda e: e.dma_start(out=wst[ws], in_=wada_v[:, :, j * D + hf * 512:j * D + (hf + 1) * 512]),
                   b_wst[ws], w=[b_wst[ws]])
            if hf == 0:
                S_.dma("sp", lambda e: e.dma_start(out=bb[j % 2], in_=bada_d[0:1, j * D:(j + 1) * D].partition_broadcast(128)),
                       b_bb[j % 2], w=[b_bb[j % 2]])

        b_wl2 = [[S_.buf("wl%d_%d" % (i, q)) for q in range(2)] for i in range(2)]
        p0_load(0)
        for jh in range(2 * NMOD):
            j, hf = jh // 2, jh % 2
            ws = jh % 2
            ms = j % 2
            if jh + 1 < 2 * NMOD:
                p0_load(jh + 1)
            if w1gu_steps:
                w1gu_steps.pop(0)()
            S_.op("act", lambda e, ws=ws: e.activation(out=whb[ws], in_=wst[ws], func=AF.Copy), r=[b_wst[ws]], w=[b_wh[ws]])
            for q in range(2):
                qsl = slice(q * 256, (q + 1) * 256)
                S_.op("dve" if q == 0 else "pool",
                      lambda e, ws=ws, qsl=qsl: e.tensor_tensor(out=wlb[ws][:, :, qsl], in0=wst[ws][:, :, qsl],
                                                                in1=whb[ws][:, :, qsl], op=ALU.subtract),
                      r=[b_wst[ws], b_wh[ws]], w=[b_wl2[ws][q]])
            pb = jh % 4
            ps = PB(pb)

            def mm(e, ws=ws, ps=ps):
                n = 0
                for (cr, wb) in ((creph, whb), (crepl, whb), (creph, wlb)):
                    for k in range(8):
                        ins = e.matmul(ps, lhsT=cr[:, k, :], rhs=wb[ws][:, k, :], start=(n == 0), stop=(n == 23))
                        n += 1
                return ins
            S_.op("pe", mm, r=[b_c, b_wh[ws], b_wl2[ws][0], b_wl2[ws][1]], w=[b_pm[pb]])
            osl = mo[ms][:, hf * 512:(hf + 1) * 512]
            bsl = bb[ms][:, hf * 512:(hf + 1) * 512]
            S_.op("dve", lambda e, osl=osl, bsl=bsl, ps=ps: e.tensor_tensor(out=osl, in0=ps, in1=bsl, op=ALU.add),
                  r=[b_pm[pb], b_bb[ms]], w=[b_mo[ms]])
            if hf == 0:
                continue
            kind = j % 3
            if kind == 1:
                gsel = gb[j // 3]
                S_.op("dve", lambda e, ms=ms, gsel=gsel: e.scalar_tensor_tensor(
                    out=mo[ms], in0=mo[ms], scalar=1.0, in1=gsel, op0=ALU.add, op1=ALU.mult),
                    r=[b_gb[j // 3]], w=[b_mo[ms]])
            elif kind == 2 and j != 5:
                S_.op("dve", lambda e, ms=ms: e.tensor_scalar(out=mo[ms], in0=mo[ms], scalar1=0.5, scalar2=None,
                                                              op0=ALU.mult), w=[b_mo[ms]])
            S_.dma("sp", lambda e, j=j, ms=ms: e.dma_start(out=modscr[j], in_=mo[ms]), b_mo[ms], r=[b_mo[ms]])
        while w1gu_steps:
            w1gu_steps.pop(0)()
        S_.barrier()

        def ffn_phase(tag, wgu_d, wd_d, iA, iB, iG, src_d, final, preloaded=False, pending=None):
            A = Arena(PERS)
            WGU, WD = ffn_weight_views(A.take(W_BYTES))
            At = V(A.take(4096), F32, [128, D])
            Bt = V(A.take(4096), F32, [128, D])
            Gt = V(A.take(4096), F32, [128, D])
            Gf = V(A.take(4096), F32, [128, D])
            gtmp = [V(A.take(2048), F32, [128, 512])] * 2
            xb = [V(A.take(4096), F32, [128, D]) for _ in range(3 * TPG)]
            tmpb = [V(A.take(4096), F32, [128, D]) for _ in range(2)]
            hbf = [V(A.take(2048), BF16, [128, D]) for _ in range(2)]
            hT = V(A.take(8 * GT * 2), BF16, [128, 8, GT])
            gact = [V(A.take(GT * 4), F32, [128, GT]) for _ in range(2)]
            actT = V(A.take(NJ * GT * 2), BF16, [128, NJ, GT])
            junk = V(A.take(2048), BF16, [128, D])
            stat = V(A.take(64 * 4), F32, [128, 64])

            bn = lambda n: S_.buf(tag + n)
            b_A, b_B, b_G, b_Gf = bn("A"), bn("B"), bn("G"), bn("Gf")
            b_gtmp = [bn("gtmp")] * 2
            b_x = [bn("x%d" % i) for i in range(3 * TPG)]
            b_tmp = [bn("tmp%d" % i) for i in range(2)]
            b_hbf = [bn("hbf%d" % i) for i in range(2)]
            b_hT, b_actT, b_junk = bn("hT"), bn("actT"), bn("junk")
            b_gact = [bn("gact%d" % i) for i in range(2)]
            b_stat = [bn("stat%d" % i) for i in range(2 * TPG)]
            b_pgu = [bn("pgu%d" % i) for i in range(4)]
            b_pT = bn("pT")
            b_pd = [bn("pd%d" % i) for i in range(3)]
            pgu = [PB(i) for i in range(4)]
            pT = PB(4, BF16).rearrange("p (a b) -> p a b", a=8, b=128)
            pd = [PB(5 + i) for i in range(3)]

            S_.dma("sp", lambda e: e.dma_start(out=At, in_=modscr[iA]), b_A, w=[b_A])
            S_.dma("sp", lambda e: e.dma_start(out=Bt, in_=modscr[iB]), b_B, w=[b_B])
            S_.dma("sp", lambda e: e.dma_start(out=Gt, in_=modscr[iG]), b_G, w=[b_G])
            if final:
                S_.dma("sp", lambda e: e.dma_start(out=Gf, in_=gvec_d[3:4, :].partition_broadcast(128)), b_Gf, w=[b_Gf])
            if preloaded:
                wbufs = []
            elif pending is not None:
                wsteps, wbufs = pending
                for st_ in wsteps:
                    st_()
            else:
                wsteps, wbufs = ffn_weight_loads(tag, wgu_d, wd_d, WGU, WD)
                for st_ in wsteps:
                    st_()
            b_wgu_l = list(wbufs)
            b_wd_l = list(wbufs)

            b_nst = [bn("nst%d" % p) for p in range(3)]
            b_fst = bn("fst")

            def x_load(g):
                p = g % 3
                for tt in range(TPG):
                    t = g * TPG + tt
                    xs = p * TPG + tt
                    S_.dma("sp", lambda e, t=t, xs=xs: e.dma_start(out=xb[xs], in_=src_d[t * 128:(t + 1) * 128, :]),
                           b_x[xs], w=[b_x[xs]])

            def norm_load_sq(g):
                p = g % 3
                for tt in range(TPG):
                    t = g * TPG + tt
                    xs = p * TPG + tt
                    S_.op("act", lambda e, xs=xs, c=p * 8 + tt: e.activation(out=junk, in_=xb[xs], func=AF.Square, scale=1.0 / 32.0,
                                                                             accum_out=stat[:, c:c + 1]),
                          r=[b_x[xs]], w=[b_junk, b_nst[p]])

            def norm_rstd(g):
                p = g % 3
                rstd_ops(stat[:, p * 8:p * 8 + TPG], stat[:, p * 8 + 4:p * 8 + 4 + TPG], [b_nst[p]], [b_nst[p]],
                         stat[:, p * 8 + 2:p * 8 + 2 + TPG])

            def norm_apply(g, tt):
                p = g % 3
                t = g * TPG + tt
                xs = p * TPG + tt
                ts = t % 2
                c = p * 8 + 4 + tt
                S_.op("act", lambda e: e.activation(out=tmpb[ts], in_=xb[xs], func=AF.Copy, scale=stat[:, c:c + 1]),
                      r=[b_x[xs], b_nst[p]], w=[b_tmp[ts]])
                S_.op("pool", lambda e: e.tensor_tensor(out=tmpb[ts], in0=tmpb[ts], in1=At, op=ALU.mult), r=[b_A], w=[b_tmp[ts]])
                S_.op("pool", lambda e: e.tensor_tensor(out=hbf[ts], in0=tmpb[ts], in1=Bt, op=ALU.add),
                      r=[b_tmp[ts], b_B], w=[b_hbf[ts]])

            def transpose_group(g):
                for tt in range(TPG):
                    t = g * TPG + tt
                    ts = t % 2

                    def tr(e, ts=ts):
                        for k in range(8):
                            ins = e.transpose(out=pT[:, k, :], in_=hbf[ts][:, k * 128:(k + 1) * 128], identity=ident)
                        return ins
                    S_.op("pe", tr, r=[b_hbf[ts], b_const], w=[b_pT])
                    S_.op("act", lambda e, tt=tt: e.activation(out=hT[:, :, tt * 128:(tt + 1) * 128], in_=pT, func=AF.Copy),
                          r=[b_pT], w=[b_hT])

            def gateup_group(g, hooks=None):
                for j in range(NJ):
                    if hooks and j in hooks:
                        hooks[j]()
                    pb = j % 4
                    gb_ = j % 2

                    def mm(e, j=j, pb=pb):
                        for k in range(8):
                            e.matmul(pgu[pb][:, 0:GT], lhsT=WGU[:, k, j * 128:(j + 1) * 128], rhs=hT[:, k, :],
                                     start=(k == 0), stop=(k == 7))
                        for k in range(8):
                            ins = e.matmul(pgu[pb][:, GT:2 * GT], lhsT=WGU[:, k, DFF + j * 128:DFF + (j + 1) * 128],
                                           rhs=hT[:, k, :], start=(k == 0), stop=(k == 7))
                        return ins
                    S_.op("pe", mm, r=b_wgu_l + [b_hT], w=[b_pgu[pb]])
                    S_.op("act", lambda e, pb=pb, gb_=gb_: e.activation(out=gact[gb_], in_=pgu[pb][:, 0:GT], func=AF.Silu),
                          r=[b_pgu[pb]], w=[b_gact[gb_]])
                    S_.op("dve", lambda e, j=j, pb=pb, gb_=gb_: e.tensor_tensor(out=actT[:, j, :], in0=pgu[pb][:, GT:2 * GT],
                                                                                in1=gact[gb_], op=ALU.mult),
                          r=[b_pgu[pb], b_gact[gb_]], w=[b_actT])

            def down_group(g):
                for tt in range(TPG):
                    t = g * TPG + tt
                    xs = (g % 3) * TPG + tt
                    for hf in range(2):
                        pi = (2 * t + hf) % 3

                        def mm(e, tt=tt, hf=hf, pi=pi):
                            for j in range(NJ):
                                ins = e.matmul(pd[pi], lhsT=actT[:, j, tt * 128:(tt + 1) * 128],
                                               rhs=WD[:, j, hf * 512:(hf + 1) * 512], start=(j == 0), stop=(j == NJ - 1))
                            return ins
                        S_.op("pe", mm, r=[b_actT] + b_wd_l, w=[b_pd[pi]])
                        gi = (2 * t + hf) % 2
                        S_.op("dve", lambda e, hf=hf, pi=pi, gi=gi: e.tensor_tensor(
                            out=gtmp[gi], in0=pd[pi], in1=Gt[:, hf * 512:(hf + 1) * 512], op=ALU.mult),
                            r=[b_pd[pi], b_G], w=[b_gtmp[gi]])
                        S_.op("dve", lambda e, xs=xs, hf=hf, gi=gi: e.tensor_tensor(
                            out=xb[xs][:, hf * 512:(hf + 1) * 512], in0=xb[xs][:, hf * 512:(hf + 1) * 512], in1=gtmp[gi],
                            op=ALU.add), r=[b_gtmp[gi]], w=[b_x[xs]])
                    if not final:
                        S_.dma("sp", lambda e, t=t, xs=xs: e.dma_start(out=x1scr[t * 128:(t + 1) * 128, :], in_=xb[xs]),
                               b_x[xs], r=[b_x[xs]])
                if final:
                    for tt in range(TPG):
                        xs = (g % 3) * TPG + tt
                        S_.op("act", lambda e, xs=xs, tt=tt: e.activation(out=junk, in_=xb[xs], func=AF.Square, scale=1.0 / 32.0,
                                                                          accum_out=stat[:, 32 + tt:33 + tt]),
                              r=[b_x[xs]], w=[b_junk, b_fst])

            def final_rstd(g):
                rstd_ops(stat[:, 32:32 + TPG], stat[:, 36:36 + TPG], [b_fst], [b_fst], stat[:, 34:34 + TPG])

            def final_apply(g):
                for tt in range(TPG):
                    t = g * TPG + tt
                    xs = (g % 3) * TPG + tt
                    S_.op("act", lambda e, xs=xs, tt=tt: e.activation(out=xb[xs], in_=xb[xs], func=AF.Copy,
                                                                      scale=stat[:, 36 + tt:37 + tt]),
                          r=[b_fst], w=[b_x[xs]])
                    S_.op("pool", lambda e, xs=xs: e.tensor_tensor(out=xb[xs], in0=xb[xs], in1=Gf, op=ALU.mult),
                          r=[b_Gf], w=[b_x[xs]])
                    S_.dma("sp", lambda e, t=t, xs=xs: e.dma_start(out=y_d[t * 128:(t + 1) * 128, :], in_=xb[xs]),
                           b_x[xs], r=[b_x[xs]])

            x_load(0)
            if NG > 1:
                x_load(1)
            norm_load_sq(0)
            norm_rstd(0)
            for tt in range(TPG):
                norm_apply(0, tt)
            transpose_group(0)
            for g in range(NG):
                hooks = {}
                if g + 1 < NG:
                    hooks[1] = (lambda g=g: norm_load_sq(g + 1))
                    for tt in range(TPG):
                        hooks[9 + 4 * tt] = (lambda g=g, tt=tt: norm_apply(g + 1, tt))

                def sqrt_hook(g=g):
                    if g + 1 < NG:
                        norm_rstd(g + 1)
                    if final and g >= 1:
                        final_rstd(g - 1)
                hooks[6] = sqrt_hook
                if final and g >= 1:
                    hooks[7] = (lambda g=g: final_apply(g - 1))
                gateup_group(g, hooks)
                if g + 1 < NG:
                    transpose_group(g + 1)
                if g + 2 < NG:
                    x_load(g + 2)
                down_group(g)
            if final:
                final_rstd(NG - 1)
                final_apply(NG - 1)
            S_.barrier()

        ffn_phase("f1", w1gu_d, w1d_d, 1, 0, 2, x_d, False, pending=(w1steps, w1bufs))

        A = Arena(PERS)
        WGU_BYTES = 8 * 2 * DFF * 2
        qnT = V(A.take(2 * S * 2), BF16, [128, 2, S])
        kvnT = V(A.take(S * 2), BF16, [128, S])
        KTm = [V(A.take(S * 2), BF16, [128, S]) for _ in range(2)]
        HOLE0 = A.o
        HOLE_END = PERS + WGU_BYTES
        if HOLE_END < HOLE0:
            HOLE_END = HOLE0
        AH = Arena(HOLE0)
        A.o = HOLE_END
        QTg = V(A.take(4 * S * 2), BF16, [128, 4, S])
        KTg = V(A.take(4 * S * 2), BF16, [128, 4, S])
        VAg = V(A.take(NT * 2 * 66 * 2), BF16, [128, NT, 2, 66])
        RES_END = A.o

        def hole_take(arena, nbytes):
            if AHH[0].o + (nbytes + 63) // 64 * 64 <= HOLE_END:
                return AHH[0].take(nbytes)
            return arena.take(nbytes)
        AHH = [AH]
        b_res = S_.buf("res")
        Win = V(hole_take(A, 8 * 1184 * 2), BF16, [128, 8, 1184])
        ABcol = V(A.take(64), F32, [128, 16])
        biash = V(hole_take(A, 1184 * 2), BF16, [128, 1184])
        biasl = V(A.take(1184 * 2), BF16, [128, 1184])
        gqk = V(hole_take(A, 640 * 4), F32, [128, 640])
        cosm = V(hole_take(A, NT * 32 * 4), F32, [128, NT, 32])
        sinm = V(hole_take(A, NT * 32 * 4), F32, [128, NT, 32])
        cosg = V(hole_take(A, NT * 64 * 4), F32, [128, NT, 64])
        sing = V(hole_take(A, NT * 64 * 4), F32, [128, NT, 64])
        wstg_off = A.o
        x2b = [V(A.take(4096), F32, [128, D]) for _ in range(2)]
        hb2 = [V(A.take(2048), BF16, [128, D]) for _ in range(2)]
        hT2 = [V(A.take(8 * 128 * 2), BF16, [128, 8, 128])]
        junk2 = V(A.take(2048), BF16, [128, D])
        assert A.o - wstg_off >= 16384
        hT2.append(V(A.take(8 * 128 * 2), BF16, [128, 8, 128]))
        biasf = V(A.o, F32, [128, 1184])
        sq2 = [V(A.take(640 * 4), F32, [128, 640]) for _ in range(2)]
        Brep = V(A.o, BF16, [128, 8, 128])
        qk2 = [V(A.take(640 * 4), F32, [128, 640]) for _ in range(2)]
        t12 = [V(A.take(640 * 4), F32, [128, 640]) for _ in range(2)]
        qkb = [V(A.take(640 * 2), BF16, [128, 640]) for _ in range(2)]
        kz = [V(A.take(4 * 128 * 2), BF16, [128, 4, 128]) for _ in range(2)]
        lat = [V(A.take(384 * 2), BF16, [128, 384]) for _ in range(2)]
        kpad = [V(A.take(128 * 2), BF16, [128, 128]) for _ in range(2)]
        kr = [V(A.take(32 * 4), F32, [128, 32]) for _ in range(2)]
        kt1 = [V(A.take(32 * 4), F32, [128, 32]) for _ in range(2)]
        kt2 = [V(A.take(32 * 4), F32, [128, 32]) for _ in range(2)]
        st2 = [V(A.take(32 * 4), F32, [128, 32]) for _ in range(2)]
        wstg = V(wstg_off, F32, [128, 8, 512])

        b_win, b_AB, b_bias, b_gqk, b_tab = (S_.buf("win"), S_.buf("AB"), S_.buf("bias"), S_.buf("gqk"), S_.buf("tab"))
        b_x2 = [S_.buf("x2_%d" % i) for i in range(2)]
        b_hT2, b_junk2 = [S_.buf("hT2_0"), S_.buf("hT2_1")], S_.buf("junk2")
        b_hb2 = [S_.buf("hb2_%d" % i) for i in range(2)]
        pb2 = lambda n: [S_.buf("%s_%d" % (n, i)) for i in range(2)]
        b_sq2, b_qk2, b_t12, b_qkb, b_kz = pb2("sq2"), pb2("qk2"), pb2("t12"), pb2("qkb"), pb2("kz")
        b_lat, b_kpad, b_kr, b_kt1, b_kt2, b_st2 = pb2("lat"), pb2("kpad"), pb2("kr"), pb2("kt1"), pb2("kt2"), pb2("st2")
        b_wstg = S_.buf("wstg")
        b_pz = [[S_.buf("pz%d_%d" % (i, j), j == 2) for j in range(3)] for i in range(2)]
        b_pT1, b_pT2 = S_.buf("pT1"), S_.buf("pT2")
        pz = [[PB(3 * i + j) for j in range(3)] for i in range(2)]
        pT1 = PB(6, BF16).rearrange("p (a b) -> p a b", a=8, b=128)
        pT2 = PB(7, BF16).rearrange("p (a b) -> p a b", a=8, b=128)

        S_.dma("sp", lambda e: e.dma_start(out=gqk, in_=gqk_d[0:1, :].partition_broadcast(128)), b_gqk, w=[b_gqk])
        S_.dma("sp", lambda e: e.dma_start(out=ABcol[:, 0:8], in_=modscr[4][0:1, :].rearrange("o (k p) -> p (o k)", p=128),
                                           allow_slow_non_contiguous=True), b_AB, w=[b_AB])
        S_.dma("sp", lambda e: e.dma_start(out=ABcol[:, 8:16], in_=modscr[3][0:1, :].rearrange("o (k p) -> p (o k)", p=128),
                                           allow_slow_non_contiguous=True), b_AB, w=[b_AB])
        for k in range(8):
            S_.op("dve", lambda e, k=k: e.tensor_copy(out=Brep[:, k, :], in_=ABcol[:, 8 + k:9 + k].to_broadcast([128, 128])),
                  r=[b_AB], w=[b_AB])
        S_.dma("sp", lambda e: e.dma_start(out=cosm, in_=cosm_d.rearrange("(t p) d -> p t d", p=128)), b_tab, w=[b_tab])
        S_.dma("sp", lambda e: e.dma_start(out=sinm, in_=sinm_d.rearrange("(t p) d -> p t d", p=128)), b_tab, w=[b_tab])
        S_.dma("sp", lambda e: e.dma_start(out=cosg, in_=cosg_d.rearrange("(t p) d -> p t d", p=128)), b_tab, w=[b_tab])
        S_.dma("sp", lambda e: e.dma_start(out=sing, in_=sing_d.rearrange("(t p) d -> p t d", p=128)), b_tab, w=[b_tab])
        win_v = win_d.rearrange("(k p) n -> p k n", p=128)
        pbias = PB(0)
        b_pbias = S_.buf("pbias")
        for (c0, c1) in ((0, 512), (512, 1024), (1024, 1184)):
            S_.dma("sp", lambda e, c0=c0, c1=c1: e.dma_start(out=wstg[:, :, 0:c1 - c0], in_=win_v[:, :, c0:c1]),
                   b_wstg, w=[b_wstg])
            S_.op("act", lambda e, c0=c0, c1=c1: e.activation(out=Win[:, :, c0:c1], in_=wstg[:, :, 0:c1 - c0], func=AF.Copy),
                  r=[b_wstg], w=[b_win])

            def bmm(e, c0=c0, c1=c1):
                for k in range(8):
                    ins = e.matmul(pbias[:, 0:c1 - c0], lhsT=Brep[:, k, :], rhs=Win[:, k, c0:c1], start=(k == 0), stop=(k == 7))
                return ins
            S_.op("pe", bmm, r=[b_AB, b_win], w=[b_pbias])
            S_.op("dve", lambda e, c0=c0, c1=c1: e.tensor_copy(out=biasf[:, c0:c1], in_=pbias[:, 0:c1 - c0]),
                  r=[b_pbias], w=[b_bias])
            for k in range(8):
                S_.op("dve", lambda e, c0=c0, c1=c1, k=k: e.tensor_scalar(out=Win[:, k, c0:c1], in0=wstg[:, k, 0:c1 - c0],
                                                                          scalar1=ABcol[:, k:k + 1], scalar2=None, op0=ALU.mult),
                      r=[b_wstg, b_AB], w=[b_win])
        S_.op("dve", lambda e: e.tensor_copy(out=biash, in_=biasf), r=[b_bias], w=[b_bias])
        S_.op("dve", lambda e: e.tensor_tensor(out=biasl, in0=biasf, in1=biash, op=ALU.subtract), r=[b_bias], w=[b_bias])
        S_.op("pool", lambda e: e.memset(VAg, 1.0), w=[b_res])
        for i in range(2):
            S_.op("pool", lambda e, i=i: e.memset(kpad[i], 0.0), w=[b_kpad[i]])
            S_.op("pool", lambda e, i=i: e.memset(kz[i], 0.0), w=[b_kz[i]])
            S_.op("pool", lambda e, i=i: e.memset(KTm[i], 0.0), w=[b_res])
        S_.barrier()

        def p2_front_a(t):
            xs = t % 2
            st = st2[xs]
            bs = b_st2[xs]
            S_.dma("sp", lambda e: e.dma_start(out=x2b[xs], in_=x1scr[t * 128:(t + 1) * 128, :]), b_x2[xs], w=[b_x2[xs]])
            S_.op("act", lambda e: e.activation(out=junk2, in_=x2b[xs], func=AF.Square, scale=1.0 / 32.0,
                                                accum_out=st[:, 0:1]), r=[b_x2[xs]], w=[b_junk2, bs])
            rstd_ops(st[:, 0:1], st[:, 2:3], [bs], [bs], st[:, 1:2])
            S_.op("dve", lambda e: e.tensor_scalar(out=hb2[xs], in0=x2b[xs], scalar1=st[:, 2:3], scalar2=None, op0=ALU.mult),
                  r=[b_x2[xs], bs], w=[b_hb2[xs]])

        def p2_front_b(t):
            xs = t % 2

            def tr(e):
                for k in range(8):
                    ins = e.transpose(out=pT1[:, k, :], in_=hb2[xs][:, k * 128:(k + 1) * 128], identity=ident)
                return ins
            S_.op("pe", tr, r=[b_hb2[xs], b_const], w=[b_pT1])
            S_.op("act", lambda e: e.activation(out=hT2[xs], in_=pT1, func=AF.Copy), r=[b_pT1], w=[b_hT2[xs]])

        def p2_front_b2(t):
            xs = t % 2

            def zmm(e):
                for (pi, c0, c1) in ((0, 0, 416), (1, 416, 928), (2, 928, 1184)):
                    for k in range(8):
                        e.matmul(pz[xs][pi][:, 0:c1 - c0], lhsT=hT2[xs][:, k, :], rhs=Win[:, k, c0:c1],
                                 start=(k == 0), stop=False)
                    e.matmul(pz[xs][pi][:, 0:c1 - c0], lhsT=ident, rhs=biash[:, c0:c1], start=False, stop=False)
                    ins = e.matmul(pz[xs][pi][:, 0:c1 - c0], lhsT=ident, rhs=biasl[:, c0:c1], start=False, stop=True)
                return ins
            S_.op("pe", zmm, r=[b_hT2[xs], b_win, b_bias, b_const], w=b_pz[xs])

        def p2_back_a(t):
            xs = t % 2
            st = st2[xs]
            bs = b_st2[xs]
            tsl = slice(t * 128, (t + 1) * 128)
            z0, z1, z2 = pz[xs]
            bz0, bz1, bz2 = b_pz[xs]
            S_.op("act", lambda e: e.activation(out=junk2[:, 0:256], in_=z0[:, 0:256], func=AF.Square,
                                                scale=1.0 / 16.0, accum_out=st[:, 4:5]), r=[bz0], w=[b_junk2, bs])
            S_.op("act", lambda e: e.activation(out=junk2[:, 256:384], in_=z0[:, 256:384], func=AF.Square,
                                                scale=1.0 / math.sqrt(128.0), accum_out=st[:, 5:6]), r=[bz0], w=[b_junk2, bs])
            S_.op("act", lambda e: e.activation(out=sq2[xs][:, 0:512], in_=z1, func=AF.Square, scale=1.0 / 8.0),
                  r=[bz1], w=[b_sq2[xs]])
            S_.op("act", lambda e: e.activation(out=sq2[xs][:, 512:640], in_=z2[:, 0:128], func=AF.Square, scale=1.0 / 8.0),
                  r=[bz2], w=[b_sq2[xs]])
            S_.op("dve", lambda e: e.tensor_reduce(out=st[:, 10:20], in_=sq2[xs].rearrange("p (h d) -> p h d", d=64),
                                                   axis=AX.X, op=ALU.add), r=[b_sq2[xs]], w=[bs])
            S_.op("dve", lambda e: e.tensor_copy(out=st[:, 8:10], in_=st[:, 4:6]), r=[bs], w=[bs])
            S_.op("act", lambda e: e.activation(out=st[:, 20:32], in_=st[:, 8:20], func=AF.Sqrt, bias=epsT[:, 0:1], scale=1.0),
                  r=[bs], w=[bs])
            S_.op("dve", lambda e: e.reciprocal(out=st[:, 8:20], in_=st[:, 20:32]), r=[bs], w=[bs])
            S_.op("dve", lambda e: e.tensor_scalar(out=lat[xs][:, 0:256], in0=z0[:, 0:256], scalar1=st[:, 8:9],
                                                   scalar2=None, op0=ALU.mult), r=[bz0, bs], w=[b_lat[xs]])
            S_.op("dve", lambda e: e.tensor_scalar(out=lat[xs][:, 256:384], in0=z0[:, 256:384], scalar1=st[:, 9:10],
                                                   scalar2=None, op0=ALU.mult), r=[bz0, bs], w=[b_lat[xs]])
            S_.op("dve", lambda e: e.tensor_copy(out=kr[xs], in_=z0[:, 384:416]), r=[bz0], w=[b_kr[xs]])
            S_.op("pool", lambda e: e.tensor_tensor(out=kt1[xs], in0=kr[xs], in1=cosm[:, t, :], op=ALU.mult),
                  r=[b_kr[xs], b_tab], w=[b_kt1[xs]])
            krv = kr[xs].rearrange("p (b h f) -> p b h f", b=2, h=2, f=8)
            kt2v = kt2[xs].rearrange("p (b h f) -> p b h f", b=2, h=2, f=8)
            for hh in range(2):
                S_.op("pool", lambda e, hh=hh: e.tensor_tensor(
                    out=kt2v[:, :, hh, :], in0=krv[:, :, 1 - hh, :],
                    in1=sinm[:, t, :].rearrange("p (b h f) -> p b h f", b=2, h=2, f=8)[:, :, hh, :], op=ALU.mult),
                    r=[b_kr[xs], b_tab], w=[b_kt2[xs]])
            S_.op("pool", lambda e: e.tensor_tensor(out=kpad[xs][:, 64:96], in0=kt1[xs], in1=kt2[xs], op=ALU.add),
                  r=[b_kt1[xs], b_kt2[xs]], w=[b_kpad[xs]])
            qk3 = qk2[xs].rearrange("p (h d) -> p h d", d=64)
            S_.op("dve", lambda e: e.tensor_tensor(out=qk3[:, 0:8, :], in0=z1.rearrange("p (h d) -> p h d", d=64),
                                                   in1=st[:, 10:18].unsqueeze(2).to_broadcast([128, 8, 64]), op=ALU.mult),
                  r=[bz1, bs], w=[b_qk2[xs]])
            S_.op("dve", lambda e: e.tensor_tensor(out=qk3[:, 8:10, :], in0=z2[:, 0:128].rearrange("p (h d) -> p h d", d=64),
                                                   in1=st[:, 18:20].unsqueeze(2).to_broadcast([128, 2, 64]), op=ALU.mult),
                  r=[bz2, bs], w=[b_qk2[xs]])
            S_.op("act", lambda e: e.activation(out=VAg[:, t, :, 0:64], in_=z2[:, 128:256].rearrange("p (h d) -> p h d", d=64),
                                                func=AF.Copy), r=[bz2], w=[b_res])

        def p2_back_b(t):
            xs = t % 2
            tsl = slice(t * 128, (t + 1) * 128)
            qk3 = qk2[xs].rearrange("p (h d) -> p h d", d=64)
            S_.op("dve", lambda e: e.tensor_tensor(out=qk2[xs], in0=qk2[xs], in1=gqk, op=ALU.mult), r=[b_gqk], w=[b_qk2[xs]])
            S_.op("pool", lambda e: e.tensor_tensor(out=t12[xs].rearrange("p (h d) -> p h d", d=64), in0=qk3,
                                                    in1=cosg[:, t, :].unsqueeze(1).to_broadcast([128, 10, 64]),
                                                    op=ALU.mult), r=[b_qk2[xs], b_tab], w=[b_t12[xs]])
            qk5 = qk2[xs].rearrange("p (h b t f) -> p h b t f", h=10, b=2, t=2, f=16)
            t25 = sq2[xs].rearrange("p (h b t f) -> p h b t f", h=10, b=2, t=2, f=16)
            for hh in range(2):
                S_.op("dve", lambda e, hh=hh: e.tensor_tensor(
                    out=t25[:, :, :, hh, :], in0=qk5[:, :, :, 1 - hh, :],
                    in1=sing[:, t, :].rearrange("p (b t f) -> p b t f", b=2, t=2, f=16)[:, :, hh, :]
                    .unsqueeze(1).to_broadcast([128, 10, 2, 16]), op=ALU.mult),
                    r=[b_qk2[xs], b_tab], w=[b_sq2[xs]])
            S_.op("pool", lambda e: e.tensor_tensor(out=qkb[xs], in0=t12[xs], in1=sq2[xs], op=ALU.add),
                  r=[b_t12[xs], b_sq2[xs]], w=[b_qkb[xs]])
            kz5 = kz[xs].rearrange("p (k v) d -> p k v d", k=2, v=2)
            kin = qkb[xs][:, 512:640].rearrange("p (k d) -> p k d", d=64)
            S_.op("pool", lambda e: e.tensor_copy(out=kz5[:, :, 0, 0:64], in_=kin), r=[b_qkb[xs]], w=[b_kz[xs]])
            S_.op("pool", lambda e: e.tensor_copy(out=kz5[:, :, 1, 64:128], in_=kin), r=[b_qkb[xs]], w=[b_kz[xs]])


        def p2_back_b2(t):
            xs = t % 2
            tsl = slice(t * 128, (t + 1) * 128)

            def tr2(e):
                e.transpose(out=pT2[:, 0, :], in_=lat[xs][:, 0:128], identity=ident)
                e.transpose(out=pT2[:, 1, :], in_=lat[xs][:, 128:256], identity=ident)
                e.transpose(out=pT2[:, 2, :], in_=lat[xs][:, 256:384], identity=ident)
                e.transpose(out=pT2[:, 3, :], in_=kpad[xs], identity=ident)
                for c in range(4):
                    ins = e.transpose(out=pT2[:, 4 + c, :], in_=qkb[xs][:, c * 128:(c + 1) * 128], identity=ident)
                return ins
            S_.op("pe", tr2, r=[b_lat[xs], b_kpad[xs], b_qkb[xs], b_const], w=[b_pT2])

            S_.op("act", lambda e: e.activation(out=qnT[:, :, tsl], in_=pT2[:, 0:2, :], func=AF.Copy), r=[b_pT2], w=[b_res])
            S_.op("dve", lambda e: e.tensor_copy(out=kvnT[:, tsl], in_=pT2[:, 2, :]), r=[b_pT2], w=[b_res])
            for i in range(2):
                S_.op("act", lambda e, i=i: e.activation(out=KTm[i][64:96, tsl], in_=pT2[64:96, 3, :], func=AF.Copy),
                      r=[b_pT2], w=[b_res])
            S_.op("act", lambda e: e.activation(out=QTg[:, :, tsl], in_=pT2[:, 4:8, :], func=AF.Copy), r=[b_pT2], w=[b_res])

            def tr3(e):
                for vv in range(4):
                    ins = e.transpose(out=pT2[:, vv, :], in_=kz[xs][:, vv, :], identity=ident)
                return ins
            S_.op("pe", tr3, r=[b_kz[xs], b_const], w=[b_pT2])
            S_.op("dve", lambda e: e.tensor_copy(out=KTg[:, :, tsl], in_=pT2[:, 0:4, :]), r=[b_pT2], w=[b_res])

        for t0 in range(min(2, NT)):
            p2_front_a(t0)
            p2_front_b(t0)
        for t0 in range(min(2, NT)):
            p2_front_b2(t0)
        if NT > 2:
            p2_front_a(2)
            p2_front_b(2)
        for t in range(NT):
            p2_back_a(t)
            if t + 2 < NT:
                p2_front_b2(t + 2)
            p2_back_b(t)
            if t + 3 < NT:
                p2_front_a(t + 3)
                p2_front_b(t + 3)
            p2_back_b2(t)
        if debug:
            dq = nc.dram_tensor("dbg_qnT", [128, 2, S], BF16, kind="ExternalOutput").ap()
            dk = nc.dram_tensor("dbg_kvnT", [128, S], BF16, kind="ExternalOutput").ap()
            dkt = nc.dram_tensor("dbg_KTm", [128, S], BF16, kind="ExternalOutput").ap()
            b_dbg = S_.buf("dbg")
            S_.dma("sp", lambda e: e.dma_start(out=dq, in_=qnT), b_dbg, r=[b_res])
            S_.dma("sp", lambda e: e.dma_start(out=dk, in_=kvnT), b_dbg, r=[b_res])
            S_.dma("sp", lambda e: e.dma_start(out=dkt, in_=KTm[0]), b_dbg, r=[b_res])
        S_.barrier()

        A = Arena(RES_END)
        AHH[0] = Arena(HOLE0)
        b_resm = S_.buf("resm")
        QTm = [V(A.take(QG * 2), BF16, [128, QG]) for _ in range(2)]
        PT = [V(A.take(1024 * 2), BF16, [128, 1024]) for _ in range(3)]
        rdn = V(A.take(QG * 4), F32, [128, QG])
        bcs = V(A.take(QG * 4), F32, [128, QG])
        ohT = [V(A.take(QG * 2), BF16, [128, QG]) for _ in range(2)]
        qt1 = V(A.take(QG * 4), F32, [128, QG])
        qt2 = V(A.take(QG * 4), F32, [128, QG])
        VAm = [V(hole_take(A, NT * 66 * 2), BF16, [128, NT, 66]) for _ in range(2)]
        rh = V(A.take(QG * 2), BF16, [128, QG])
        rl = V(A.take(QG * 2), BF16, [128, QG])
        b_rh = S_.buf("rh")
        cosf = V(hole_take(A, S * 4), F32, [128, S])
        sinf = V(hole_take(A, S * 4), F32, [128, S])
        b_qt1, b_qt2, b_tabf = S_.buf("qt1"), S_.buf("qt2"), S_.buf("tabf")
        WqA = V(hole_take(A, 2 * 768 * 2), BF16, [128, 2, 768])
        WqB = V(hole_take(A, 2 * 768 * 2), BF16, [128, 2, 768])
        Wk = V(hole_take(A, 512 * 2), BF16, [128, 512])
        Wv = V(hole_take(A, 512 * 2), BF16, [128, 512])
        gcol = V(A.take(64), F32, [128, 4])
        wq_st = V(A.take(2 * 768 * 4), F32, [128, 2, 768])
        b_wq, b_gcol, b_wstg3 = S_.buf("wq"), S_.buf("gcol"), S_.buf("wstg3")
        S_.dma("sp", lambda e: e.dma_start(out=gcol[:, 0:2], in_=gql_d), b_gcol, w=[b_gcol])
        S_.dma("sp", lambda e: e.dma_start(out=gcol[:, 2:3], in_=gkvl_d), b_gcol, w=[b_gcol])
        for (src, dst) in ((wuq_d, WqA), (wuqr_d, WqB)):
            S_.dma("sp", lambda e, src=src: e.dma_start(out=wq_st, in_=src.rearrange("(c p) n -> p c n", p=128)),
                   b_wstg3, w=[b_wstg3])
            for c in range(2):
                S_.op("dve", lambda e, c=c, dst=dst: e.tensor_scalar(out=dst[:, c, :], in0=wq_st[:, c, :],
                                                                     scalar1=gcol[:, c:c + 1], scalar2=None, op0=ALU.mult),
                      r=[b_wstg3, b_gcol], w=[b_wq])
        for (src, dst) in ((wuk_d, Wk), (wuv_d, Wv)):
            S_.dma("sp", lambda e, src=src: e.dma_start(out=wq_st[:, 0, 0:512], in_=src), b_wstg3, w=[b_wstg3])
            S_.op("dve", lambda e, dst=dst: e.tensor_scalar(out=dst, in0=wq_st[:, 0, 0:512], scalar1=gcol[:, 2:3],
                                                            scalar2=None, op0=ALU.mult), r=[b_wstg3, b_gcol], w=[b_wq])
        b_VAm = [S_.buf("VAm%d" % i) for i in range(2)]
        S_.dma("sp", lambda e: e.dma_start(out=cosf[64:96, :], in_=cosf_d), b_tabf, w=[b_tabf])
        S_.dma("sp", lambda e: e.dma_start(out=sinf[64:96, :], in_=sinf_d), b_tabf, w=[b_tabf])
        for i in range(2):
            S_.op("pool", lambda e, i=i: e.memset(VAm[i], 1.0), w=[b_VAm[i]])
        b_QTm = [S_.buf("QTm%d" % i) for i in range(2)]
        for i in range(2):
            S_.op("pool", lambda e, i=i: e.memset(QTm[i], 0.0), w=[b_QTm[i]])
        b_KTm = [S_.buf("KTm%d" % i) for i in range(2)]
        b_PT = [S_.buf("PT%d" % i) for i in range(3)]
        b_rdn, b_bcs = S_.buf("rdn"), S_.buf("bcs")
        b_ohT = [S_.buf("ohT%d" % i) for i in range(2)]
        b_pS = [S_.buf("pS%d" % i) for i in range(2)]
        b_pO = [S_.buf("pO%d" % i) for i in range(2)]
        b_pM = [S_.buf("pM%d" % i) for i in range(2)]
        pS = [pall[:, i * 1024:(i + 1) * 1024] for i in range(2)]
        pO = [PB(4 + i) for i in range(2)]
        pM = [PB(6 + i) for i in range(2)]
        items = [(h, qg) for h in range(HM + HG) for qg in range(NQG)]
        sc_m = 96.0 ** -0.5
        sc_g = 64.0 ** -0.5

        def prep_k_steps(h):
            kb = h % 2
            subs = []
            for qg in range(NQG):
                def sub(qg=qg):
                    m = qg % 2
                    cs = slice(qg * QG, (qg + 1) * QG)
                    S_.op("pe", lambda e: e.matmul(pM[m][0:64, :], lhsT=Wk[:, h * 64:(h + 1) * 64], rhs=kvnT[:, cs],
                                                   start=True, stop=True), r=[b_resm, b_wq], w=[b_pM[m]])
                    S_.op("dve", lambda e: e.tensor_copy(out=KTm[kb][0:64, cs], in_=pM[m][0:64, :]),
                          r=[b_pM[m]], w=[b_KTm[kb]])
                subs.append(sub)
            nb8 = min(8, NT)
            for g8 in range(NT // nb8):
                def sub(g8=g8):
                    m = g8 % 2

                    def vmm(e):
                        for u in range(nb8):
                            blk = g8 * nb8 + u
                            ins = e.matmul(pM[m][:, u * 64:(u + 1) * 64], lhsT=kvnT[:, blk * 128:(blk + 1) * 128],
                                           rhs=Wv[:, h * 64:(h + 1) * 64], start=True, stop=True)
                        return ins
                    S_.op("pe", vmm, r=[b_resm, b_wq], w=[b_pM[m]])
                    S_.op("dve", lambda e: e.tensor_copy(out=VAm[kb][:, g8 * nb8:(g8 + 1) * nb8, 0:64],
                                                         in_=pM[m][:, 0:nb8 * 64].rearrange("p (u d) -> p u d", d=64)),
                          r=[b_pM[m]], w=[b_VAm[kb]])
                subs.append(sub)
            return subs

        def prep_k(h):
            for sub in prep_k_steps(h):
                sub()

        def prep_q(idx):
            h, qg = items[idx]
            if h >= HM:
                return
            qb = idx % 2
            cs = slice(qg * QG, (qg + 1) * QG)

            def mm(e, h=h, cs=cs):
                for (m, W) in ((0, WqA), (1, WqB)):
                    for c in range(2):
                        ins = e.matmul(pM[m][0:96, :], lhsT=W[:, c, h * 96:(h + 1) * 96], rhs=qnT[:, c, cs],
                                       start=(c == 0), stop=(c == 1))
                return ins
            S_.op("pe", mm, r=[b_resm, b_wq], w=[b_pM[0], b_pM[1]])
            S_.op("dve", lambda e, qb=qb: e.tensor_copy(out=QTm[qb][0:64, :], in_=pM[0][0:64, :]),
                  r=[b_pM[0]], w=[b_QTm[qb]])
            S_.op("dve", lambda e, cs=cs: e.tensor_tensor(out=qt1[64:96, :], in0=pM[0][64:96, :], in1=cosf[64:96, cs], op=ALU.mult),
                  r=[b_pM[0], b_tabf], w=[b_qt1])
            S_.op("dve", lambda e, cs=cs: e.tensor_tensor(out=qt2[64:96, :], in0=pM[1][64:96, :], in1=sinf[64:96, cs], op=ALU.mult),
                  r=[b_pM[1], b_tabf], w=[b_qt2])
            S_.op("dve", lambda e, qb=qb: e.tensor_tensor(out=QTm[qb][64:96, :], in0=qt1[64:96, :], in1=qt2[64:96, :], op=ALU.add),
                  r=[b_qt1, b_qt2], w=[b_QTm[qb]])

        prep_k(0)
        prep_q(0)
        if debug:
            d1 = nc.dram_tensor("dbg_KTm0", [128, S], BF16, kind="ExternalOutput").ap()
            d2 = nc.dram_tensor("dbg_QTm0", [128, QG], BF16, kind="ExternalOutput").ap()
            d3 = nc.dram_tensor("dbg_VAm0", [128, NT, 66], BF16, kind="ExternalOutput").ap()
            S_.dma("sp", lambda e: e.dma_start(out=d1, in_=KTm[0]), b_dbg, r=[b_KTm[0], b_res])
            S_.dma("sp", lambda e: e.dma_start(out=d2, in_=QTm[0]), b_dbg, r=[b_QTm[0]])
            S_.dma("sp", lambda e: e.dma_start(out=d3, in_=VAm[0]), b_dbg, r=[b_VAm[0]])
            S_.barrier()
        gctr = 0
        pend_fin = None

        def fin_b(pf):
            fidx, fh, fcs, fpo = pf
            ob = fidx % 2

            def bmm(e):
                e.matmul(pM[0][0:64, :], lhsT=onesb[64:65, 0:64], rhs=rh[64:65, :], start=True, stop=False)
                return e.matmul(pM[0][0:64, :], lhsT=onesb[64:65, 0:64], rhs=rl[64:65, :], start=False, stop=True)
            S_.op("pe", bmm, r=[b_rh, b_const], w=[b_pM[0]])
            S_.op("dve", lambda e: e.tensor_copy(out=bcs[0:64, :], in_=pM[0][0:64, :]), r=[b_pM[0]], w=[b_bcs])
            S_.op("dve", lambda e, fpo=fpo, ob=ob: e.tensor_tensor(out=ohT[ob][0:64, :], in0=pO[fpo][0:64, :], in1=bcs[0:64, :],
                                                                   op=ALU.mult), r=[b_pO[fpo], b_bcs], w=[b_ohT[ob]])
            S_.dma("sp", lambda e, fh=fh, fcs=fcs, ob=ob: e.dma_start(out=otscr[fh * 64:(fh + 1) * 64, fcs], in_=ohT[ob][0:64, :]),
                   b_ohT[ob], r=[b_ohT[ob]])

        npair = NT // 2

        def item_cfg(idx):
            h, qg = items[idx]
            cs = slice(qg * QG, (qg + 1) * QG)
            c = dict(h=h, cs=cs, po=idx % 2, idx=idx)
            if h < HM:
                kb, qb = h % 2, idx % 2
                c.update(kt_ap=lambda blk: KTm[kb][:, blk * 128:(blk + 1) * 128], q_ap=QTm[qb][:, :],
                         v_ap=lambda blk: VAm[kb][:, blk, 0:65], rbufs=[b_KTm[kb], b_QTm[qb], b_resm],
                         vbuf=b_VAm[kb], scale=sc_m)
            else:
                hg = h - HM
                var = (hg // 4) * 2 + hg % 2
                kv = hg // 4
                c.update(kt_ap=lambda blk: KTg[:, var, blk * 128:(blk + 1) * 128], q_ap=QTg[:, hg // 2, cs],
                         v_ap=lambda blk: VAg[:, blk, kv, 0:65], rbufs=[b_res], vbuf=b_res, scale=sc_g)
            return c

        def emit_pv(c, pi, ppt):
            def pvmm(e):
                for u in range(2):
                    blk = 2 * pi + u
                    ins = e.matmul(pO[c["po"]][0:65, :], lhsT=c["v_ap"](blk), rhs=PT[ppt][:, u * 512:(u + 1) * 512],
                                   start=(blk == 0), stop=(blk == NT - 1))
                return ins
            S_.op("pe", pvmm, r=[b_PT[ppt], c["vbuf"]], w=[b_pO[c["po"]]])

        def fin_a(c):
            po = c["po"]
            S_.op("dve", lambda e: e.reciprocal(out=rdn[64:65, :], in_=pO[po][64:65, :]), r=[b_pO[po]], w=[b_rdn])
            S_.op("dve", lambda e: e.tensor_copy(out=rh[64:65, :], in_=rdn[64:65, :]), r=[b_rdn], w=[b_rh])
            S_.op("dve", lambda e: e.tensor_tensor(out=rl[64:65, :], in0=rdn[64:65, :], in1=rh[64:65, :], op=ALU.subtract),
                  r=[b_rdn, b_rh], w=[b_rh])
            return (c["idx"], c["h"], c["cs"], po)

        pending_prep = []
        cfgs = {}
        W2GU, W2D = ffn_weight_views(PERS)
        w2steps, w2bufs = ffn_weight_loads("f2", w2gu_d, w2d_d, W2GU, W2D)
        n_gu = 2 * DFF // 512
        mla_dead = [b_resm, b_KTm[0], b_KTm[1], b_VAm[0], b_VAm[1], b_tabf, b_wq]
        w2_gu_steps = []
        if HOLE_END == PERS + WGU_BYTES and HOLE0 <= HOLE_END:
            wgu_v2 = w2gu_d.rearrange("(k p) n -> p k n", p=128)
            for cg in range(n_gu):
                w2steps.pop(0)
                def st(cg=cg, b=w2bufs[cg % 4]):
                    S_.dma("pool", lambda e: e.dma_start(out=W2GU[:, :, cg * 512:(cg + 1) * 512],
                                                         in_=wgu_v2[:, :, cg * 512:(cg + 1) * 512]), b, w=[b] + mla_dead)
                w2_gu_steps.append(st)

        def cfg_of(idx):
            if idx not in cfgs:
                cfgs[idx] = item_cfg(idx)
            return cfgs[idx]

        nsteps = len(items) * npair

        def emit_S(k):
            idx, i = k // npair, k % npair
            c = cfg_of(idx)
            sb = k % 2

            def smm(e):
                for u in range(2):
                    ins = e.matmul(pS[sb][:, u * 512:(u + 1) * 512], lhsT=c["kt_ap"](2 * i + u), rhs=c["q_ap"],
                                   start=True, stop=True)
                return ins
            S_.op("pe", smm, r=c["rbufs"], w=[b_pS[sb]])

        def emit_exp(k):
            c = cfg_of(k // npair)
            sb, pt = k % 2, k % 3
            S_.op("act", lambda e: e.activation(out=PT[pt], in_=pS[sb], func=AF.Exp, scale=c["scale"]),
                  r=[b_pS[sb]], w=[b_PT[pt]])

        emit_S(0)
        for k in range(nsteps):
            idx, i = k // npair, k % npair
            h, qg = items[idx]
            if i == 0:
                if idx + 1 < len(items):
                    prep_q(idx + 1)
                if qg == 0 and h + 1 < HM:
                    pending_prep = prep_k_steps(h + 1)
            emit_exp(k)
            if k + 1 < nsteps:
                emit_S(k + 1)
            if k >= 1:
                pidx, pi = (k - 1) // npair, (k - 1) % npair
                emit_pv(cfg_of(pidx), pi, (k - 1) % 3)
                if pi == npair - 1:
                    pend_fin = fin_a(cfg_of(pidx))
            if i == min(8, npair - 1) and pend_fin is not None:
                fin_b(pend_fin)
                pend_fin = None
            if w2_gu_steps and h >= HM and (idx - HM * NQG) >= 1:
                w2_gu_steps.pop(0)()
            if pending_prep and (i >= 4 or i == npair - 1):
                n_emit = 1 if i < npair - 1 else (len(pending_prep) if qg == NQG - 1 else 1)
                for _ in range(n_emit):
                    pending_prep.pop(0)()
        emit_pv(cfg_of((nsteps - 1) // npair), (nsteps - 1) % npair, (nsteps - 1) % 3)
        pend_fin = fin_a(cfg_of((nsteps - 1) // npair))
        fin_b(pend_fin)
        while w2_gu_steps:
            w2_gu_steps.pop(0)()
        S_.barrier()

        A = Arena(PERS + W_BYTES)
        Wo = V(A.take(8 * D * 2), BF16, [128, 8, D])
        Gm_ = V(A.take(4096), F32, [128, D])
        goc = V(A.take(64), F32, [128, 8])
        wos = V(A.take(16384), F32, [128, 4, D])
        x3b = [V(A.take(4096), F32, [128, D]) for _ in range(2)]
        oTg = [V(A.take(8 * QG * 2), BF16, [128, 8, QG]) for _ in range(2)]
        tq = V(A.take(512 * 4), F32, [128, 512])
        dg = V(A.take(256 * 4), F32, [128, 256])
        st3 = [V(A.take(32), F32, [128, 8]) for _ in range(2)]
        b_Wo, b_Gm, b_goc, b_wos = S_.buf("Wo"), S_.buf("Gm"), S_.buf("goc"), S_.buf("wos")
        b_x3 = [S_.buf("x3_%d" % i) for i in range(2)]
        b_oTg = [S_.buf("oTg%d" % i) for i in range(2)]
        b_tq, b_dg = S_.buf("tq"), S_.buf("dg")
        b_st3 = [S_.buf("st3_%d" % i) for i in range(2)]
        b_pa = [S_.buf("pa%d" % i) for i in range(2)]
        b_pb = [S_.buf("pb%d" % i) for i in range(2)]
        b_pg = S_.buf("pgram")
        pa = [PB(i) for i in range(2)]
        pbb = [PB(2 + i) for i in range(2)]
        pg = PB(4)
        S_.dma("sp", lambda e: e.dma_start(out=Gm_, in_=modscr[5]), b_Gm, w=[b_Gm])
        S_.dma("sp", lambda e: e.dma_start(out=goc, in_=gout_d), b_goc, w=[b_goc])
        wout_v = wout_d.rearrange("(c p) n -> p c n", p=128)
        for c0 in (0, 4):
            S_.dma("sp", lambda e, c0=c0: e.dma_start(out=wos, in_=wout_v[:, c0:c0 + 4, :]), b_wos, w=[b_wos])
            for c in range(c0, c0 + 4):
                S_.op("dve",
                      lambda e, c=c, c0=c0: e.scalar_tensor_tensor(out=Wo[:, c, :], in0=wos[:, c - c0, :], scalar=goc[:, c:c + 1],
                                                                   in1=Gm_, op0=ALU.mult, op1=ALU.mult),
                      r=[b_wos, b_goc, b_Gm], w=[b_Wo])
        S_.barrier()
        x3b = x3b + [wos[:, 0, :], wos[:, 1, :]]
        b_x3 = b_x3 + [S_.buf("x3_2"), S_.buf("x3_3")]
        NX3 = len(x3b)
        otv = otscr.rearrange("(c p) n -> p c n", p=128)
        tq2 = [tq, V(A.take(512 * 4), F32, [128, 512])]
        tq4 = [[tq2[0], V(A.take(512 * 4), F32, [128, 512])], [tq2[1], V(A.take(512 * 4), F32, [128, 512])]]
        b_tq4 = [[S_.buf("tq4_%d%d" % (i, j)) for j in range(2)] for i in range(2)]
        dg2 = [dg, V(A.take(256 * 4), F32, [128, 256])]
        b_tq2 = [b_tq, S_.buf("tq1")]
        b_dg2 = [b_dg, S_.buf("dg1")]
        pg2 = [PB(4), PB(5)]
        b_pg2 = [b_pg, S_.buf("pgram1")]

        def p35_load(t):
            qg, tt = t // 4, t % 4
            og = qg % 2
            xq = t % NX3
            S_.dma("sp", lambda e: e.dma_start(out=x3b[xq], in_=x1scr[t * 128:(t + 1) * 128, :]), b_x3[xq], w=[b_x3[xq]])

        def p35_compute(t):
            qg, tt = t // 4, t % 4
            og = qg % 2
            xs = t % 2
            st = st3[xs]
            bs = b_st3[xs]
            tcs = slice(tt * 128, (tt + 1) * 128)
            pgx, tqx, dgx = pg2[xs], tq2[xs], dg2[xs]

            def gram(e):
                for grp in range(2):
                    for c in range(4):
                        cc = grp * 4 + c
                        ins = e.matmul(pgx[:, grp * 128:(grp + 1) * 128], lhsT=oTg[og][:, cc, tcs], rhs=oTg[og][:, cc, tcs],
                                       start=(c == 0), stop=(c == 3))
                return ins
            S_.op("pe", gram, r=[b_oTg[og]], w=[b_pg2[xs]])
            for grp in range(2):
                S_.op("dve", lambda e, grp=grp: e.tensor_tensor(out=dgx[:, grp * 128:(grp + 1) * 128],
                                                                in0=pgx[:, grp * 128:(grp + 1) * 128], in1=identf, op=ALU.mult),
                      r=[b_pg2[xs], b_const], w=[b_dg2[xs]])
            S_.op("dve", lambda e: e.tensor_reduce(out=st[:, 0:2], in_=dgx.rearrange("p (g d) -> p g d", d=128),
                                                   axis=AX.X, op=ALU.add), r=[b_dg2[xs]], w=[bs])
            S_.op("act", lambda e: e.activation(out=st[:, 2:4], in_=st[:, 0:2], func=AF.Sqrt, bias=epsT[:, 0:1],
                                                scale=1.0 / 512.0), r=[bs], w=[bs])
            S_.op("dve", lambda e: e.reciprocal(out=st[:, 4:6], in_=st[:, 2:4]), r=[bs], w=[bs])

        def p35_out(t):
            qg, tt = t // 4, t % 4
            og = qg % 2
            xs = t % 2
            st = st3[xs]
            bs = b_st3[xs]
            tcs = slice(tt * 128, (tt + 1) * 128)
            xq = t % NX3
            for hf in range(2):
                hs = slice(hf * 512, (hf + 1) * 512)

                def omm(e, hs=hs, hf=hf):
                    for c in range(4):
                        e.matmul(pa[hf], lhsT=oTg[og][:, c, tcs], rhs=Wo[:, c, hs], start=(c == 0), stop=(c == 3))
                    for c in range(4):
                        ins = e.matmul(pbb[hf], lhsT=oTg[og][:, 4 + c, tcs], rhs=Wo[:, 4 + c, hs], start=(c == 0), stop=(c == 3))
                    return ins
                S_.op("pe", omm, r=[b_oTg[og], b_Wo], w=[b_pa[hf], b_pb[hf]])
                tqh = tq4[xs][hf]
                btq = b_tq4[xs][hf]
                S_.op("act", lambda e, hf=hf, tqh=tqh: e.activation(out=tqh, in_=pa[hf], func=AF.Copy, scale=st[:, 4:5]),
                      r=[b_pa[hf], bs], w=[btq])
                S_.op("dve", lambda e, hf=hf, tqh=tqh: e.scalar_tensor_tensor(out=tqh, in0=pbb[hf], scalar=st[:, 5:6], in1=tqh,
                                                                              op0=ALU.mult, op1=ALU.add),
                      r=[b_pb[hf], bs], w=[btq])
                S_.op("pool", lambda e, hs=hs, tqh=tqh: e.tensor_tensor(out=x3b[xq][:, hs], in0=x3b[xq][:, hs], in1=tqh, op=ALU.add),
                      r=[btq], w=[b_x3[xq]])
            S_.dma("sp", lambda e: e.dma_start(out=x1scr[t * 128:(t + 1) * 128, :], in_=x3b[xq]), b_x3[xq], r=[b_x3[xq]])

        def otg_load(q2):
            if q2 < NQG:
                S_.dma("sp", lambda e: e.dma_start(out=oTg[q2 % 2], in_=otv[:, :, q2 * QG:(q2 + 1) * QG]),
                       b_oTg[q2 % 2], w=[b_oTg[q2 % 2]])

        otg_load(0)
        otg_load(1)
        p35_load(0)
        if NT > 1:
            p35_load(1)
        p35_compute(0)
        for t in range(NT):
            if t + 2 < NT:
                p35_load(t + 2)
            if t + 1 < NT:
                p35_compute(t + 1)
            p35_out(t)
            if t % 4 == 3 and t + 1 < NT:
                otg_load(t // 4 + 2)
            for _ in range(max(1, (len(w2steps) + NT - 1) // NT) if w2steps else 0):
                if w2steps:
                    w2steps.pop(0)()
        while w2steps:
            w2steps.pop(0)()
        S_.barrier()

        ffn_phase("f2", w2gu_d, w2d_d, 7, 6, 8, x1scr, True, preloaded=True)

        S_.emit(nc, stack)
    return nc


def _tables(S):
    tok = np.arange(S)
    row = (tok // GRID_W).astype(np.float64)
    col = (tok % GRID_W).astype(np.float64)

    def tab(dim):
        q = dim // 4
        axis_dim = dim // 2
        inv = THETA ** (-(np.arange(q, dtype=np.float64) * 2.0 / axis_dim))
        inv = inv.astype(np.float32).astype(np.float64)
        ar = (row[:, None].astype(np.float32) * inv[None, :].astype(np.float32)).astype(np.float64)
        ac = (col[:, None].astype(np.float32) * inv[None, :].astype(np.float32)).astype(np.float64)
        cos = np.concatenate([np.cos(ar), np.cos(ar), np.cos(ac), np.cos(ac)], axis=1)
        sin = np.concatenate([-np.sin(ar), np.sin(ar), -np.sin(ac), np.sin(ac)], axis=1)
        return cos.astype(np.float32), sin.astype(np.float32)

    cosm, sinm = tab(32)
    cosg, sing = tab(64)
    return cosm, sinm, cosg, sing


_ROT32 = np.concatenate([np.arange(8, 16), np.arange(0, 8), np.arange(24, 32), np.arange(16, 24)])


def make_in_maps(inp, S):
    B = inp["x"].shape[0]
    f = lambda a: np.ascontiguousarray(np.asarray(a, dtype=np.float32))
    cosm, sinm, cosg, sing = _tables(S)
    w_uq = f(inp["w_uq"][0])
    idx = np.arange(768).reshape(8, 96)
    idx_rot = idx.copy()
    idx_rot[:, 64:96] = idx[:, 64:96][:, _ROT32]
    w_uq_rot = np.ascontiguousarray(w_uq[:, idx_rot.reshape(-1)])
    w_ukv = f(inp["w_ukv"][0]).reshape(128, 8, 128)
    shared = {
        "w_ada": f(inp["w_ada"][0]), "b_ada": f(inp["b_ada"][0]).reshape(1, -1),
        "gvecs": f(np.stack([inp["g_ffn1"][0], inp["g_mix"][0], inp["g_ffn2"][0], inp["g_final"]])),
        "w1_gu": f(inp["w1_gu"][0]), "w1_down": f(inp["w1_down"][0]),
        "w2_gu": f(inp["w2_gu"][0]), "w2_down": f(inp["w2_down"][0]),
        "w_in": f(inp["w_in"][0]), "w_uq": w_uq, "w_uq_rot": w_uq_rot,
        "w_ukv_k": f(w_ukv[:, :, 0:64].reshape(128, 512)), "w_ukv_v": f(w_ukv[:, :, 64:128].reshape(128, 512)),
        "g_q_lat_col": f(np.asarray(inp["g_q_lat"][0]).reshape(2, 128).T),
        "g_kv_lat_col": f(np.asarray(inp["g_kv_lat"][0]).reshape(1, 128).T),
        "g_qk_row": f(np.concatenate([np.tile(np.asarray(inp["g_qhead"][0]), 8), np.tile(np.asarray(inp["g_khead"][0]), 2)]).reshape(1, 640)),
        "g_out_col": f(np.concatenate([np.asarray(inp["g_out_mla"][0]), np.asarray(inp["g_out_gqa"][0])]).reshape(8, 128).T),
        "w_out": f(inp["w_out"][0]),
        "cosm": cosm, "sinm": sinm, "cosg": cosg, "sing": sing,
        "cosf": f(cosm.T), "sinf": f(sinm.T),
    }
    maps = []
    x = np.asarray(inp["x"], dtype=np.float32)
    c = np.asarray(inp["c"], dtype=np.float32)
    for b in range(B):
        m = dict(shared)
        m["x"] = np.ascontiguousarray(x[b])
        m["c_col"] = np.ascontiguousarray(c[b].reshape(8, 128).T)
        maps.append(m)
    return maps


_CACHE = {}


def kernel(**inputs):
    x = np.asarray(inputs["x"])
    B, S, _ = x.shape
    if S not in _CACHE:
        _CACHE[S] = build_program(S)
    nc = _CACHE[S]
    in_maps = make_in_maps(inputs, S)
    res = run_bass_kernel_spmd(nc, in_maps, core_ids=list(range(B)))
    return np.stack([np.asarray(r["y"], dtype=np.float32) for r in res.results], axis=0)
```

```python
import math
from contextlib import ExitStack

import numpy as np
import concourse.bass as bass
import concourse.mybir as mybir
from concourse.bass_utils import run_bass_kernel_spmd

F32 = mybir.dt.float32
BF16 = mybir.dt.bfloat16
AF = mybir.ActivationFunctionType
ALU = mybir.AluOpType
AX = mybir.AxisListType

D = 1024
DFF = 2816
NJ = DFF // 128
NMOD = 9
GRID_W = 64
THETA = 10000.0
EPS = 1e-6
HM, HG = 8, 8
ARENA_BYTES = 212736


class Buf:
    __slots__ = ("name", "lw", "rd", "dsem", "dcnt", "dseen", "excl")

    def __init__(self, name):
        self.name = name
        self.excl = False
        self.lw = {}
        self.rd = {}
        self.dsem = None
        self.dcnt = 0
        self.dseen = 0


class Op:
    __slots__ = ("eng", "fn", "deps", "dwaits", "signal", "tick", "dbuf")

    def __init__(self, eng, fn):
        self.eng = eng
        self.fn = fn
        self.deps = []
        self.dwaits = []
        self.signal = False
        self.tick = 0
        self.dbuf = None


COMPUTE = ("act", "pool", "dve", "pe")
ENGS = ("sp", "act", "pool", "dve", "pe")


class Sched:
    def __init__(self):
        self.ops = {e: [] for e in ENGS}
        self.bufs = []

    def buf(self, name, excl=False):
        b = Buf(name)
        b.excl = excl
        self.bufs.append(b)
        return b

    def _gather(self, op, r, w):
        for b in r:
            for o in b.lw.values():
                op.deps.append(o)
            if b.excl:
                for e2, o in b.rd.items():
                    if e2 != op.eng:
                        op.deps.append(o)
            if b.dcnt:
                op.dwaits.append((b, b.dcnt))
        for b in w:
            for o in b.lw.values():
                op.deps.append(o)
            for o in b.rd.values():
                op.deps.append(o)
            if b.dcnt:
                op.dwaits.append((b, b.dcnt))

    def op(self, eng, fn, r=(), w=()):
        op = Op(eng, fn)
        self._gather(op, r, w)
        for b in r:
            b.rd[eng] = op
        for b in w:
            b.lw[eng] = op
            b.rd = {}
        self.ops[eng].append(op)
        return op

    def dma(self, eng, fn, track, r=(), w=()):
        op = Op(eng, fn)
        self._gather(op, r, w)
        op.dbuf = track
        track.dcnt += 1
        if track in w:
            track.lw = {}
            track.rd = {}
        self.ops[eng].append(op)
        return op

    def barrier(self):
        last = {}
        for e in COMPUTE:
            last[e] = None
            for o in reversed(self.ops[e]):
                if o.fn is None:
                    break
                if o.dbuf is None:
                    last[e] = o
                    break
        dl = [(b, b.dcnt) for b in self.bufs if b.dcnt > b.dseen]
        for b in self.bufs:
            b.dseen = b.dcnt
            b.lw = {}
            b.rd = {}
        for e in ENGS:
            op = Op(e, None)
            for e2 in COMPUTE:
                o = last[e2]
                while o is not None and o.fn is None:
                    o = None
                if o is not None:
                    op.deps.append(o)
            op.dwaits = list(dl)
            self.ops[e].append(op)

    def emit(self, nc, stack):
        for e in ENGS:
            for op in self.ops[e]:
                for d in op.deps:
                    if d.fn is None:
                        continue
                    if d.eng == "pe" and op.eng == "pe":
                        continue
                    d.signal = True
        esem = {e: stack.enter_context(nc.semaphore("s_" + e)) for e in COMPUTE}
        for e in COMPUTE:
            t = 0
            for op in self.ops[e]:
                if op.signal:
                    t += 1
                    op.tick = t
        for b in self.bufs:
            if b.dcnt:
                b.dsem = stack.enter_context(nc.semaphore("d_" + b.name))
        block = stack.enter_context(nc.Block())
        sched = self

        def run(eng_name):
            def body(eng):
                seen = {}
                for op in sched.ops[eng_name]:
                    need = {}
                    for d in op.deps:
                        if d.fn is None:
                            continue
                        if d.eng == "pe" and eng_name == "pe":
                            continue
                        s = esem[d.eng]
                        if d.tick > need.get(s, (0,))[0]:
                            need[s] = (d.tick,)
                    for b, c in op.dwaits:
                        s = b.dsem
                        if 16 * c > need.get(s, (0,))[0]:
                            need[s] = (16 * c,)
                    for s, (v,) in need.items():
                        if seen.get(s, 0) < v:
                            eng.wait_ge(s, v)
                            seen[s] = v
                    if op.fn is None:
                        continue
                    ins = op.fn(eng)
                    if op.dbuf is not None:
                        ins.then_inc(op.dbuf.dsem, 16)
                    elif op.signal:
                        ins.then_inc(esem[eng_name], 1)
            return body

        block.sync(run("sp"))
        block.scalar(run("act"))
        block.gpsimd(run("pool"))
        block.vector(run("dve"))
        block.tensor(run("pe"))


def build_program(S, debug=False):
    NT = S // 128
    GT = 256
    TPG = GT // 128
    NG = S // GT
    QG = 512
    NQG = S // QG
    nc = bass.Bass("TRN2", target_bir_lowering=False)

    def din(name, shape, dt=F32):
        return nc.dram_tensor(name, list(shape), dt, kind="ExternalInput").ap()

    x_d = din("x", [S, D])
    ccol_d = din("c_col", [128, 8])
    wada_d = din("w_ada", [D, NMOD * D])
    bada_d = din("b_ada", [1, NMOD * D])
    gvec_d = din("gvecs", [4, D])
    w1gu_d = din("w1_gu", [D, 2 * DFF])
    w1d_d = din("w1_down", [DFF, D])
    w2gu_d = din("w2_gu", [D, 2 * DFF])
    w2d_d = din("w2_down", [DFF, D])
    win_d = din("w_in", [D, 1184])
    wuq_d = din("w_uq", [256, 768])
    wuqr_d = din("w_uq_rot", [256, 768])
    wuk_d = din("w_ukv_k", [128, 512])
    wuv_d = din("w_ukv_v", [128, 512])
    gql_d = din("g_q_lat_col", [128, 2])
    gkvl_d = din("g_kv_lat_col", [128, 1])
    gqk_d = din("g_qk_row", [1, 640])
    gout_d = din("g_out_col", [128, 8])
    wout_d = din("w_out", [D, D])
    cosm_d = din("cosm", [S, 32])
    sinm_d = din("sinm", [S, 32])
    cosg_d = din("cosg", [S, 64])
    sing_d = din("sing", [S, 64])
    cosf_d = din("cosf", [32, S])
    sinf_d = din("sinf", [32, S])
    y_d = nc.dram_tensor("y", [S, D], F32, kind="ExternalOutput").ap()
    skind = "ExternalOutput" if debug else "Internal"
    modscr = nc.dram_tensor("modscr", [NMOD, 128, D], F32, kind=skind).ap()
    x1scr = nc.dram_tensor("x1scr", [S, D], F32, kind=skind).ap()
    otscr = nc.dram_tensor("otscr", [D, S], BF16, kind=skind).ap()

    S_ = Sched()
    stack = ExitStack()
    with stack:
        big = stack.enter_context(nc.sbuf_tensor("arena", [128, ARENA_BYTES // 4], F32))
        pall = stack.enter_context(nc.psum_tensor("psum", [128, 4096], F32))

        def V(off, dt, shape):
            esz = 4 if dt == F32 else 2
            n = 1
            for s in shape[1:]:
                n *= s
            nb = n * esz
            assert off % 4 == 0 and nb % 4 == 0, (off, nb)
            assert off + nb <= ARENA_BYTES, (off, nb)
            ap = big[:, off // 4:(off + nb) // 4]
            if dt != F32:
                ap = ap.bitcast(dt)
            if len(shape) == 3:
                ap = ap.rearrange("p (a b) -> p a b", a=shape[1], b=shape[2])
            elif len(shape) == 4:
                ap = ap.rearrange("p (a b c) -> p a b c", a=shape[1], b=shape[2], c=shape[3])
            return ap

        def PB(bank, dt=F32):
            ap = pall[:, bank * 512:(bank + 1) * 512]
            if dt != F32:
                ap = ap.bitcast(dt)
            return ap

        class Arena:
            def __init__(self, base):
                self.o = base

            def take(self, nbytes):
                o = self.o
                self.o += (nbytes + 63) // 64 * 64
                assert self.o <= ARENA_BYTES, self.o
                return o

        A0 = Arena(0)
        ident = V(A0.take(256), BF16, [128, 128])
        identf = V(A0.take(512), F32, [128, 128])
        onesf = V(A0.take(256), F32, [128, 64])
        epsT = V(A0.take(64), F32, [128, 1])
        onesb = V(A0.take(128), BF16, [128, 64])
        b_const = S_.buf("const")
        S_.op("pool", lambda e: e.memset(identf, 0.0), w=[b_const])
        S_.op("pool", lambda e: e.affine_select(out=identf, in_=identf, pattern=[[-1, 128]],
                                                compare_op=ALU.not_equal, fill=1.0, base=0,
                                                channel_multiplier=1), w=[b_const])
        S_.op("pool", lambda e: e.tensor_copy(out=ident, in_=identf), r=[b_const], w=[b_const])
        S_.op("pool", lambda e: e.memset(onesf, 1.0), w=[b_const])
        S_.op("pool", lambda e: e.memset(epsT, EPS), w=[b_const])
        S_.op("pool", lambda e: e.memset(onesb, 1.0), w=[b_const])
        PERS = A0.o
        S_.barrier()

        def rstd_ops(ms_ap, out_ap, bufs_r, bufs_w, tmp_ap):
            S_.op("act", lambda e: e.activation(out=tmp_ap, in_=ms_ap, func=AF.Sqrt, bias=epsT[:, 0:1], scale=1.0),
                  r=bufs_r, w=bufs_w)
            S_.op("dve", lambda e: e.reciprocal(out=out_ap, in_=tmp_ap), r=bufs_w, w=bufs_w)

        W_BYTES = 8 * 2 * DFF * 2 + NJ * D * 2

        def ffn_weight_views(base):
            WGU = V(base, BF16, [128, 8, 2 * DFF])
            WD = V(base + 8 * 2 * DFF * 2, BF16, [128, NJ, D])
            return WGU, WD

        def ffn_weight_loads(tag, wgu_d, wd_d, WGU, WD):
            bufs = [S_.buf(tag + "w%d" % i) for i in range(4)]
            wgu_v = wgu_d.rearrange("(k p) n -> p k n", p=128)
            wd_v = wd_d.rearrange("(j p) n -> p j n", p=128)
            steps = []
            n = 0
            for cg in range(2 * DFF // 512):
                def st(cg=cg, b=bufs[n % 4]):
                    S_.dma("pool", lambda e: e.dma_start(out=WGU[:, :, cg * 512:(cg + 1) * 512],
                                                         in_=wgu_v[:, :, cg * 512:(cg + 1) * 512]), b, w=[b])
                steps.append(st)
                n += 1
            for j0 in range(0, NJ, 4):
                nj = min(4, NJ - j0)
                for c0 in (0, 512):
                    def st(j0=j0, nj=nj, c0=c0, b=bufs[n % 4]):
                        S_.dma("pool", lambda e: e.dma_start(out=WD[:, j0:j0 + nj, c0:c0 + 512],
                                                             in_=wd_v[:, j0:j0 + nj, c0:c0 + 512]), b, w=[b])
                    steps.append(st)
                    n += 1
            return steps, bufs

        W1GU, W1D = ffn_weight_views(PERS)
        w1steps, w1bufs = ffn_weight_loads("f1", w1gu_d, w1d_d, W1GU, W1D)
        n_gu_steps = 2 * DFF // 512
        w1gu_steps = [w1steps.pop(0) for _ in range(n_gu_steps)]
        A = Arena(PERS + 8 * 2 * DFF * 2)
        cact = V(A.take(32), F32, [128, 8])
        chb = V(A.take(16), BF16, [128, 8])
        clb = V(A.take(16), BF16, [128, 8])
        creph = V(A.take(2048), BF16, [128, 8, 128])
        crepl = V(A.take(2048), BF16, [128, 8, 128])
        gb = [V(A.take(4096), F32, [128, D]) for _ in range(3)]
        bb = [V(A.take(4096), F32, [128, D]) for _ in range(2)]
        mo = [V(A.take(4096), F32, [128, D]) for _ in range(2)]
        wst = [V(A.take(16384), F32, [128, 8, 512]) for _ in range(3)]
        whb = [V(A.take(8192), BF16, [128, 8, 512]) for _ in range(2)]
        wlb = [V(A.take(8192), BF16, [128, 8, 512]) for _ in range(2)]
        b_c = S_.buf("cact")
        b_gb = [S_.buf("gb%d" % i) for i in range(3)]
        b_bb = [S_.buf("bb%d" % i) for i in range(2)]
        b_mo = [S_.buf("mo%d" % i) for i in range(2)]
        b_wst = [S_.buf("wst%d" % i) for i in range(3)]
        b_wh = [S_.buf("wh%d" % i) for i in range(2)]
        b_wl = [S_.buf("wl%d" % i) for i in range(2)]
        b_pm = [S_.buf("pm%d" % i) for i in range(4)]
        S_.dma("sp", lambda e: e.dma_start(out=cact, in_=ccol_d), b_c, w=[b_c])
        for i in range(3):
            S_.dma("sp", lambda e, i=i: e.dma_start(out=gb[i], in_=gvec_d[i:i + 1, :].partition_broadcast(128)),
                   b_gb[i], w=[b_gb[i]])
        S_.op("act", lambda e: e.activation(out=cact, in_=cact, func=AF.Silu), r=[b_c], w=[b_c])
        S_.op("dve", lambda e: e.tensor_copy(out=chb, in_=cact), r=[b_c], w=[b_c])
        S_.op("dve", lambda e: e.tensor_tensor(out=clb, in0=cact, in1=chb, op=ALU.subtract), r=[b_c], w=[b_c])
        for k in range(8):
            S_.op("dve", lambda e, k=k: e.tensor_copy(out=creph[:, k, :], in_=chb[:, k:k + 1].to_broadcast([128, 128])),
                  r=[b_c], w=[b_c])
            S_.op("dve", lambda e, k=k: e.tensor_copy(out=crepl[:, k, :], in_=clb[:, k:k + 1].to_broadcast([128, 128])),
                  r=[b_c], w=[b_c])
        wada_v = wada_d.rearrange("(k p) n -> p k n", p=128)
        def p0_load(jh):
            j, hf = jh // 2, jh % 2
            w3 = jh % 3
            S_.dma("sp", lambda e: e.dma_start(out=wst[w3], in_=wada_v[:, :, j * D + hf * 512:j * D + (hf + 1) * 512]),
                   b_wst[w3], w=[b_wst[w3]])
            if hf == 0:
                S_.dma("sp", lambda e: e.dma_start(out=bb[j % 2], in_=bada_d[0:1, j * D:(j + 1) * D].partition_broadcast(128)),
                       b_bb[j % 2], w=[b_bb[j % 2]])

        b_wl2 = [[S_.buf("wl%d_%d" % (i, q)) for q in range(2)] for i in range(2)]
        p0_load(0)
        p0_load(1)
        for jh in range(2 * NMOD):
            j, hf = jh // 2, jh % 2
            ws = jh % 2
            w3 = jh % 3
            ms = j % 2
            if jh + 2 < 2 * NMOD:
                p0_load(jh + 2)
            if w1gu_steps:
                w1gu_steps.pop(0)()
            S_.op("act", lambda e, ws=ws, w3=w3: e.activation(out=whb[ws], in_=wst[w3], func=AF.Copy), r=[b_wst[w3]], w=[b_wh[ws]])
            for q in range(2):
                qsl = slice(q * 256, (q + 1) * 256)
                S_.op("dve" if q == 0 else "pool",
                      lambda e, ws=ws, w3=w3, qsl=qsl: e.tensor_tensor(out=wlb[ws][:, :, qsl], in0=wst[w3][:, :, qsl],
                                                                       in1=whb[ws][:, :, qsl], op=ALU.subtract),
                      r=[b_wst[w3], b_wh[ws]], w=[b_wl2[ws][q]])
            pb = jh % 4
            ps = PB(pb)

            def mm(e, ws=ws, ps=ps):
                n = 0
                for (cr, wb) in ((creph, whb), (crepl, whb), (creph, wlb)):
                    for k in range(8):
                        ins = e.matmul(ps, lhsT=cr[:, k, :], rhs=wb[ws][:, k, :], start=(n == 0), stop=(n == 23))
                        n += 1
                return ins
            S_.op("pe", mm, r=[b_c, b_wh[ws], b_wl2[ws][0], b_wl2[ws][1]], w=[b_pm[pb]])
            osl = mo[ms][:, hf * 512:(hf + 1) * 512]
            bsl = bb[ms][:, hf * 512:(hf + 1) * 512]
            S_.op("dve", lambda e, osl=osl, bsl=bsl, ps=ps: e.tensor_tensor(out=osl, in0=ps, in1=bsl, op=ALU.add),
                  r=[b_pm[pb], b_bb[ms]], w=[b_mo[ms]])
            if hf == 0:
                continue
            kind = j % 3
            if kind == 1:
                gsel = gb[j // 3]
                S_.op("dve", lambda e, ms=ms, gsel=gsel: e.scalar_tensor_tensor(
                    out=mo[ms], in0=mo[ms], scalar=1.0, in1=gsel, op0=ALU.add, op1=ALU.mult),
                    r=[b_gb[j // 3]], w=[b_mo[ms]])
            elif kind == 2 and j != 5:
                S_.op("dve", lambda e, ms=ms: e.tensor_scalar(out=mo[ms], in0=mo[ms], scalar1=0.5, scalar2=None,
                                                              op0=ALU.mult), w=[b_mo[ms]])
            S_.dma("sp", lambda e, j=j, ms=ms: e.dma_start(out=modscr[j], in_=mo[ms]), b_mo[ms], r=[b_mo[ms]])
        while w1gu_steps:
            w1gu_steps.pop(0)()
        S_.barrier()

        def ffn_phase(tag, wgu_d, wd_d, iA, iB, iG, src_d, final, preloaded=False, pending=None):
            A = Arena(PERS)
            WGU, WD = ffn_weight_views(A.take(W_BYTES))
            At = V(A.take(4096), F32, [128, D])
            Bt = V(A.take(4096), F32, [128, D])
            Gt = V(A.take(4096), F32, [128, D])
            Gf = V(A.take(4096), F32, [128, D])
            gtmp = [V(A.take(2048), F32, [128, 512])] * 2
            xb = [V(A.take(4096), F32, [128, D]) for _ in range(3 * TPG)]
            tmpb = [V(A.take(4096), F32, [128, D]) for _ in range(2)]
            hbf = [V(A.take(2048), BF16, [128, D]) for _ in range(2)]
            hT = V(A.take(8 * GT * 2), BF16, [128, 8, GT])
            gact = [V(A.take(GT * 4), F32, [128, GT]) for _ in range(2)]
            actT = V(A.take(NJ * GT * 2), BF16, [128, NJ, GT])
            junk = V(A.take(2048), BF16, [128, D])
            stat = V(A.take(64 * 4), F32, [128, 64])

            bn = lambda n: S_.buf(tag + n)
            b_A, b_B, b_G, b_Gf = bn("A"), bn("B"), bn("G"), bn("Gf")
            b_gtmp = [bn("gtmp")] * 2
            b_x = [bn("x%d" % i) for i in range(3 * TPG)]
            b_tmp = [bn("tmp%d" % i) for i in range(2)]
            b_hbf = [bn("hbf%d" % i) for i in range(2)]
            b_hT, b_actT, b_junk = bn("hT"), bn("actT"), bn("junk")
            b_gact = [bn("gact%d" % i) for i in range(2)]
            b_stat = [bn("stat%d" % i) for i in range(2 * TPG)]
            b_pgu = [bn("pgu%d" % i) for i in range(4)]
            b_pT = bn("pT")
            b_pd = [bn("pd%d" % i) for i in range(3)]
            pgu = [PB(i) for i in range(4)]
            pT = PB(4, BF16).rearrange("p (a b) -> p a b", a=8, b=128)
            pd = [PB(5 + i) for i in range(3)]

            S_.dma("sp", lambda e: e.dma_start(out=At, in_=modscr[iA]), b_A, w=[b_A])
            S_.dma("sp", lambda e: e.dma_start(out=Bt, in_=modscr[iB]), b_B, w=[b_B])
            S_.dma("sp", lambda e: e.dma_start(out=Gt, in_=modscr[iG]), b_G, w=[b_G])
            if final:
                S_.dma("sp", lambda e: e.dma_start(out=Gf, in_=gvec_d[3:4, :].partition_broadcast(128)), b_Gf, w=[b_Gf])
            if preloaded:
                wbufs = []
            elif pending is not None:
                wsteps, wbufs = pending
                for st_ in wsteps:
                    st_()
            else:
                wsteps, wbufs = ffn_weight_loads(tag, wgu_d, wd_d, WGU, WD)
                for st_ in wsteps:
                    st_()
            b_wgu_l = list(wbufs)
            b_wd_l = list(wbufs)

            b_nst = [bn("nst%d" % p) for p in range(3)]
            b_fst = bn("fst")

            def x_load(g):
                p = g % 3
                for tt in range(TPG):
                    t = g * TPG + tt
                    xs = p * TPG + tt
                    S_.dma("sp", lambda e, t=t, xs=xs: e.dma_start(out=xb[xs], in_=src_d[t * 128:(t + 1) * 128, :]),
                           b_x[xs], w=[b_x[xs]])

            def norm_load_sq(g):
                p = g % 3
                for tt in range(TPG):
                    t = g * TPG + tt
                    xs = p * TPG + tt
                    S_.op("act", lambda e, xs=xs, c=p * 8 + tt: e.activation(out=junk, in_=xb[xs], func=AF.Square, scale=1.0 / 32.0,
                                                                             accum_out=stat[:, c:c + 1]),
                          r=[b_x[xs]], w=[b_junk, b_nst[p]])

            def norm_rstd(g):
                p = g % 3
                rstd_ops(stat[:, p * 8:p * 8 + TPG], stat[:, p * 8 + 4:p * 8 + 4 + TPG], [b_nst[p]], [b_nst[p]],
                         stat[:, p * 8 + 2:p * 8 + 2 + TPG])

            def norm_apply(g, tt):
                p = g % 3
                t = g * TPG + tt
                xs = p * TPG + tt
                ts = t % 2
                c = p * 8 + 4 + tt
                S_.op("act", lambda e: e.activation(out=tmpb[ts], in_=xb[xs], func=AF.Copy, scale=stat[:, c:c + 1]),
                      r=[b_x[xs], b_nst[p]], w=[b_tmp[ts]])
                S_.op("pool", lambda e: e.tensor_tensor(out=tmpb[ts], in0=tmpb[ts], in1=At, op=ALU.mult), r=[b_A], w=[b_tmp[ts]])
                S_.op("pool", lambda e: e.tensor_tensor(out=hbf[ts], in0=tmpb[ts], in1=Bt, op=ALU.add),
                      r=[b_tmp[ts], b_B], w=[b_hbf[ts]])

            def transpose_group(g):
                for tt in range(TPG):
                    t = g * TPG + tt
                    ts = t % 2

                    def tr(e, ts=ts):
                        for k in range(8):
                            ins = e.transpose(out=pT[:, k, :], in_=hbf[ts][:, k * 128:(k + 1) * 128], identity=ident)
                        return ins
                    S_.op("pe", tr, r=[b_hbf[ts], b_const], w=[b_pT])
                    S_.op("act", lambda e, tt=tt: e.activation(out=hT[:, :, tt * 128:(tt + 1) * 128], in_=pT, func=AF.Copy),
                          r=[b_pT], w=[b_hT])

            def gateup_group(g, hooks=None):
                for j in range(NJ):
                    if hooks and j in hooks:
                        hooks[j]()
                    pb = j % 4
                    gb_ = j % 2

                    def mm(e, j=j, pb=pb):
                        for k in range(8):
                            e.matmul(pgu[pb][:, 0:GT], lhsT=WGU[:, k, j * 128:(j + 1) * 128], rhs=hT[:, k, :],
                                     start=(k == 0), stop=(k == 7))
                        for k in range(8):
                            ins = e.matmul(pgu[pb][:, GT:2 * GT], lhsT=WGU[:, k, DFF + j * 128:DFF + (j + 1) * 128],
                                           rhs=hT[:, k, :], start=(k == 0), stop=(k == 7))
                        return ins
                    S_.op("pe", mm, r=b_wgu_l + [b_hT], w=[b_pgu[pb]])
                    S_.op("act", lambda e, pb=pb, gb_=gb_: e.activation(out=gact[gb_], in_=pgu[pb][:, 0:GT], func=AF.Silu),
                          r=[b_pgu[pb]], w=[b_gact[gb_]])
                    S_.op("dve", lambda e, j=j, pb=pb, gb_=gb_: e.tensor_tensor(out=actT[:, j, :], in0=pgu[pb][:, GT:2 * GT],
                                                                                in1=gact[gb_], op=ALU.mult),
                          r=[b_pgu[pb], b_gact[gb_]], w=[b_actT])

            def down_group(g):
                for tt in range(TPG):
                    t = g * TPG + tt
                    xs = (g % 3) * TPG + tt
                    for hf in range(2):
                        pi = (2 * t + hf) % 3

                        def mm(e, tt=tt, hf=hf, pi=pi):
                            for j in range(NJ):
                                ins = e.matmul(pd[pi], lhsT=actT[:, j, tt * 128:(tt + 1) * 128],
                                               rhs=WD[:, j, hf * 512:(hf + 1) * 512], start=(j == 0), stop=(j == NJ - 1))
                            return ins
                        S_.op("pe", mm, r=[b_actT] + b_wd_l, w=[b_pd[pi]])
                        gi = (2 * t + hf) % 2
                        S_.op("dve", lambda e, hf=hf, pi=pi, gi=gi: e.tensor_tensor(
                            out=gtmp[gi], in0=pd[pi], in1=Gt[:, hf * 512:(hf + 1) * 512], op=ALU.mult),
                            r=[b_pd[pi], b_G], w=[b_gtmp[gi]])
                        S_.op("dve", lambda e, xs=xs, hf=hf, gi=gi: e.tensor_tensor(
                            out=xb[xs][:, hf * 512:(hf + 1) * 512], in0=xb[xs][:, hf * 512:(hf + 1) * 512], in1=gtmp[gi],
                            op=ALU.add), r=[b_gtmp[gi]], w=[b_x[xs]])
                    if not final:
                        S_.dma("sp", lambda e, t=t, xs=xs: e.dma_start(out=x1scr[t * 128:(t + 1) * 128, :], in_=xb[xs]),
                               b_x[xs], r=[b_x[xs]])
                if final:
                    for tt in range(TPG):
                        xs = (g % 3) * TPG + tt
                        S_.op("act", lambda e, xs=xs, tt=tt: e.activation(out=junk, in_=xb[xs], func=AF.Square, scale=1.0 / 32.0,
                                                                          accum_out=stat[:, 32 + tt:33 + tt]),
                              r=[b_x[xs]], w=[b_junk, b_fst])

            def final_rstd(g):
                rstd_ops(stat[:, 32:32 + TPG], stat[:, 36:36 + TPG], [b_fst], [b_fst], stat[:, 34:34 + TPG])

            def final_apply(g):
                for tt in range(TPG):
                    t = g * TPG + tt
                    xs = (g % 3) * TPG + tt
                    S_.op("act", lambda e, xs=xs, tt=tt: e.activation(out=xb[xs], in_=xb[xs], func=AF.Copy,
                                                                      scale=stat[:, 36 + tt:37 + tt]),
                          r=[b_fst], w=[b_x[xs]])
                    S_.op("pool", lambda e, xs=xs: e.tensor_tensor(out=xb[xs], in0=xb[xs], in1=Gf, op=ALU.mult),
                          r=[b_Gf], w=[b_x[xs]])
                    S_.dma("sp", lambda e, t=t, xs=xs: e.dma_start(out=y_d[t * 128:(t + 1) * 128, :], in_=xb[xs]),
                           b_x[xs], r=[b_x[xs]])

            x_load(0)
            if NG > 1:
                x_load(1)
            norm_load_sq(0)
            norm_rstd(0)
            for tt in range(TPG):
                norm_apply(0, tt)
            transpose_group(0)
            for g in range(NG):
                hooks = {}
                if g + 1 < NG:
                    hooks[1] = (lambda g=g: norm_load_sq(g + 1))
                    for tt in range(TPG):
                        hooks[9 + 4 * tt] = (lambda g=g, tt=tt: norm_apply(g + 1, tt))

                def sqrt_hook(g=g):
                    if g + 1 < NG:
                        norm_rstd(g + 1)
                    if final and g >= 1:
                        final_rstd(g - 1)
                hooks[6] = sqrt_hook
                if final and g >= 1:
                    hooks[7] = (lambda g=g: final_apply(g - 1))
                gateup_group(g, hooks)
                if g + 1 < NG:
                    transpose_group(g + 1)
                if g + 2 < NG:
                    x_load(g + 2)
                down_group(g)
            if final:
                final_rstd(NG - 1)
                final_apply(NG - 1)
            S_.barrier()

        ffn_phase("f1", w1gu_d, w1d_d, 1, 0, 2, x_d, False, pending=(w1steps, w1bufs))

        A = Arena(PERS)
        WGU_BYTES = 8 * 2 * DFF * 2
        qnT = V(A.take(2 * S * 2), BF16, [128, 2, S])
        kvnT = V(A.take(S * 2), BF16, [128, S])
        KTm = [V(A.take(S * 2), BF16, [128, S]) for _ in range(2)]
        HOLE0 = A.o
        HOLE_END = PERS + WGU_BYTES
        if HOLE_END < HOLE0:
            HOLE_END = HOLE0
        AH = Arena(HOLE0)
        A.o = HOLE_END
        QTg = V(A.take(4 * S * 2), BF16, [128, 4, S])
        KTg = V(A.take(4 * S * 2), BF16, [128, 4, S])
        VAg = V(A.take(NT * 2 * 66 * 2), BF16, [128, NT, 2, 66])
        RES_END = A.o

        def hole_take(arena, nbytes):
            if AHH[0].o + (nbytes + 63) // 64 * 64 <= HOLE_END:
                return AHH[0].take(nbytes)
            return arena.take(nbytes)
        AHH = [AH]
        b_res = S_.buf("res")
        Win = V(hole_take(A, 8 * 1184 * 2), BF16, [128, 8, 1184])
        ABcol = V(A.take(64), F32, [128, 16])
        biash = V(hole_take(A, 1184 * 2), BF16, [128, 1184])
        biasl = V(A.take(1184 * 2), BF16, [128, 1184])
        gqk = V(hole_take(A, 640 * 4), F32, [128, 640])
        cosm = V(hole_take(A, NT * 32 * 4), F32, [128, NT, 32])
        sinm = V(hole_take(A, NT * 32 * 4), F32, [128, NT, 32])
        cosg = V(hole_take(A, NT * 64 * 4), F32, [128, NT, 64])
        sing = V(hole_take(A, NT * 64 * 4), F32, [128, NT, 64])
        wstg_off = A.o
        x2b = [V(A.take(4096), F32, [128, D]) for _ in range(2)]
        hb2 = [V(A.take(2048), BF16, [128, D]) for _ in range(2)]
        hT2 = [V(A.take(8 * 128 * 2), BF16, [128, 8, 128])]
        junk2 = V(A.take(2048), BF16, [128, D])
        assert A.o - wstg_off >= 16384
        hT2.append(V(A.take(8 * 128 * 2), BF16, [128, 8, 128]))
        biasf = V(A.o, F32, [128, 1184])
        sq2 = [V(A.take(640 * 4), F32, [128, 640]) for _ in range(2)]
        Brep = V(A.o, BF16, [128, 8, 128])
        qk2 = [V(A.take(640 * 4), F32, [128, 640]) for _ in range(2)]
        t12 = [V(A.take(640 * 4), F32, [128, 640]) for _ in range(2)]
        qkb = [V(A.take(640 * 2), BF16, [128, 640]) for _ in range(2)]
        kz = [V(A.take(4 * 128 * 2), BF16, [128, 4, 128]) for _ in range(2)]
        lat = [V(A.take(384 * 2), BF16, [128, 384]) for _ in range(2)]
        kpad = [V(A.take(128 * 2), BF16, [128, 128]) for _ in range(2)]
        kr = [V(A.take(32 * 4), F32, [128, 32]) for _ in range(2)]
        kt1 = [V(A.take(32 * 4), F32, [128, 32]) for _ in range(2)]
        kt2 = [V(A.take(32 * 4), F32, [128, 32]) for _ in range(2)]
        st2 = [V(A.take(32 * 4), F32, [128, 32]) for _ in range(2)]
        wstg = V(wstg_off, F32, [128, 8, 512])

        b_win, b_AB, b_bias, b_gqk, b_tab = (S_.buf("win"), S_.buf("AB"), S_.buf("bias"), S_.buf("gqk"), S_.buf("tab"))
        b_x2 = [S_.buf("x2_%d" % i) for i in range(2)]
        b_hT2, b_junk2 = [S_.buf("hT2_0"), S_.buf("hT2_1")], S_.buf("junk2")
        b_hb2 = [S_.buf("hb2_%d" % i) for i in range(2)]
        pb2 = lambda n: [S_.buf("%s_%d" % (n, i)) for i in range(2)]
        b_sq2, b_qk2, b_t12, b_qkb, b_kz = pb2("sq2"), pb2("qk2"), pb2("t12"), pb2("qkb"), pb2("kz")
        b_lat, b_kpad, b_kr, b_kt1, b_kt2, b_st2 = pb2("lat"), pb2("kpad"), pb2("kr"), pb2("kt1"), pb2("kt2"), pb2("st2")
        b_wstg = S_.buf("wstg")
        b_pz = [[S_.buf("pz%d_%d" % (i, j), j == 2) for j in range(3)] for i in range(2)]
        b_pT1, b_pT2 = S_.buf("pT1"), S_.buf("pT2")
        pz = [[PB(3 * i + j) for j in range(3)] for i in range(2)]
        pT1 = PB(6, BF16).rearrange("p (a b) -> p a b", a=8, b=128)
        pT2 = PB(7, BF16).rearrange("p (a b) -> p a b", a=8, b=128)

        S_.dma("sp", lambda e: e.dma_start(out=gqk, in_=gqk_d[0:1, :].partition_broadcast(128)), b_gqk, w=[b_gqk])
        S_.dma("sp", lambda e: e.dma_start(out=ABcol[:, 0:8], in_=modscr[4][0:1, :].rearrange("o (k p) -> p (o k)", p=128),
                                           allow_slow_non_contiguous=True), b_AB, w=[b_AB])
        S_.dma("sp", lambda e: e.dma_start(out=ABcol[:, 8:16], in_=modscr[3][0:1, :].rearrange("o (k p) -> p (o k)", p=128),
                                           allow_slow_non_contiguous=True), b_AB, w=[b_AB])
        for k in range(8):
            S_.op("dve", lambda e, k=k: e.tensor_copy(out=Brep[:, k, :], in_=ABcol[:, 8 + k:9 + k].to_broadcast([128, 128])),
                  r=[b_AB], w=[b_AB])
        S_.dma("sp", lambda e: e.dma_start(out=cosm, in_=cosm_d.rearrange("(t p) d -> p t d", p=128)), b_tab, w=[b_tab])
        S_.dma("sp", lambda e: e.dma_start(out=sinm, in_=sinm_d.rearrange("(t p) d -> p t d", p=128)), b_tab, w=[b_tab])
        S_.dma("sp", lambda e: e.dma_start(out=cosg, in_=cosg_d.rearrange("(t p) d -> p t d", p=128)), b_tab, w=[b_tab])
        S_.dma("sp", lambda e: e.dma_start(out=sing, in_=sing_d.rearrange("(t p) d -> p t d", p=128)), b_tab, w=[b_tab])
        win_v = win_d.rearrange("(k p) n -> p k n", p=128)
        pbias = PB(0)
        b_pbias = S_.buf("pbias")
        for (c0, c1) in ((0, 512), (512, 1024), (1024, 1184)):
            S_.dma("sp", lambda e, c0=c0, c1=c1: e.dma_start(out=wstg[:, :, 0:c1 - c0], in_=win_v[:, :, c0:c1]),
                   b_wstg, w=[b_wstg])
            S_.op("act", lambda e, c0=c0, c1=c1: e.activation(out=Win[:, :, c0:c1], in_=wstg[:, :, 0:c1 - c0], func=AF.Copy),
                  r=[b_wstg], w=[b_win])

            def bmm(e, c0=c0, c1=c1):
                for k in range(8):
                    ins = e.matmul(pbias[:, 0:c1 - c0], lhsT=Brep[:, k, :], rhs=Win[:, k, c0:c1], start=(k == 0), stop=(k == 7))
                return ins
            S_.op("pe", bmm, r=[b_AB, b_win], w=[b_pbias])
            S_.op("dve", lambda e, c0=c0, c1=c1: e.tensor_copy(out=biasf[:, c0:c1], in_=pbias[:, 0:c1 - c0]),
                  r=[b_pbias], w=[b_bias])
            for k in range(8):
                S_.op("dve", lambda e, c0=c0, c1=c1, k=k: e.tensor_scalar(out=Win[:, k, c0:c1], in0=wstg[:, k, 0:c1 - c0],
                                                                          scalar1=ABcol[:, k:k + 1], scalar2=None, op0=ALU.mult),
                      r=[b_wstg, b_AB], w=[b_win])
        S_.op("dve", lambda e: e.tensor_copy(out=biash, in_=biasf), r=[b_bias], w=[b_bias])
        S_.op("dve", lambda e: e.tensor_tensor(out=biasl, in0=biasf, in1=biash, op=ALU.subtract), r=[b_bias], w=[b_bias])
        S_.op("pool", lambda e: e.memset(VAg, 1.0), w=[b_res])
        for i in range(2):
            S_.op("pool", lambda e, i=i: e.memset(kpad[i], 0.0), w=[b_kpad[i]])
            S_.op("pool", lambda e, i=i: e.memset(kz[i], 0.0), w=[b_kz[i]])
            S_.op("pool", lambda e, i=i: e.memset(KTm[i], 0.0), w=[b_res])
        S_.barrier()

        def p2_front_a(t):
            xs = t % 2
            st = st2[xs]
            bs = b_st2[xs]
            S_.dma("sp", lambda e: e.dma_start(out=x2b[xs], in_=x1scr[t * 128:(t + 1) * 128, :]), b_x2[xs], w=[b_x2[xs]])
            S_.op("act", lambda e: e.activation(out=junk2, in_=x2b[xs], func=AF.Square, scale=1.0 / 32.0,
                                                accum_out=st[:, 0:1]), r=[b_x2[xs]], w=[b_junk2, bs])
            rstd_ops(st[:, 0:1], st[:, 2:3], [bs], [bs], st[:, 1:2])
            S_.op("dve", lambda e: e.tensor_scalar(out=hb2[xs], in0=x2b[xs], scalar1=st[:, 2:3], scalar2=None, op0=ALU.mult),
                  r=[b_x2[xs], bs], w=[b_hb2[xs]])

        def p2_front_b(t):
            xs = t % 2

            def tr(e):
                for k in range(8):
                    ins = e.transpose(out=pT1[:, k, :], in_=hb2[xs][:, k * 128:(k + 1) * 128], identity=ident)
                return ins
            S_.op("pe", tr, r=[b_hb2[xs], b_const], w=[b_pT1])
            S_.op("act", lambda e: e.activation(out=hT2[xs], in_=pT1, func=AF.Copy), r=[b_pT1], w=[b_hT2[xs]])

        def p2_front_b2(t):
            xs = t % 2

            def zmm(e):
                for (pi, c0, c1) in ((0, 0, 416), (1, 416, 928), (2, 928, 1184)):
                    for k in range(8):
                        e.matmul(pz[xs][pi][:, 0:c1 - c0], lhsT=hT2[xs][:, k, :], rhs=Win[:, k, c0:c1],
                                 start=(k == 0), stop=False)
                    e.matmul(pz[xs][pi][:, 0:c1 - c0], lhsT=ident, rhs=biash[:, c0:c1], start=False, stop=False)
                    ins = e.matmul(pz[xs][pi][:, 0:c1 - c0], lhsT=ident, rhs=biasl[:, c0:c1], start=False, stop=True)
                return ins
            S_.op("pe", zmm, r=[b_hT2[xs], b_win, b_bias, b_const], w=b_pz[xs])

        def p2_back_a(t):
            xs = t % 2
            st = st2[xs]
            bs = b_st2[xs]
            tsl = slice(t * 128, (t + 1) * 128)
            z0, z1, z2 = pz[xs]
            bz0, bz1, bz2 = b_pz[xs]
            S_.op("act", lambda e: e.activation(out=junk2[:, 0:256], in_=z0[:, 0:256], func=AF.Square,
                                                scale=1.0 / 16.0, accum_out=st[:, 4:5]), r=[bz0], w=[b_junk2, bs])
            S_.op("act", lambda e: e.activation(out=junk2[:, 256:384], in_=z0[:, 256:384], func=AF.Square,
                                                scale=1.0 / math.sqrt(128.0), accum_out=st[:, 5:6]), r=[bz0], w=[b_junk2, bs])
            S_.op("act", lambda e: e.activation(out=sq2[xs][:, 0:512], in_=z1, func=AF.Square, scale=1.0 / 8.0),
                  r=[bz1], w=[b_sq2[xs]])
            S_.op("act", lambda e: e.activation(out=sq2[xs][:, 512:640], in_=z2[:, 0:128], func=AF.Square, scale=1.0 / 8.0),
                  r=[bz2], w=[b_sq2[xs]])
            S_.op("dve", lambda e: e.tensor_reduce(out=st[:, 10:20], in_=sq2[xs].rearrange("p (h d) -> p h d", d=64),
                                                   axis=AX.X, op=ALU.add), r=[b_sq2[xs]], w=[bs])
            S_.op("dve", lambda e: e.tensor_copy(out=st[:, 8:10], in_=st[:, 4:6]), r=[bs], w=[bs])
            S_.op("act", lambda e: e.activation(out=st[:, 20:32], in_=st[:, 8:20], func=AF.Sqrt, bias=epsT[:, 0:1], scale=1.0),
                  r=[bs], w=[bs])
            S_.op("dve", lambda e: e.reciprocal(out=st[:, 8:20], in_=st[:, 20:32]), r=[bs], w=[bs])
            S_.op("dve", lambda e: e.tensor_scalar(out=lat[xs][:, 0:256], in0=z0[:, 0:256], scalar1=st[:, 8:9],
                                                   scalar2=None, op0=ALU.mult), r=[bz0, bs], w=[b_lat[xs]])
            S_.op("dve", lambda e: e.tensor_scalar(out=lat[xs][:, 256:384], in0=z0[:, 256:384], scalar1=st[:, 9:10],
                                                   scalar2=None, op0=ALU.mult), r=[bz0, bs], w=[b_lat[xs]])
            S_.op("dve", lambda e: e.tensor_copy(out=kr[xs], in_=z0[:, 384:416]), r=[bz0], w=[b_kr[xs]])
            S_.op("pool", lambda e: e.tensor_tensor(out=kt1[xs], in0=kr[xs], in1=cosm[:, t, :], op=ALU.mult),
                  r=[b_kr[xs], b_tab], w=[b_kt1[xs]])
            krv = kr[xs].rearrange("p (b h f) -> p b h f", b=2, h=2, f=8)
            kt2v = kt2[xs].rearrange("p (b h f) -> p b h f", b=2, h=2, f=8)
            for hh in range(2):
                S_.op("pool", lambda e, hh=hh: e.tensor_tensor(
                    out=kt2v[:, :, hh, :], in0=krv[:, :, 1 - hh, :],
                    in1=sinm[:, t, :].rearrange("p (b h f) -> p b h f", b=2, h=2, f=8)[:, :, hh, :], op=ALU.mult),
                    r=[b_kr[xs], b_tab], w=[b_kt2[xs]])
            S_.op("pool", lambda e: e.tensor_tensor(out=kpad[xs][:, 64:96], in0=kt1[xs], in1=kt2[xs], op=ALU.add),
                  r=[b_kt1[xs], b_kt2[xs]], w=[b_kpad[xs]])
            qk3 = qk2[xs].rearrange("p (h d) -> p h d", d=64)
            S_.op("dve", lambda e: e.tensor_tensor(out=qk3[:, 0:8, :], in0=z1.rearrange("p (h d) -> p h d", d=64),
                                                   in1=st[:, 10:18].unsqueeze(2).to_broadcast([128, 8, 64]), op=ALU.mult),
                  r=[bz1, bs], w=[b_qk2[xs]])
            S_.op("dve", lambda e: e.tensor_tensor(out=qk3[:, 8:10, :], in0=z2[:, 0:128].rearrange("p (h d) -> p h d", d=64),
                                                   in1=st[:, 18:20].unsqueeze(2).to_broadcast([128, 2, 64]), op=ALU.mult),
                  r=[bz2, bs], w=[b_qk2[xs]])
            S_.op("act", lambda e: e.activation(out=VAg[:, t, :, 0:64], in_=z2[:, 128:256].rearrange("p (h d) -> p h d", d=64),
                                                func=AF.Copy), r=[bz2], w=[b_res])

        def p2_back_b(t):
            xs = t % 2
            tsl = slice(t * 128, (t + 1) * 128)
            qk3 = qk2[xs].rearrange("p (h d) -> p h d", d=64)
            S_.op("dve", lambda e: e.tensor_tensor(out=qk2[xs], in0=qk2[xs], in1=gqk, op=ALU.mult), r=[b_gqk], w=[b_qk2[xs]])
            S_.op("pool", lambda e: e.tensor_tensor(out=t12[xs].rearrange("p (h d) -> p h d", d=64), in0=qk3,
                                                    in1=cosg[:, t, :].unsqueeze(1).to_broadcast([128, 10, 64]),
                                                    op=ALU.mult), r=[b_qk2[xs], b_tab], w=[b_t12[xs]])
            qk5 = qk2[xs].rearrange("p (h b t f) -> p h b t f", h=10, b=2, t=2, f=16)
            t25 = sq2[xs].rearrange("p (h b t f) -> p h b t f", h=10, b=2, t=2, f=16)
            for hh in range(2):
                S_.op("dve", lambda e, hh=hh: e.tensor_tensor(
                    out=t25[:, :, :, hh, :], in0=qk5[:, :, :, 1 - hh, :],
                    in1=sing[:, t, :].rearrange("p (b t f) -> p b t f", b=2, t=2, f=16)[:, :, hh, :]
                    .unsqueeze(1).to_broadcast([128, 10, 2, 16]), op=ALU.mult),
                    r=[b_qk2[xs], b_tab], w=[b_sq2[xs]])
            S_.op("pool", lambda e: e.tensor_tensor(out=qkb[xs], in0=t12[xs], in1=sq2[xs], op=ALU.add),
                  r=[b_t12[xs], b_sq2[xs]], w=[b_qkb[xs]])
            kz5 = kz[xs].rearrange("p (k v) d -> p k v d", k=2, v=2)
            kin = qkb[xs][:, 512:640].rearrange("p (k d) -> p k d", d=64)
            S_.op("pool", lambda e: e.tensor_copy(out=kz5[:, :, 0, 0:64], in_=kin), r=[b_qkb[xs]], w=[b_kz[xs]])
            S_.op("pool", lambda e: e.tensor_copy(out=kz5[:, :, 1, 64:128], in_=kin), r=[b_qkb[xs]], w=[b_kz[xs]])


        def p2_back_b2(t):
            xs = t % 2
            tsl = slice(t * 128, (t + 1) * 128)

            def tr2(e):
                e.transpose(out=pT2[:, 0, :], in_=lat[xs][:, 0:128], identity=ident)
                e.transpose(out=pT2[:, 1, :], in_=lat[xs][:, 128:256], identity=ident)
                e.transpose(out=pT2[:, 2, :], in_=lat[xs][:, 256:384], identity=ident)
                e.transpose(out=pT2[:, 3, :], in_=kpad[xs], identity=ident)
                for c in range(4):
                    ins = e.transpose(out=pT2[:, 4 + c, :], in_=qkb[xs][:, c * 128:(c + 1) * 128], identity=ident)
                return ins
            S_.op("pe", tr2, r=[b_lat[xs], b_kpad[xs], b_qkb[xs], b_const], w=[b_pT2])

            S_.op("act", lambda e: e.activation(out=qnT[:, :, tsl], in_=pT2[:, 0:2, :], func=AF.Copy), r=[b_pT2], w=[b_res])
            S_.op("dve", lambda e: e.tensor_copy(out=kvnT[:, tsl], in_=pT2[:, 2, :]), r=[b_pT2], w=[b_res])
            for i in range(2):
                S_.op("act", lambda e, i=i: e.activation(out=KTm[i][64:96, tsl], in_=pT2[64:96, 3, :], func=AF.Copy),
                      r=[b_pT2], w=[b_res])
            S_.op("act", lambda e: e.activation(out=QTg[:, :, tsl], in_=pT2[:, 4:8, :], func=AF.Copy), r=[b_pT2], w=[b_res])

            def tr3(e):
                for vv in range(4):
                    ins = e.transpose(out=pT2[:, vv, :], in_=kz[xs][:, vv, :], identity=ident)
                return ins
            S_.op("pe", tr3, r=[b_kz[xs], b_const], w=[b_pT2])
            S_.op("dve", lambda e: e.tensor_copy(out=KTg[:, :, tsl], in_=pT2[:, 0:4, :]), r=[b_pT2], w=[b_res])

        for t0 in range(min(2, NT)):
            p2_front_a(t0)
            p2_front_b(t0)
        for t0 in range(min(2, NT)):
            p2_front_b2(t0)
        if NT > 2:
            p2_front_a(2)
            p2_front_b(2)
        for t in range(NT):
            p2_back_a(t)
            if t + 2 < NT:
                p2_front_b2(t + 2)
            p2_back_b(t)
            if t + 3 < NT:
                p2_front_a(t + 3)
                p2_front_b(t + 3)
            p2_back_b2(t)
        if debug:
            dq = nc.dram_tensor("dbg_qnT", [128, 2, S], BF16, kind="ExternalOutput").ap()
            dk = nc.dram_tensor("dbg_kvnT", [128, S], BF16, kind="ExternalOutput").ap()
            dkt = nc.dram_tensor("dbg_KTm", [128, S], BF16, kind="ExternalOutput").ap()
            b_dbg = S_.buf("dbg")
            S_.dma("sp", lambda e: e.dma_start(out=dq, in_=qnT), b_dbg, r=[b_res])
            S_.dma("sp", lambda e: e.dma_start(out=dk, in_=kvnT), b_dbg, r=[b_res])
            S_.dma("sp", lambda e: e.dma_start(out=dkt, in_=KTm[0]), b_dbg, r=[b_res])
        S_.barrier()

        A = Arena(RES_END)
        AHH[0] = Arena(HOLE0)
        b_resm = S_.buf("resm")
        QTm = [V(A.take(QG * 2), BF16, [128, QG]) for _ in range(2)]
        PT = [V(A.take(1024 * 2), BF16, [128, 1024]) for _ in range(3)]
        rdn = V(A.take(QG * 4), F32, [128, QG])
        bcs = V(A.take(QG * 4), F32, [128, QG])
        ohT = [V(A.take(QG * 2), BF16, [128, QG]) for _ in range(2)]
        qt1 = V(A.take(QG * 4), F32, [128, QG])
        qt2 = V(A.take(QG * 4), F32, [128, QG])
        VAm = [V(hole_take(A, NT * 66 * 2), BF16, [128, NT, 66]) for _ in range(2)]
        rh = V(A.take(QG * 2), BF16, [128, QG])
        rl = V(A.take(QG * 2), BF16, [128, QG])
        b_rh = S_.buf("rh")
        cosf = V(hole_take(A, S * 4), F32, [128, S])
        sinf = V(hole_take(A, S * 4), F32, [128, S])
        b_qt1, b_qt2, b_tabf = S_.buf("qt1"), S_.buf("qt2"), S_.buf("tabf")
        WqA = V(hole_take(A, 2 * 768 * 2), BF16, [128, 2, 768])
        WqB = V(hole_take(A, 2 * 768 * 2), BF16, [128, 2, 768])
        Wk = V(hole_take(A, 512 * 2), BF16, [128, 512])
        Wv = V(hole_take(A, 512 * 2), BF16, [128, 512])
        gcol = V(A.take(64), F32, [128, 4])
        wq_st = V(A.take(2 * 768 * 4), F32, [128, 2, 768])
        b_wq, b_gcol, b_wstg3 = S_.buf("wq"), S_.buf("gcol"), S_.buf("wstg3")
        S_.dma("sp", lambda e: e.dma_start(out=gcol[:, 0:2], in_=gql_d), b_gcol, w=[b_gcol])
        S_.dma("sp", lambda e: e.dma_start(out=gcol[:, 2:3], in_=gkvl_d), b_gcol, w=[b_gcol])
        for (src, dst) in ((wuq_d, WqA), (wuqr_d, WqB)):
            S_.dma("sp", lambda e, src=src: e.dma_start(out=wq_st, in_=src.rearrange("(c p) n -> p c n", p=128)),
                   b_wstg3, w=[b_wstg3])
            for c in range(2):
                S_.op("dve", lambda e, c=c, dst=dst: e.tensor_scalar(out=dst[:, c, :], in0=wq_st[:, c, :],
                                                                     scalar1=gcol[:, c:c + 1], scalar2=None, op0=ALU.mult),
                      r=[b_wstg3, b_gcol], w=[b_wq])
        for (src, dst) in ((wuk_d, Wk), (wuv_d, Wv)):
            S_.dma("sp", lambda e, src=src: e.dma_start(out=wq_st[:, 0, 0:512], in_=src), b_wstg3, w=[b_wstg3])
            S_.op("dve", lambda e, dst=dst: e.tensor_scalar(out=dst, in0=wq_st[:, 0, 0:512], scalar1=gcol[:, 2:3],
                                                            scalar2=None, op0=ALU.mult), r=[b_wstg3, b_gcol], w=[b_wq])
        b_VAm = [S_.buf("VAm%d" % i) for i in range(2)]
        S_.dma("sp", lambda e: e.dma_start(out=cosf[64:96, :], in_=cosf_d), b_tabf, w=[b_tabf])
        S_.dma("sp", lambda e: e.dma_start(out=sinf[64:96, :], in_=sinf_d), b_tabf, w=[b_tabf])
        for i in range(2):
            S_.op("pool", lambda e, i=i: e.memset(VAm[i], 1.0), w=[b_VAm[i]])
        b_QTm = [S_.buf("QTm%d" % i) for i in range(2)]
        for i in range(2):
            S_.op("pool", lambda e, i=i: e.memset(QTm[i], 0.0), w=[b_QTm[i]])
        b_KTm = [S_.buf("KTm%d" % i) for i in range(2)]
        b_PT = [S_.buf("PT%d" % i) for i in range(3)]
        b_rdn, b_bcs = S_.buf("rdn"), S_.buf("bcs")
        b_ohT = [S_.buf("ohT%d" % i) for i in range(2)]
        b_pS = [S_.buf("pS%d" % i) for i in range(2)]
        b_pO = [S_.buf("pO%d" % i) for i in range(2)]
        b_pM = [S_.buf("pM%d" % i) for i in range(2)]
        pS = [pall[:, i * 1024:(i + 1) * 1024] for i in range(2)]
        pO = [PB(4 + i) for i in range(2)]
        pM = [PB(6 + i) for i in range(2)]
        items = [(h, qg) for h in range(HM + HG) for qg in range(NQG)]
        sc_m = 96.0 ** -0.5
        sc_g = 64.0 ** -0.5

        def prep_k_steps(h):
            kb = h % 2
            subs = []
            for qg in range(NQG):
                def sub(qg=qg):
                    m = qg % 2
                    cs = slice(qg * QG, (qg + 1) * QG)
                    S_.op("pe", lambda e: e.matmul(pM[m][0:64, :], lhsT=Wk[:, h * 64:(h + 1) * 64], rhs=kvnT[:, cs],
                                                   start=True, stop=True), r=[b_resm, b_wq], w=[b_pM[m]])
                    S_.op("dve", lambda e: e.tensor_copy(out=KTm[kb][0:64, cs], in_=pM[m][0:64, :]),
                          r=[b_pM[m]], w=[b_KTm[kb]])
                subs.append(sub)
            nb8 = min(8, NT)
            for g8 in range(NT // nb8):
                def sub(g8=g8):
                    m = g8 % 2

                    def vmm(e):
                        for u in range(nb8):
                            blk = g8 * nb8 + u
                            ins = e.matmul(pM[m][:, u * 64:(u + 1) * 64], lhsT=kvnT[:, blk * 128:(blk + 1) * 128],
                                           rhs=Wv[:, h * 64:(h + 1) * 64], start=True, stop=True)
                        return ins
                    S_.op("pe", vmm, r=[b_resm, b_wq], w=[b_pM[m]])
                    S_.op("dve", lambda e: e.tensor_copy(out=VAm[kb][:, g8 * nb8:(g8 + 1) * nb8, 0:64],
                                                         in_=pM[m][:, 0:nb8 * 64].rearrange("p (u d) -> p u d", d=64)),
                          r=[b_pM[m]], w=[b_VAm[kb]])
                subs.append(sub)
            return subs

        def prep_k(h):
            for sub in prep_k_steps(h):
                sub()

        def prep_q(idx):
            h, qg = items[idx]
            if h >= HM:
                return
            qb = idx % 2
            cs = slice(qg * QG, (qg + 1) * QG)

            def mm(e, h=h, cs=cs):
                for (m, W) in ((0, WqA), (1, WqB)):
                    for c in range(2):
                        ins = e.matmul(pM[m][0:96, :], lhsT=W[:, c, h * 96:(h + 1) * 96], rhs=qnT[:, c, cs],
                                       start=(c == 0), stop=(c == 1))
                return ins
            S_.op("pe", mm, r=[b_resm, b_wq], w=[b_pM[0], b_pM[1]])
            S_.op("dve", lambda e, qb=qb: e.tensor_copy(out=QTm[qb][0:64, :], in_=pM[0][0:64, :]),
                  r=[b_pM[0]], w=[b_QTm[qb]])
            S_.op("dve", lambda e, cs=cs: e.tensor_tensor(out=qt1[64:96, :], in0=pM[0][64:96, :], in1=cosf[64:96, cs], op=ALU.mult),
                  r=[b_pM[0], b_tabf], w=[b_qt1])
            S_.op("dve", lambda e, cs=cs: e.tensor_tensor(out=qt2[64:96, :], in0=pM[1][64:96, :], in1=sinf[64:96, cs], op=ALU.mult),
                  r=[b_pM[1], b_tabf], w=[b_qt2])
            S_.op("dve", lambda e, qb=qb: e.tensor_tensor(out=QTm[qb][64:96, :], in0=qt1[64:96, :], in1=qt2[64:96, :], op=ALU.add),
                  r=[b_qt1, b_qt2], w=[b_QTm[qb]])

        prep_k(0)
        prep_q(0)
        if debug:
            d1 = nc.dram_tensor("dbg_KTm0", [128, S], BF16, kind="ExternalOutput").ap()
            d2 = nc.dram_tensor("dbg_QTm0", [128, QG], BF16, kind="ExternalOutput").ap()
            d3 = nc.dram_tensor("dbg_VAm0", [128, NT, 66], BF16, kind="ExternalOutput").ap()
            S_.dma("sp", lambda e: e.dma_start(out=d1, in_=KTm[0]), b_dbg, r=[b_KTm[0], b_res])
            S_.dma("sp", lambda e: e.dma_start(out=d2, in_=QTm[0]), b_dbg, r=[b_QTm[0]])
            S_.dma("sp", lambda e: e.dma_start(out=d3, in_=VAm[0]), b_dbg, r=[b_VAm[0]])
            S_.barrier()
        gctr = 0
        pend_fin = None

        def fin_b(pf):
            fidx, fh, fcs, fpo = pf
            ob = fidx % 2

            def bmm(e):
                e.matmul(pM[0][0:64, :], lhsT=onesb[64:65, 0:64], rhs=rh[64:65, :], start=True, stop=False)
                return e.matmul(pM[0][0:64, :], lhsT=onesb[64:65, 0:64], rhs=rl[64:65, :], start=False, stop=True)
            S_.op("pe", bmm, r=[b_rh, b_const], w=[b_pM[0]])
            S_.op("dve", lambda e: e.tensor_copy(out=bcs[0:64, :], in_=pM[0][0:64, :]), r=[b_pM[0]], w=[b_bcs])
            S_.op("dve", lambda e, fpo=fpo, ob=ob: e.tensor_tensor(out=ohT[ob][0:64, :], in0=pO[fpo][0:64, :], in1=bcs[0:64, :],
                                                                   op=ALU.mult), r=[b_pO[fpo], b_bcs], w=[b_ohT[ob]])
            S_.dma("sp", lambda e, fh=fh, fcs=fcs, ob=ob: e.dma_start(out=otscr[fh * 64:(fh + 1) * 64, fcs], in_=ohT[ob][0:64, :]),
                   b_ohT[ob], r=[b_ohT[ob]])

        npair = NT // 2

        def item_cfg(idx):
            h, qg = items[idx]
            cs = slice(qg * QG, (qg + 1) * QG)
            c = dict(h=h, cs=cs, po=idx % 2, idx=idx)
            if h < HM:
                kb, qb = h % 2, idx % 2
                c.update(kt_ap=lambda blk: KTm[kb][:, blk * 128:(blk + 1) * 128], q_ap=QTm[qb][:, :],
                         v_ap=lambda blk: VAm[kb][:, blk, 0:65], rbufs=[b_KTm[kb], b_QTm[qb], b_resm],
                         vbuf=b_VAm[kb], scale=sc_m)
            else:
                hg = h - HM
                var = (hg // 4) * 2 + hg % 2
                kv = hg // 4
                c.update(kt_ap=lambda blk: KTg[:, var, blk * 128:(blk + 1) * 128], q_ap=QTg[:, hg // 2, cs],
                         v_ap=lambda blk: VAg[:, blk, kv, 0:65], rbufs=[b_res], vbuf=b_res, scale=sc_g)
            return c

        def emit_pv(c, pi, ppt):
            def pvmm(e):
                for u in range(2):
                    blk = 2 * pi + u
                    ins = e.matmul(pO[c["po"]][0:65, :], lhsT=c["v_ap"](blk), rhs=PT[ppt][:, u * 512:(u + 1) * 512],
                                   start=(blk == 0), stop=(blk == NT - 1))
                return ins
            S_.op("pe", pvmm, r=[b_PT[ppt], c["vbuf"]], w=[b_pO[c["po"]]])

        def fin_a(c):
            po = c["po"]
            S_.op("dve", lambda e: e.reciprocal(out=rdn[64:65, :], in_=pO[po][64:65, :]), r=[b_pO[po]], w=[b_rdn])
            S_.op("dve", lambda e: e.tensor_copy(out=rh[64:65, :], in_=rdn[64:65, :]), r=[b_rdn], w=[b_rh])
            S_.op("dve", lambda e: e.tensor_tensor(out=rl[64:65, :], in0=rdn[64:65, :], in1=rh[64:65, :], op=ALU.subtract),
                  r=[b_rdn, b_rh], w=[b_rh])
            return (c["idx"], c["h"], c["cs"], po)

        pending_prep = []
        cfgs = {}
        W2GU, W2D = ffn_weight_views(PERS)
        w2steps, w2bufs = ffn_weight_loads("f2", w2gu_d, w2d_d, W2GU, W2D)
        n_gu = 2 * DFF // 512
        mla_dead = [b_resm, b_KTm[0], b_KTm[1], b_VAm[0], b_VAm[1], b_tabf, b_wq]
        w2_gu_steps = []
        if HOLE_END == PERS + WGU_BYTES and HOLE0 <= HOLE_END:
            wgu_v2 = w2gu_d.rearrange("(k p) n -> p k n", p=128)
            for cg in range(n_gu):
                w2steps.pop(0)
                def st(cg=cg, b=w2bufs[cg % 4]):
                    S_.dma("pool", lambda e: e.dma_start(out=W2GU[:, :, cg * 512:(cg + 1) * 512],
                                                         in_=wgu_v2[:, :, cg * 512:(cg + 1) * 512]), b, w=[b] + mla_dead)
                w2_gu_steps.append(st)

        def cfg_of(idx):
            if idx not in cfgs:
                cfgs[idx] = item_cfg(idx)
            return cfgs[idx]

        nsteps = len(items) * npair

        def emit_S(k):
            idx, i = k // npair, k % npair
            c = cfg_of(idx)
            sb = k % 2

            def smm(e):
                for u in range(2):
                    ins = e.matmul(pS[sb][:, u * 512:(u + 1) * 512], lhsT=c["kt_ap"](2 * i + u), rhs=c["q_ap"],
                                   start=True, stop=True)
                return ins
            S_.op("pe", smm, r=c["rbufs"], w=[b_pS[sb]])

        def emit_exp(k):
            c = cfg_of(k // npair)
            sb, pt = k % 2, k % 3
            S_.op("act", lambda e: e.activation(out=PT[pt], in_=pS[sb], func=AF.Exp, scale=c["scale"]),
                  r=[b_pS[sb]], w=[b_PT[pt]])

        emit_S(0)
        for k in range(nsteps):
            idx, i = k // npair, k % npair
            h, qg = items[idx]
            if i == 0:
                if idx + 1 < len(items):
                    prep_q(idx + 1)
                if qg == 0 and h + 1 < HM:
                    pending_prep = prep_k_steps(h + 1)
            emit_exp(k)
            if k + 1 < nsteps:
                emit_S(k + 1)
            if k >= 1:
                pidx, pi = (k - 1) // npair, (k - 1) % npair
                emit_pv(cfg_of(pidx), pi, (k - 1) % 3)
                if pi == npair - 1:
                    pend_fin = fin_a(cfg_of(pidx))
            if i == min(8, npair - 1) and pend_fin is not None:
                fin_b(pend_fin)
                pend_fin = None
            if w2_gu_steps and h >= HM and (idx - HM * NQG) >= 1:
                w2_gu_steps.pop(0)()
            if pending_prep and (i >= 4 or i == npair - 1):
                n_emit = 1 if i < npair - 1 else (len(pending_prep) if qg == NQG - 1 else 1)
                for _ in range(n_emit):
                    pending_prep.pop(0)()
        emit_pv(cfg_of((nsteps - 1) // npair), (nsteps - 1) % npair, (nsteps - 1) % 3)
        pend_fin = fin_a(cfg_of((nsteps - 1) // npair))
        fin_b(pend_fin)
        while w2_gu_steps:
            w2_gu_steps.pop(0)()
        S_.barrier()

        A = Arena(PERS + W_BYTES)
        Wo = V(A.take(8 * D * 2), BF16, [128, 8, D])
        Gm_ = V(A.take(4096), F32, [128, D])
        goc = V(A.take(64), F32, [128, 8])
        wos = V(A.take(16384), F32, [128, 4, D])
        x3b = [V(A.take(4096), F32, [128, D]) for _ in range(2)]
        oTg = [V(A.take(8 * QG * 2), BF16, [128, 8, QG]) for _ in range(2)]
        tq = V(A.take(512 * 4), F32, [128, 512])
        dg = V(A.take(256 * 4), F32, [128, 256])
        st3 = [V(A.take(32), F32, [128, 8]) for _ in range(2)]
        b_Wo, b_Gm, b_goc, b_wos = S_.buf("Wo"), S_.buf("Gm"), S_.buf("goc"), S_.buf("wos")
        b_x3 = [S_.buf("x3_%d" % i) for i in range(2)]
        b_oTg = [S_.buf("oTg%d" % i) for i in range(2)]
        b_tq, b_dg = S_.buf("tq"), S_.buf("dg")
        b_st3 = [S_.buf("st3_%d" % i) for i in range(2)]
        b_pa = [S_.buf("pa%d" % i) for i in range(2)]
        b_pb = [S_.buf("pb%d" % i) for i in range(2)]
        b_pg = S_.buf("pgram")
        pa = [PB(i) for i in range(2)]
        pbb = [PB(2 + i) for i in range(2)]
        pg = PB(4)
        S_.dma("sp", lambda e: e.dma_start(out=Gm_, in_=modscr[5]), b_Gm, w=[b_Gm])
        S_.dma("sp", lambda e: e.dma_start(out=goc, in_=gout_d), b_goc, w=[b_goc])
        wout_v = wout_d.rearrange("(c p) n -> p c n", p=128)
        for c0 in (0, 4):
            S_.dma("sp", lambda e, c0=c0: e.dma_start(out=wos, in_=wout_v[:, c0:c0 + 4, :]), b_wos, w=[b_wos])
            for c in range(c0, c0 + 4):
                S_.op("dve",
                      lambda e, c=c, c0=c0: e.scalar_tensor_tensor(out=Wo[:, c, :], in0=wos[:, c - c0, :], scalar=goc[:, c:c + 1],
                                                                   in1=Gm_, op0=ALU.mult, op1=ALU.mult),
                      r=[b_wos, b_goc, b_Gm], w=[b_Wo])
        S_.barrier()
        x3b = x3b + [wos[:, 0, :], wos[:, 1, :]]
        b_x3 = b_x3 + [S_.buf("x3_2"), S_.buf("x3_3")]
        NX3 = len(x3b)
        otv = otscr.rearrange("(c p) n -> p c n", p=128)
        tq2 = [tq, V(A.take(512 * 4), F32, [128, 512])]
        tq4 = [[tq2[0], V(A.take(512 * 4), F32, [128, 512])], [tq2[1], V(A.take(512 * 4), F32, [128, 512])]]
        b_tq4 = [[S_.buf("tq4_%d%d" % (i, j)) for j in range(2)] for i in range(2)]
        dg2 = [dg, V(A.take(256 * 4), F32, [128, 256])]
        b_tq2 = [b_tq, S_.buf("tq1")]
        b_dg2 = [b_dg, S_.buf("dg1")]
        pg2 = [PB(4), PB(5)]
        b_pg2 = [b_pg, S_.buf("pgram1")]

        def p35_load(t):
            qg, tt = t // 4, t % 4
            og = qg % 2
            xq = t % NX3
            S_.dma("sp", lambda e: e.dma_start(out=x3b[xq], in_=x1scr[t * 128:(t + 1) * 128, :]), b_x3[xq], w=[b_x3[xq]])

        def p35_compute(t):
            qg, tt = t // 4, t % 4
            og = qg % 2
            xs = t % 2
            st = st3[xs]
            bs = b_st3[xs]
            tcs = slice(tt * 128, (tt + 1) * 128)
            pgx, tqx, dgx = pg2[xs], tq2[xs], dg2[xs]

            def gram(e):
                for grp in range(2):
                    for c in range(4):
                        cc = grp * 4 + c
                        ins = e.matmul(pgx[:, grp * 128:(grp + 1) * 128], lhsT=oTg[og][:, cc, tcs], rhs=oTg[og][:, cc, tcs],
                                       start=(c == 0), stop=(c == 3))
                return ins
            S_.op("pe", gram, r=[b_oTg[og]], w=[b_pg2[xs]])
            for grp in range(2):
                S_.op("dve", lambda e, grp=grp: e.tensor_tensor(out=dgx[:, grp * 128:(grp + 1) * 128],
                                                                in0=pgx[:, grp * 128:(grp + 1) * 128], in1=identf, op=ALU.mult),
                      r=[b_pg2[xs], b_const], w=[b_dg2[xs]])
            S_.op("dve", lambda e: e.tensor_reduce(out=st[:, 0:2], in_=dgx.rearrange("p (g d) -> p g d", d=128),
                                                   axis=AX.X, op=ALU.add), r=[b_dg2[xs]], w=[bs])
            S_.op("act", lambda e: e.activation(out=st[:, 2:4], in_=st[:, 0:2], func=AF.Sqrt, bias=epsT[:, 0:1],
                                                scale=1.0 / 512.0), r=[bs], w=[bs])
            S_.op("dve", lambda e: e.reciprocal(out=st[:, 4:6], in_=st[:, 2:4]), r=[bs], w=[bs])

        def p35_out(t):
            qg, tt = t // 4, t % 4
            og = qg % 2
            xs = t % 2
            st = st3[xs]
            bs = b_st3[xs]
            tcs = slice(tt * 128, (tt + 1) * 128)
            xq = t % NX3
            for hf in range(2):
                hs = slice(hf * 512, (hf + 1) * 512)

                def omm(e, hs=hs, hf=hf):
                    for c in range(4):
                        e.matmul(pa[hf], lhsT=oTg[og][:, c, tcs], rhs=Wo[:, c, hs], start=(c == 0), stop=(c == 3))
                    for c in range(4):
                        ins = e.matmul(pbb[hf], lhsT=oTg[og][:, 4 + c, tcs], rhs=Wo[:, 4 + c, hs], start=(c == 0), stop=(c == 3))
                    return ins
                S_.op("pe", omm, r=[b_oTg[og], b_Wo], w=[b_pa[hf], b_pb[hf]])
                tqh = tq4[xs][hf]
                btq = b_tq4[xs][hf]
                S_.op("act", lambda e, hf=hf, tqh=tqh: e.activation(out=tqh, in_=pa[hf], func=AF.Copy, scale=st[:, 4:5]),
                      r=[b_pa[hf], bs], w=[btq])
                S_.op("dve", lambda e, hf=hf, tqh=tqh: e.scalar_tensor_tensor(out=tqh, in0=pbb[hf], scalar=st[:, 5:6], in1=tqh,
                                                                              op0=ALU.mult, op1=ALU.add),
                      r=[b_pb[hf], bs], w=[btq])
                S_.op("pool", lambda e, hs=hs, tqh=tqh: e.tensor_tensor(out=x3b[xq][:, hs], in0=x3b[xq][:, hs], in1=tqh, op=ALU.add),
                      r=[btq], w=[b_x3[xq]])
            S_.dma("sp", lambda e: e.dma_start(out=x1scr[t * 128:(t + 1) * 128, :], in_=x3b[xq]), b_x3[xq], r=[b_x3[xq]])

        def otg_load(q2):
            if q2 < NQG:
                S_.dma("sp", lambda e: e.dma_start(out=oTg[q2 % 2], in_=otv[:, :, q2 * QG:(q2 + 1) * QG]),
                       b_oTg[q2 % 2], w=[b_oTg[q2 % 2]])

        otg_load(0)
        otg_load(1)
        p35_load(0)
        if NT > 1:
            p35_load(1)
        p35_compute(0)
        for t in range(NT):
            if t + 2 < NT:
                p35_load(t + 2)
            if t + 1 < NT:
                p35_compute(t + 1)
            p35_out(t)
            if t % 4 == 3 and t + 1 < NT:
                otg_load(t // 4 + 2)
            for _ in range(max(1, (len(w2steps) + NT - 1) // NT) if w2steps else 0):
                if w2steps:
                    w2steps.pop(0)()
        while w2steps:
            w2steps.pop(0)()
        S_.barrier()

        ffn_phase("f2", w2gu_d, w2d_d, 7, 6, 8, x1scr, True, preloaded=True)

        S_.emit(nc, stack)
    return nc


def _tables(S):
    tok = np.arange(S)
    row = (tok // GRID_W).astype(np.float64)
    col = (tok % GRID_W).astype(np.float64)

    def tab(dim):
        q = dim // 4
        axis_dim = dim // 2
        inv = THETA ** (-(np.arange(q, dtype=np.float64) * 2.0 / axis_dim))
        inv = inv.astype(np.float32).astype(np.float64)
        ar = (row[:, None].astype(np.float32) * inv[None, :].astype(np.float32)).astype(np.float64)
        ac = (col[:, None].astype(np.float32) * inv[None, :].astype(np.float32)).astype(np.float64)
        cos = np.concatenate([np.cos(ar), np.cos(ar), np.cos(ac), np.cos(ac)], axis=1)
        sin = np.concatenate([-np.sin(ar), np.sin(ar), -np.sin(ac), np.sin(ac)], axis=1)
        return cos.astype(np.float32), sin.astype(np.float32)

    cosm, sinm = tab(32)
    cosg, sing = tab(64)
    return cosm, sinm, cosg, sing


_ROT32 = np.concatenate([np.arange(8, 16), np.arange(0, 8), np.arange(24, 32), np.arange(16, 24)])


def make_in_maps(inp, S):
    B = inp["x"].shape[0]
    f = lambda a: np.ascontiguousarray(np.asarray(a, dtype=np.float32))
    cosm, sinm, cosg, sing = _tables(S)
    w_uq = f(inp["w_uq"][0])
    idx = np.arange(768).reshape(8, 96)
    idx_rot = idx.copy()
    idx_rot[:, 64:96] = idx[:, 64:96][:, _ROT32]
    w_uq_rot = np.ascontiguousarray(w_uq[:, idx_rot.reshape(-1)])
    w_ukv = f(inp["w_ukv"][0]).reshape(128, 8, 128)
    shared = {
        "w_ada": f(inp["w_ada"][0]), "b_ada": f(inp["b_ada"][0]).reshape(1, -1),
        "gvecs": f(np.stack([inp["g_ffn1"][0], inp["g_mix"][0], inp["g_ffn2"][0], inp["g_final"]])),
        "w1_gu": f(inp["w1_gu"][0]), "w1_down": f(inp["w1_down"][0]),
        "w2_gu": f(inp["w2_gu"][0]), "w2_down": f(inp["w2_down"][0]),
        "w_in": f(inp["w_in"][0]), "w_uq": w_uq, "w_uq_rot": w_uq_rot,
        "w_ukv_k": f(w_ukv[:, :, 0:64].reshape(128, 512)), "w_ukv_v": f(w_ukv[:, :, 64:128].reshape(128, 512)),
        "g_q_lat_col": f(np.asarray(inp["g_q_lat"][0]).reshape(2, 128).T),
        "g_kv_lat_col": f(np.asarray(inp["g_kv_lat"][0]).reshape(1, 128).T),
        "g_qk_row": f(np.concatenate([np.tile(np.asarray(inp["g_qhead"][0]), 8), np.tile(np.asarray(inp["g_khead"][0]), 2)]).reshape(1, 640)),
        "g_out_col": f(np.concatenate([np.asarray(inp["g_out_mla"][0]), np.asarray(inp["g_out_gqa"][0])]).reshape(8, 128).T),
        "w_out": f(inp["w_out"][0]),
        "cosm": cosm, "sinm": sinm, "cosg": cosg, "sing": sing,
        "cosf": f(cosm.T), "sinf": f(sinm.T),
    }
    maps = []
    x = np.asarray(inp["x"], dtype=np.float32)
    c = np.asarray(inp["c"], dtype=np.float32)
    for b in range(B):
        m = dict(shared)
        m["x"] = np.ascontiguousarray(x[b])
        m["c_col"] = np.ascontiguousarray(c[b].reshape(8, 128).T)
        maps.append(m)
    return maps


_CACHE = {}


def kernel(**inputs):
    x = np.asarray(inputs["x"])
    B, S, _ = x.shape
    if S not in _CACHE:
        _CACHE[S] = build_program(S)
    nc = _CACHE[S]
    in_maps = make_in_maps(inputs, S)
    res = run_bass_kernel_spmd(nc, in_maps, core_ids=list(range(B)))
    return np.stack([np.asarray(r["y"], dtype=np.float32) for r in res.results], axis=0)
```

```python
import math
from contextlib import ExitStack

import numpy as np
import concourse.bass as bass
import concourse.mybir as mybir
from concourse.bass_utils import run_bass_kernel_spmd

F32 = mybir.dt.float32
BF16 = mybir.dt.bfloat16
AF = mybir.ActivationFunctionType
ALU = mybir.AluOpType
AX = mybir.AxisListType

D = 1024
DFF = 2816
NJ = DFF // 128
NMOD = 9
GRID_W = 64
THETA = 10000.0
EPS = 1e-6
HM, HG = 8, 8
ARENA_BYTES = 212736


class Buf:
    __slots__ = ("name", "lw", "rd", "dsem", "dcnt", "dseen", "excl")

    def __init__(self, name):
        self.name = name
        self.excl = False
        self.lw = {}
        self.rd = {}
        self.dsem = None
        self.dcnt = 0
        self.dseen = 0


class Op:
    __slots__ = ("eng", "fn", "deps", "dwaits", "signal", "tick", "dbuf")

    def __init__(self, eng, fn):
        self.eng = eng
        self.fn = fn
        self.deps = []
        self.dwaits = []
        self.signal = False
        self.tick = 0
        self.dbuf = None


COMPUTE = ("act", "pool", "dve", "pe")
ENGS = ("sp", "act", "pool", "dve", "pe")


class Sched:
    def __init__(self):
        self.ops = {e: [] for e in ENGS}
        self.bufs = []

    def buf(self, name, excl=False):
        b = Buf(name)
        b.excl = excl
        self.bufs.append(b)
        return b

    def _gather(self, op, r, w):
        for b in r:
            for o in b.lw.values():
                op.deps.append(o)
            if b.excl:
                for e2, o in b.rd.items():
                    if e2 != op.eng:
                        op.deps.append(o)
            if b.dcnt:
                op.dwaits.append((b, b.dcnt))
        for b in w:
            for o in b.lw.values():
                op.deps.append(o)
            for o in b.rd.values():
                op.deps.append(o)
            if b.dcnt:
                op.dwaits.append((b, b.dcnt))

    def op(self, eng, fn, r=(), w=()):
        op = Op(eng, fn)
        self._gather(op, r, w)
        for b in r:
            b.rd[eng] = op
        for b in w:
            b.lw[eng] = op
            b.rd = {}
        self.ops[eng].append(op)
        return op

    def dma(self, eng, fn, track, r=(), w=()):
        op = Op(eng, fn)
        self._gather(op, r, w)
        op.dbuf = track
        track.dcnt += 1
        if track in w:
            track.lw = {}
            track.rd = {}
        self.ops[eng].append(op)
        return op

    def barrier(self):
        last = {}
        for e in COMPUTE:
            last[e] = None
            for o in reversed(self.ops[e]):
                if o.fn is None:
                    break
                if o.dbuf is None:
                    last[e] = o
                    break
        dl = [(b, b.dcnt) for b in self.bufs if b.dcnt > b.dseen]
        for b in self.bufs:
            b.dseen = b.dcnt
            b.lw = {}
            b.rd = {}
        for e in ENGS:
            op = Op(e, None)
            for e2 in COMPUTE:
                o = last[e2]
                while o is not None and o.fn is None:
                    o = None
                if o is not None:
                    op.deps.append(o)
            op.dwaits = list(dl)
            self.ops[e].append(op)

    def emit(self, nc, stack):
        for e in ENGS:
            for op in self.ops[e]:
                for d in op.deps:
                    if d.fn is None:
                        continue
                    if d.eng == "pe" and op.eng == "pe":
                        continue
                    d.signal = True
        esem = {e: stack.enter_context(nc.semaphore("s_" + e)) for e in COMPUTE}
        for e in COMPUTE:
            t = 0
            for op in self.ops[e]:
                if op.signal:
                    t += 1
                    op.tick = t
        for b in self.bufs:
            if b.dcnt:
                b.dsem = stack.enter_context(nc.semaphore("d_" + b.name))
        block = stack.enter_context(nc.Block())
        sched = self

        def run(eng_name):
            def body(eng):
                seen = {}
                for op in sched.ops[eng_name]:
                    need = {}
                    for d in op.deps:
                        if d.fn is None:
                            continue
                        if d.eng == "pe" and eng_name == "pe":
                            continue
                        s = esem[d.eng]
                        if d.tick > need.get(s, (0,))[0]:
                            need[s] = (d.tick,)
                    for b, c in op.dwaits:
                        s = b.dsem
                        if 16 * c > need.get(s, (0,))[0]:
                            need[s] = (16 * c,)
                    for s, (v,) in need.items():
                        if seen.get(s, 0) < v:
                            eng.wait_ge(s, v)
                            seen[s] = v
                    if op.fn is None:
                        continue
                    ins = op.fn(eng)
                    if op.dbuf is not None:
                        ins.then_inc(op.dbuf.dsem, 16)
                    elif op.signal:
                        ins.then_inc(esem[eng_name], 1)
            return body

        block.sync(run("sp"))
        block.scalar(run("act"))
        block.gpsimd(run("pool"))
        block.vector(run("dve"))
        block.tensor(run("pe"))


def build_program(S, debug=False):
    NT = S // 128
    GT = 256
    TPG = GT // 128
    NG = S // GT
    QG = 512
    NQG = S // QG
    nc = bass.Bass("TRN2", target_bir_lowering=False)

    def din(name, shape, dt=F32):
        return nc.dram_tensor(name, list(shape), dt, kind="ExternalInput").ap()

    x_d = din("x", [S, D])
    ccol_d = din("c_col", [128, 8])
    wada_d = din("w_ada", [D, NMOD * D])
    bada_d = din("b_ada", [1, NMOD * D])
    gvec_d = din("gvecs", [4, D])
    w1gu_d = din("w1_gu", [D, 2 * DFF])
    w1d_d = din("w1_down", [DFF, D])
    w2gu_d = din("w2_gu", [D, 2 * DFF])
    w2d_d = din("w2_down", [DFF, D])
    win_d = din("w_in", [D, 1184])
    wuq_d = din("w_uq", [256, 768])
    wuqr_d = din("w_uq_rot", [256, 768])
    wuk_d = din("w_ukv_k", [128, 512])
    wuv_d = din("w_ukv_v", [128, 512])
    gql_d = din("g_q_lat_col", [128, 2])
    gkvl_d = din("g_kv_lat_col", [128, 1])
    gqk_d = din("g_qk_row", [1, 640])
    gout_d = din("g_out_col", [128, 8])
    wout_d = din("w_out", [D, D])
    cosm_d = din("cosm", [S, 32])
    sinm_d = din("sinm", [S, 32])
    cosg_d = din("cosg", [S, 64])
    sing_d = din("sing", [S, 64])
    cosf_d = din("cosf", [32, S])
    sinf_d = din("sinf", [32, S])
    y_d = nc.dram_tensor("y", [S, D], F32, kind="ExternalOutput").ap()
    skind = "ExternalOutput" if debug else "Internal"
    modscr = nc.dram_tensor("modscr", [NMOD, 128, D], F32, kind=skind).ap()
    x1scr = nc.dram_tensor("x1scr", [S, D], F32, kind=skind).ap()
    otscr = nc.dram_tensor("otscr", [D, S], BF16, kind=skind).ap()

    S_ = Sched()
    stack = ExitStack()
    with stack:
        big = stack.enter_context(nc.sbuf_tensor("arena", [128, ARENA_BYTES // 4], F32))
        pall = stack.enter_context(nc.psum_tensor("psum", [128, 4096], F32))

        def V(off, dt, shape):
            esz = 4 if dt == F32 else 2
            n = 1
            for s in shape[1:]:
                n *= s
            nb = n * esz
            assert off % 4 == 0 and nb % 4 == 0, (off, nb)
            assert off + nb <= ARENA_BYTES, (off, nb)
            ap = big[:, off // 4:(off + nb) // 4]
            if dt != F32:
                ap = ap.bitcast(dt)
            if len(shape) == 3:
                ap = ap.rearrange("p (a b) -> p a b", a=shape[1], b=shape[2])
            elif len(shape) == 4:
                ap = ap.rearrange("p (a b c) -> p a b c", a=shape[1], b=shape[2], c=shape[3])
            return ap

        def PB(bank, dt=F32):
            ap = pall[:, bank * 512:(bank + 1) * 512]
            if dt != F32:
                ap = ap.bitcast(dt)
            return ap

        class Arena:
            def __init__(self, base):
                self.o = base

            def take(self, nbytes):
                o = self.o
                self.o += (nbytes + 63) // 64 * 64
                assert self.o <= ARENA_BYTES, self.o
                return o

        A0 = Arena(0)
        ident = V(A0.take(256), BF16, [128, 128])
        identf = V(A0.take(512), F32, [128, 128])
        onesf = V(A0.take(256), F32, [128, 64])
        epsT = V(A0.take(64), F32, [128, 1])
        onesb = V(A0.take(128), BF16, [128, 64])
        b_const = S_.buf("const")
        S_.op("pool", lambda e: e.memset(identf, 0.0), w=[b_const])
        S_.op("pool", lambda e: e.affine_select(out=identf, in_=identf, pattern=[[-1, 128]],
                                                compare_op=ALU.not_equal, fill=1.0, base=0,
                                                channel_multiplier=1), w=[b_const])
        S_.op("pool", lambda e: e.tensor_copy(out=ident, in_=identf), r=[b_const], w=[b_const])
        S_.op("pool", lambda e: e.memset(onesf, 1.0), w=[b_const])
        S_.op("pool", lambda e: e.memset(epsT, EPS), w=[b_const])
        S_.op("pool", lambda e: e.memset(onesb, 1.0), w=[b_const])
        PERS = A0.o
        S_.barrier()

        def rstd_ops(ms_ap, out_ap, bufs_r, bufs_w, tmp_ap):
            S_.op("act", lambda e: e.activation(out=tmp_ap, in_=ms_ap, func=AF.Sqrt, bias=epsT[:, 0:1], scale=1.0),
                  r=bufs_r, w=bufs_w)
            S_.op("dve", lambda e: e.reciprocal(out=out_ap, in_=tmp_ap), r=bufs_w, w=bufs_w)

        W_BYTES = 8 * 2 * DFF * 2 + NJ * D * 2

        def ffn_weight_views(base):
            WGU = V(base, BF16, [128, 8, 2 * DFF])
            WD = V(base + 8 * 2 * DFF * 2, BF16, [128, NJ, D])
            return WGU, WD

        def ffn_weight_loads(tag, wgu_d, wd_d, WGU, WD):
            bufs = [S_.buf(tag + "w%d" % i) for i in range(4)]
            wgu_v = wgu_d.rearrange("(k p) n -> p k n", p=128)
            wd_v = wd_d.rearrange("(j p) n -> p j n", p=128)
            steps = []
            n = 0
            for cg in range(2 * DFF // 512):
                def st(cg=cg, b=bufs[n % 4]):
                    S_.dma("pool", lambda e: e.dma_start(out=WGU[:, :, cg * 512:(cg + 1) * 512],
                                                         in_=wgu_v[:, :, cg * 512:(cg + 1) * 512]), b, w=[b])
                steps.append(st)
                n += 1
            for j0 in range(0, NJ, 4):
                nj = min(4, NJ - j0)
                for c0 in (0, 512):
                    def st(j0=j0, nj=nj, c0=c0, b=bufs[n % 4]):
                        S_.dma("pool", lambda e: e.dma_start(out=WD[:, j0:j0 + nj, c0:c0 + 512],
                                                             in_=wd_v[:, j0:j0 + nj, c0:c0 + 512]), b, w=[b])
                    steps.append(st)
                    n += 1
            return steps, bufs

        W1GU, W1D = ffn_weight_views(PERS)
        w1steps, w1bufs = ffn_weight_loads("f1", w1gu_d, w1d_d, W1GU, W1D)
        n_gu_steps = 2 * DFF // 512
        w1gu_steps = [w1steps.pop(0) for _ in range(n_gu_steps)]
        A = Arena(PERS + 8 * 2 * DFF * 2)
        cact = V(A.take(32), F32, [128, 8])
        chb = V(A.take(16), BF16, [128, 8])
        clb = V(A.take(16), BF16, [128, 8])
        creph = V(A.take(2048), BF16, [128, 8, 128])
        crepl = V(A.take(2048), BF16, [128, 8, 128])
        gb = [V(A.take(4096), F32, [128, D]) for _ in range(3)]
        bb = [V(A.take(4096), F32, [128, D]) for _ in range(2)]
        mo = [V(A.take(4096), F32, [128, D]) for _ in range(2)]
        wst = [V(A.take(16384), F32, [128, 8, 512]) for _ in range(3)]
        whb = [V(A.take(8192), BF16, [128, 8, 512]) for _ in range(2)]
        wlb = [V(A.take(8192), BF16, [128, 8, 512]) for _ in range(2)]
        b_c = S_.buf("cact")
        b_gb = [S_.buf("gb%d" % i) for i in range(3)]
        b_bb = [S_.buf("bb%d" % i) for i in range(2)]
        b_mo = [S_.buf("mo%d" % i) for i in range(2)]
        b_wst = [S_.buf("wst%d" % i) for i in range(3)]
        b_wh = [S_.buf("wh%d" % i) for i in range(2)]
        b_wl = [S_.buf("wl%d" % i) for i in range(2)]
        b_pm = [S_.buf("pm%d" % i) for i in range(4)]
        S_.dma("sp", lambda e: e.dma_start(out=cact, in_=ccol_d), b_c, w=[b_c])
        for i in range(3):
            S_.dma("sp", lambda e, i=i: e.dma_start(out=gb[i], in_=gvec_d[i:i + 1, :].partition_broadcast(128)),
                   b_gb[i], w=[b_gb[i]])
        S_.op("act", lambda e: e.activation(out=cact, in_=cact, func=AF.Silu), r=[b_c], w=[b_c])
        S_.op("dve", lambda e: e.tensor_copy(out=chb, in_=cact), r=[b_c], w=[b_c])
        S_.op("dve", lambda e: e.tensor_tensor(out=clb, in0=cact, in1=chb, op=ALU.subtract), r=[b_c], w=[b_c])
        for k in range(8):
            S_.op("dve", lambda e, k=k: e.tensor_copy(out=creph[:, k, :], in_=chb[:, k:k + 1].to_broadcast([128, 128])),
                  r=[b_c], w=[b_c])
            S_.op("dve", lambda e, k=k: e.tensor_copy(out=crepl[:, k, :], in_=clb[:, k:k + 1].to_broadcast([128, 128])),
                  r=[b_c], w=[b_c])
        wada_v = wada_d.rearrange("(k p) n -> p k n", p=128)
        def p0_load(jh):
            j, hf = jh // 2, jh % 2
            w3 = jh % 3
            S_.dma("sp", lambda e: e.dma_start(out=wst[w3], in_=wada_v[:, :, j * D + hf * 512:j * D + (hf + 1) * 512]),
                   b_wst[w3], w=[b_wst[w3]])
            if hf == 0:
                S_.dma("sp", lambda e: e.dma_start(out=bb[j % 2], in_=bada_d[0:1, j * D:(j + 1) * D].partition_broadcast(128)),
                       b_bb[j % 2], w=[b_bb[j % 2]])

        b_wl2 = [[S_.buf("wl%d_%d" % (i, q)) for q in range(2)] for i in range(2)]
        p0_load(0)
        p0_load(1)
        for jh in range(2 * NMOD):
            j, hf = jh // 2, jh % 2
            ws = jh % 2
            w3 = jh % 3
            ms = j % 2
            if jh + 2 < 2 * NMOD:
                p0_load(jh + 2)
            if w1gu_steps:
                w1gu_steps.pop(0)()
            S_.op("act", lambda e, ws=ws, w3=w3: e.activation(out=whb[ws], in_=wst[w3], func=AF.Copy), r=[b_wst[w3]], w=[b_wh[ws]])
            for q in range(2):
                qsl = slice(q * 256, (q + 1) * 256)
                S_.op("dve" if q == 0 else "pool",
                      lambda e, ws=ws, w3=w3, qsl=qsl: e.tensor_tensor(out=wlb[ws][:, :, qsl], in0=wst[w3][:, :, qsl],
                                                                       in1=whb[ws][:, :, qsl], op=ALU.subtract),
                      r=[b_wst[w3], b_wh[ws]], w=[b_wl2[ws][q]])
            pb = jh % 4
            ps = PB(pb)

            def mm(e, ws=ws, ps=ps):
                n = 0
                for (cr, wb) in ((creph, whb), (crepl, whb), (creph, wlb)):
                    for k in range(8):
                        ins = e.matmul(ps, lhsT=cr[:, k, :], rhs=wb[ws][:, k, :], start=(n == 0), stop=(n == 23))
                        n += 1
                return ins
            S_.op("pe", mm, r=[b_c, b_wh[ws], b_wl2[ws][0], b_wl2[ws][1]], w=[b_pm[pb]])
            osl = mo[ms][:, hf * 512:(hf + 1) * 512]
            bsl = bb[ms][:, hf * 512:(hf + 1) * 512]
            S_.op("dve", lambda e, osl=osl, bsl=bsl, ps=ps: e.tensor_tensor(out=osl, in0=ps, in1=bsl, op=ALU.add),
                  r=[b_pm[pb], b_bb[ms]], w=[b_mo[ms]])
            if hf == 0:
                continue
            kind = j % 3
            if kind == 1:
                gsel = gb[j // 3]
                S_.op("dve", lambda e, ms=ms, gsel=gsel: e.scalar_tensor_tensor(
                    out=mo[ms], in0=mo[ms], scalar=1.0, in1=gsel, op0=ALU.add, op1=ALU.mult),
                    r=[b_gb[j // 3]], w=[b_mo[ms]])
            elif kind == 2 and j != 5:
                S_.op("dve", lambda e, ms=ms: e.tensor_scalar(out=mo[ms], in0=mo[ms], scalar1=0.5, scalar2=None,
                                                              op0=ALU.mult), w=[b_mo[ms]])
            S_.dma("sp", lambda e, j=j, ms=ms: e.dma_start(out=modscr[j], in_=mo[ms]), b_mo[ms], r=[b_mo[ms]])
        while w1gu_steps:
            w1gu_steps.pop(0)()
        S_.barrier()

        def ffn_phase(tag, wgu_d, wd_d, iA, iB, iG, src_d, final, preloaded=False, pending=None):
            A = Arena(PERS)
            WGU, WD = ffn_weight_views(A.take(W_BYTES))
            At = V(A.take(4096), F32, [128, D])
            Bt = V(A.take(4096), F32, [128, D])
            Gt = V(A.take(4096), F32, [128, D])
            Gf = V(A.take(4096), F32, [128, D])
            gtmp = [V(A.take(2048), F32, [128, 512])] * 2
            xb = [V(A.take(4096), F32, [128, D]) for _ in range(3 * TPG)]
            tmpb = [V(A.take(4096), F32, [128, D]) for _ in range(2)]
            hbf = [V(A.take(2048), BF16, [128, D]) for _ in range(2)]
            hT = V(A.take(8 * GT * 2), BF16, [128, 8, GT])
            gact = [V(A.take(GT * 4), F32, [128, GT]) for _ in range(2)]
            actT = V(A.take(NJ * GT * 2), BF16, [128, NJ, GT])
            junk = V(A.take(2048), BF16, [128, D])
            stat = V(A.take(64 * 4), F32, [128, 64])

            bn = lambda n: S_.buf(tag + n)
            b_A, b_B, b_G, b_Gf = bn("A"), bn("B"), bn("G"), bn("Gf")
            b_gtmp = [bn("gtmp")] * 2
            b_x = [bn("x%d" % i) for i in range(3 * TPG)]
            b_tmp = [bn("tmp%d" % i) for i in range(2)]
            b_hbf = [bn("hbf%d" % i) for i in range(2)]
            b_hT, b_actT, b_junk = bn("hT"), bn("actT"), bn("junk")
            b_gact = [bn("gact%d" % i) for i in range(2)]
            b_stat = [bn("stat%d" % i) for i in range(2 * TPG)]
            b_pgu = [bn("pgu%d" % i) for i in range(4)]
            b_pT = bn("pT")
            b_pd = [bn("pd%d" % i) for i in range(3)]
            pgu = [PB(i) for i in range(4)]
            pT = PB(4, BF16).rearrange("p (a b) -> p a b", a=8, b=128)
            pd = [PB(5 + i) for i in range(3)]

            S_.dma("sp", lambda e: e.dma_start(out=At, in_=modscr[iA]), b_A, w=[b_A])
            S_.dma("sp", lambda e: e.dma_start(out=Bt, in_=modscr[iB]), b_B, w=[b_B])
            S_.dma("sp", lambda e: e.dma_start(out=Gt, in_=modscr[iG]), b_G, w=[b_G])
            if final:
                S_.dma("sp", lambda e: e.dma_start(out=Gf, in_=gvec_d[3:4, :].partition_broadcast(128)), b_Gf, w=[b_Gf])
            if preloaded:
                wbufs = []
            elif pending is not None:
                wsteps, wbufs = pending
                for st_ in wsteps:
                    st_()
            else:
                wsteps, wbufs = ffn_weight_loads(tag, wgu_d, wd_d, WGU, WD)
                for st_ in wsteps:
                    st_()
            b_wgu_l = list(wbufs)
            b_wd_l = list(wbufs)

            b_nst = [bn("nst%d" % p) for p in range(3)]
            b_fst = bn("fst")

            def x_load(g):
                p = g % 3
                for tt in range(TPG):
                    t = g * TPG + tt
                    xs = p * TPG + tt
                    S_.dma("sp", lambda e, t=t, xs=xs: e.dma_start(out=xb[xs], in_=src_d[t * 128:(t + 1) * 128, :]),
                           b_x[xs], w=[b_x[xs]])

            def norm_load_sq(g):
                p = g % 3
                for tt in range(TPG):
                    t = g * TPG + tt
                    xs = p * TPG + tt
                    S_.op("act", lambda e, xs=xs, c=p * 8 + tt: e.activation(out=junk, in_=xb[xs], func=AF.Square, scale=1.0 / 32.0,
                                                                             accum_out=stat[:, c:c + 1]),
                          r=[b_x[xs]], w=[b_junk, b_nst[p]])

            def norm_rstd(g):
                p = g % 3
                rstd_ops(stat[:, p * 8:p * 8 + TPG], stat[:, p * 8 + 4:p * 8 + 4 + TPG], [b_nst[p]], [b_nst[p]],
                         stat[:, p * 8 + 2:p * 8 + 2 + TPG])

            def norm_apply(g, tt):
                p = g % 3
                t = g * TPG + tt
                xs = p * TPG + tt
                ts = t % 2
                c = p * 8 + 4 + tt
                S_.op("act", lambda e: e.activation(out=tmpb[ts], in_=xb[xs], func=AF.Copy, scale=stat[:, c:c + 1]),
                      r=[b_x[xs], b_nst[p]], w=[b_tmp[ts]])
                S_.op("pool", lambda e: e.tensor_tensor(out=tmpb[ts], in0=tmpb[ts], in1=At, op=ALU.mult), r=[b_A], w=[b_tmp[ts]])
                S_.op("pool", lambda e: e.tensor_tensor(out=hbf[ts], in0=tmpb[ts], in1=Bt, op=ALU.add),
                      r=[b_tmp[ts], b_B], w=[b_hbf[ts]])

            def transpose_group(g):
                for tt in range(TPG):
                    t = g * TPG + tt
                    ts = t % 2

                    def tr(e, ts=ts):
                        for k in range(8):
                            ins = e.transpose(out=pT[:, k, :], in_=hbf[ts][:, k * 128:(k + 1) * 128], identity=ident)
                        return ins
                    S_.op("pe", tr, r=[b_hbf[ts], b_const], w=[b_pT])
                    S_.op("act", lambda e, tt=tt: e.activation(out=hT[:, :, tt * 128:(tt + 1) * 128], in_=pT, func=AF.Copy),
                          r=[b_pT], w=[b_hT])

            def gateup_group(g, hooks=None):
                for j in range(NJ):
                    if hooks and j in hooks:
                        hooks[j]()
                    pb = j % 4
                    gb_ = j % 2

                    def mm(e, j=j, pb=pb):
                        for k in range(8):
                            e.matmul(pgu[pb][:, 0:GT], lhsT=WGU[:, k, j * 128:(j + 1) * 128], rhs=hT[:, k, :],
                                     start=(k == 0), stop=(k == 7))
                        for k in range(8):
                            ins = e.matmul(pgu[pb][:, GT:2 * GT], lhsT=WGU[:, k, DFF + j * 128:DFF + (j + 1) * 128],
                                           rhs=hT[:, k, :], start=(k == 0), stop=(k == 7))
                        return ins
                    S_.op("pe", mm, r=b_wgu_l + [b_hT], w=[b_pgu[pb]])
                    S_.op("act", lambda e, pb=pb, gb_=gb_: e.activation(out=gact[gb_], in_=pgu[pb][:, 0:GT], func=AF.Silu),
                          r=[b_pgu[pb]], w=[b_gact[gb_]])
                    S_.op("dve", lambda e, j=j, pb=pb, gb_=gb_: e.tensor_tensor(out=actT[:, j, :], in0=pgu[pb][:, GT:2 * GT],
                                                                                in1=gact[gb_], op=ALU.mult),
                          r=[b_pgu[pb], b_gact[gb_]], w=[b_actT])

            def down_group(g):
                for tt in range(TPG):
                    t = g * TPG + tt
                    xs = (g % 3) * TPG + tt
                    for hf in range(2):
                        pi = (2 * t + hf) % 3

                        def mm(e, tt=tt, hf=hf, pi=pi):
                            for j in range(NJ):
                                ins = e.matmul(pd[pi], lhsT=actT[:, j, tt * 128:(tt + 1) * 128],
                                               rhs=WD[:, j, hf * 512:(hf + 1) * 512], start=(j == 0), stop=(j == NJ - 1))
                            return ins
                        S_.op("pe", mm, r=[b_actT] + b_wd_l, w=[b_pd[pi]])
                        gi = (2 * t + hf) % 2
                        S_.op("dve", lambda e, hf=hf, pi=pi, gi=gi: e.tensor_tensor(
                            out=gtmp[gi], in0=pd[pi], in1=Gt[:, hf * 512:(hf + 1) * 512], op=ALU.mult),
                            r=[b_pd[pi], b_G], w=[b_gtmp[gi]])
                        S_.op("dve", lambda e, xs=xs, hf=hf, gi=gi: e.tensor_tensor(
                            out=xb[xs][:, hf * 512:(hf + 1) * 512], in0=xb[xs][:, hf * 512:(hf + 1) * 512], in1=gtmp[gi],
                            op=ALU.add), r=[b_gtmp[gi]], w=[b_x[xs]])
                    if not final:
                        S_.dma("sp", lambda e, t=t, xs=xs: e.dma_start(out=x1scr[t * 128:(t + 1) * 128, :], in_=xb[xs]),
                               b_x[xs], r=[b_x[xs]])
                if final:
                    for tt in range(TPG):
                        xs = (g % 3) * TPG + tt
                        S_.op("act", lambda e, xs=xs, tt=tt: e.activation(out=junk, in_=xb[xs], func=AF.Square, scale=1.0 / 32.0,
                                                                          accum_out=stat[:, 32 + tt:33 + tt]),
                              r=[b_x[xs]], w=[b_junk, b_fst])

            def final_rstd(g):
                rstd_ops(stat[:, 32:32 + TPG], stat[:, 36:36 + TPG], [b_fst], [b_fst], stat[:, 34:34 + TPG])

            def final_apply(g):
                for tt in range(TPG):
                    t = g * TPG + tt
                    xs = (g % 3) * TPG + tt
                    S_.op("act", lambda e, xs=xs, tt=tt: e.activation(out=xb[xs], in_=xb[xs], func=AF.Copy,
                                                                      scale=stat[:, 36 + tt:37 + tt]),
                          r=[b_fst], w=[b_x[xs]])
                    S_.op("pool", lambda e, xs=xs: e.tensor_tensor(out=xb[xs], in0=xb[xs], in1=Gf, op=ALU.mult),
                          r=[b_Gf], w=[b_x[xs]])
                    S_.dma("sp", lambda e, t=t, xs=xs: e.dma_start(out=y_d[t * 128:(t + 1) * 128, :], in_=xb[xs]),
                           b_x[xs], r=[b_x[xs]])

            x_load(0)
            if NG > 1:
                x_load(1)
            norm_load_sq(0)
            norm_rstd(0)
            for tt in range(TPG):
                norm_apply(0, tt)
            transpose_group(0)
            for g in range(NG):
                hooks = {}
                if g + 1 < NG:
                    hooks[1] = (lambda g=g: norm_load_sq(g + 1))
                    for tt in range(TPG):
                        hooks[9 + 4 * tt] = (lambda g=g, tt=tt: norm_apply(g + 1, tt))

                def sqrt_hook(g=g):
                    if g + 1 < NG:
                        norm_rstd(g + 1)
                    if final and g >= 1:
                        final_rstd(g - 1)
                hooks[6] = sqrt_hook
                if final and g >= 1:
                    hooks[7] = (lambda g=g: final_apply(g - 1))
                gateup_group(g, hooks)
                if g + 1 < NG:
                    transpose_group(g + 1)
                if g + 2 < NG:
                    x_load(g + 2)
                down_group(g)
            if final:
                final_rstd(NG - 1)
                final_apply(NG - 1)
            S_.barrier()

        ffn_phase("f1", w1gu_d, w1d_d, 1, 0, 2, x_d, False, pending=(w1steps, w1bufs))

        A = Arena(PERS)
        WGU_BYTES = 8 * 2 * DFF * 2
        qnT = V(A.take(2 * S * 2), BF16, [128, 2, S])
        kvnT = V(A.take(S * 2), BF16, [128, S])
        KTm = [V(A.take(S * 2), BF16, [128, S]) for _ in range(2)]
        HOLE0 = A.o
        HOLE_END = PERS + WGU_BYTES
        if HOLE_END < HOLE0:
            HOLE_END = HOLE0
        AH = Arena(HOLE0)
        A.o = HOLE_END
        QTg = V(A.take(4 * S * 2), BF16, [128, 4, S])
        KTg = V(A.take(4 * S * 2), BF16, [128, 4, S])
        VAg = V(A.take(NT * 2 * 66 * 2), BF16, [128, NT, 2, 66])
        RES_END = A.o

        def hole_take(arena, nbytes):
            if AHH[0].o + (nbytes + 63) // 64 * 64 <= HOLE_END:
                return AHH[0].take(nbytes)
            return arena.take(nbytes)
        AHH = [AH]
        b_res = S_.buf("res")
        Win = V(hole_take(A, 8 * 1184 * 2), BF16, [128, 8, 1184])
        ABcol = V(A.take(64), F32, [128, 16])
        biash = V(hole_take(A, 1184 * 2), BF16, [128, 1184])
        biasl = V(A.take(1184 * 2), BF16, [128, 1184])
        gqk = V(hole_take(A, 640 * 4), F32, [128, 640])
        cosm = V(hole_take(A, NT * 32 * 4), F32, [128, NT, 32])
        sinm = V(hole_take(A, NT * 32 * 4), F32, [128, NT, 32])
        cosg = V(hole_take(A, NT * 64 * 4), F32, [128, NT, 64])
        sing = V(hole_take(A, NT * 64 * 4), F32, [128, NT, 64])
        wstg_off = A.o
        x2b = [V(A.take(4096), F32, [128, D]) for _ in range(2)]
        hb2 = [V(A.take(2048), BF16, [128, D]) for _ in range(2)]
        hT2 = [V(A.take(8 * 128 * 2), BF16, [128, 8, 128])]
        junk2 = V(A.take(2048), BF16, [128, D])
        assert A.o - wstg_off >= 16384
        hT2.append(V(A.take(8 * 128 * 2), BF16, [128, 8, 128]))
        biasf = V(A.o, F32, [128, 1184])
        sq2 = [V(A.take(640 * 4), F32, [128, 640]) for _ in range(2)]
        Brep = V(A.o, BF16, [128, 8, 128])
        qk2 = [V(A.take(640 * 4), F32, [128, 640]) for _ in range(2)]
        t12 = [V(A.take(640 * 4), F32, [128, 640]) for _ in range(2)]
        qkb = [V(A.take(640 * 2), BF16, [128, 640]) for _ in range(2)]
        kz = [V(A.take(4 * 128 * 2), BF16, [128, 4, 128]) for _ in range(2)]
        lat = [V(A.take(384 * 2), BF16, [128, 384]) for _ in range(2)]
        kpad = [V(A.take(128 * 2), BF16, [128, 128]) for _ in range(2)]
        kr = [V(A.take(32 * 4), F32, [128, 32]) for _ in range(2)]
        kt1 = [V(A.take(32 * 4), F32, [128, 32]) for _ in range(2)]
        kt2 = [V(A.take(32 * 4), F32, [128, 32]) for _ in range(2)]
        st2 = [V(A.take(32 * 4), F32, [128, 32]) for _ in range(2)]
        wstg = V(wstg_off, F32, [128, 8, 512])

        b_win, b_AB, b_bias, b_gqk, b_tab = (S_.buf("win"), S_.buf("AB"), S_.buf("bias"), S_.buf("gqk"), S_.buf("tab"))
        b_x2 = [S_.buf("x2_%d" % i) for i in range(2)]
        b_hT2, b_junk2 = [S_.buf("hT2_0"), S_.buf("hT2_1")], S_.buf("junk2")
        b_hb2 = [S_.buf("hb2_%d" % i) for i in range(2)]
        pb2 = lambda n: [S_.buf("%s_%d" % (n, i)) for i in range(2)]
        b_sq2, b_qk2, b_t12, b_qkb, b_kz = pb2("sq2"), pb2("qk2"), pb2("t12"), pb2("qkb"), pb2("kz")
        b_lat, b_kpad, b_kr, b_kt1, b_kt2, b_st2 = pb2("lat"), pb2("kpad"), pb2("kr"), pb2("kt1"), pb2("kt2"), pb2("st2")
        b_wstg = S_.buf("wstg")
        b_pz = [[S_.buf("pz%d_%d" % (i, j), j == 2) for j in range(3)] for i in range(2)]
        b_pT1, b_pT2 = S_.buf("pT1"), S_.buf("pT2")
        pz = [[PB(3 * i + j) for j in range(3)] for i in range(2)]
        pT1 = PB(6, BF16).rearrange("p (a b) -> p a b", a=8, b=128)
        pT2 = PB(7, BF16).rearrange("p (a b) -> p a b", a=8, b=128)

        S_.dma("sp", lambda e: e.dma_start(out=gqk, in_=gqk_d[0:1, :].partition_broadcast(128)), b_gqk, w=[b_gqk])
        S_.dma("sp", lambda e: e.dma_start(out=ABcol[:, 0:8], in_=modscr[4][0:1, :].rearrange("o (k p) -> p (o k)", p=128),
                                           allow_slow_non_contiguous=True), b_AB, w=[b_AB])
        S_.dma("sp", lambda e: e.dma_start(out=ABcol[:, 8:16], in_=modscr[3][0:1, :].rearrange("o (k p) -> p (o k)", p=128),
                                           allow_slow_non_contiguous=True), b_AB, w=[b_AB])
        for k in range(8):
            S_.op("dve", lambda e, k=k: e.tensor_copy(out=Brep[:, k, :], in_=ABcol[:, 8 + k:9 + k].to_broadcast([128, 128])),
                  r=[b_AB], w=[b_AB])
        S_.dma("sp", lambda e: e.dma_start(out=cosm, in_=cosm_d.rearrange("(t p) d -> p t d", p=128)), b_tab, w=[b_tab])
        S_.dma("sp", lambda e: e.dma_start(out=sinm, in_=sinm_d.rearrange("(t p) d -> p t d", p=128)), b_tab, w=[b_tab])
        S_.dma("sp", lambda e: e.dma_start(out=cosg, in_=cosg_d.rearrange("(t p) d -> p t d", p=128)), b_tab, w=[b_tab])
        S_.dma("sp", lambda e: e.dma_start(out=sing, in_=sing_d.rearrange("(t p) d -> p t d", p=128)), b_tab, w=[b_tab])
        win_v = win_d.rearrange("(k p) n -> p k n", p=128)
        pbias = PB(0)
        b_pbias = S_.buf("pbias")
        for (c0, c1) in ((0, 512), (512, 1024), (1024, 1184)):
            S_.dma("sp", lambda e, c0=c0, c1=c1: e.dma_start(out=wstg[:, :, 0:c1 - c0], in_=win_v[:, :, c0:c1]),
                   b_wstg, w=[b_wstg])
            S_.op("act", lambda e, c0=c0, c1=c1: e.activation(out=Win[:, :, c0:c1], in_=wstg[:, :, 0:c1 - c0], func=AF.Copy),
                  r=[b_wstg], w=[b_win])

            def bmm(e, c0=c0, c1=c1):
                for k in range(8):
                    ins = e.matmul(pbias[:, 0:c1 - c0], lhsT=Brep[:, k, :], rhs=Win[:, k, c0:c1], start=(k == 0), stop=(k == 7))
                return ins
            S_.op("pe", bmm, r=[b_AB, b_win], w=[b_pbias])
            S_.op("dve", lambda e, c0=c0, c1=c1: e.tensor_copy(out=biasf[:, c0:c1], in_=pbias[:, 0:c1 - c0]),
                  r=[b_pbias], w=[b_bias])
            for k in range(8):
                S_.op("dve", lambda e, c0=c0, c1=c1, k=k: e.tensor_scalar(out=Win[:, k, c0:c1], in0=wstg[:, k, 0:c1 - c0],
                                                                          scalar1=ABcol[:, k:k + 1], scalar2=None, op0=ALU.mult),
                      r=[b_wstg, b_AB], w=[b_win])
        S_.op("dve", lambda e: e.tensor_copy(out=biash, in_=biasf), r=[b_bias], w=[b_bias])
        S_.op("dve", lambda e: e.tensor_tensor(out=biasl, in0=biasf, in1=biash, op=ALU.subtract), r=[b_bias], w=[b_bias])
        S_.op("pool", lambda e: e.memset(VAg, 1.0), w=[b_res])
        for i in range(2):
            S_.op("pool", lambda e, i=i: e.memset(kpad[i], 0.0), w=[b_kpad[i]])
            S_.op("pool", lambda e, i=i: e.memset(kz[i], 0.0), w=[b_kz[i]])
            S_.op("pool", lambda e, i=i: e.memset(KTm[i], 0.0), w=[b_res])
        S_.barrier()

        def p2_front_a(t):
            xs = t % 2
            st = st2[xs]
            bs = b_st2[xs]
            S_.dma("sp", lambda e: e.dma_start(out=x2b[xs], in_=x1scr[t * 128:(t + 1) * 128, :]), b_x2[xs], w=[b_x2[xs]])
            S_.op("act", lambda e: e.activation(out=junk2, in_=x2b[xs], func=AF.Square, scale=1.0 / 32.0,
                                                accum_out=st[:, 0:1]), r=[b_x2[xs]], w=[b_junk2, bs])
            rstd_ops(st[:, 0:1], st[:, 2:3], [bs], [bs], st[:, 1:2])
            S_.op("dve", lambda e: e.tensor_scalar(out=hb2[xs], in0=x2b[xs], scalar1=st[:, 2:3], scalar2=None, op0=ALU.mult),
                  r=[b_x2[xs], bs], w=[b_hb2[xs]])

        def p2_front_b(t):
            xs = t % 2

            def tr(e):
                for k in range(8):
                    ins = e.transpose(out=pT1[:, k, :], in_=hb2[xs][:, k * 128:(k + 1) * 128], identity=ident)
                return ins
            S_.op("pe", tr, r=[b_hb2[xs], b_const], w=[b_pT1])
            S_.op("act", lambda e: e.activation(out=hT2[xs], in_=pT1, func=AF.Copy), r=[b_pT1], w=[b_hT2[xs]])

        def p2_front_b2(t):
            xs = t % 2

            for (pi, c0, c1) in ((0, 0, 416), (1, 416, 928), (2, 928, 1184)):
                def zmm(e, pi=pi, c0=c0, c1=c1):
                    for k in range(8):
                        e.matmul(pz[xs][pi][:, 0:c1 - c0], lhsT=hT2[xs][:, k, :], rhs=Win[:, k, c0:c1],
                                 start=(k == 0), stop=False)
                    e.matmul(pz[xs][pi][:, 0:c1 - c0], lhsT=ident, rhs=biash[:, c0:c1], start=False, stop=False)
                    return e.matmul(pz[xs][pi][:, 0:c1 - c0], lhsT=ident, rhs=biasl[:, c0:c1], start=False, stop=True)
                S_.op("pe", zmm, r=[b_hT2[xs], b_win, b_bias, b_const], w=[b_pz[xs][pi]])

        def p2_back_a(t):
            xs = t % 2
            st = st2[xs]
            bs = b_st2[xs]
            tsl = slice(t * 128, (t + 1) * 128)
            z0, z1, z2 = pz[xs]
            bz0, bz1, bz2 = b_pz[xs]
            S_.op("act", lambda e: e.activation(out=junk2[:, 0:256], in_=z0[:, 0:256], func=AF.Square,
                                                scale=1.0 / 16.0, accum_out=st[:, 4:5]), r=[bz0], w=[b_junk2, bs])
            S_.op("act", lambda e: e.activation(out=junk2[:, 256:384], in_=z0[:, 256:384], func=AF.Square,
                                                scale=1.0 / math.sqrt(128.0), accum_out=st[:, 5:6]), r=[bz0], w=[b_junk2, bs])
            S_.op("act", lambda e: e.activation(out=sq2[xs][:, 0:512], in_=z1, func=AF.Square, scale=1.0 / 8.0),
                  r=[bz1], w=[b_sq2[xs]])
            S_.op("act", lambda e: e.activation(out=sq2[xs][:, 512:640], in_=z2[:, 0:128], func=AF.Square, scale=1.0 / 8.0),
                  r=[bz2], w=[b_sq2[xs]])
            S_.op("dve", lambda e: e.tensor_reduce(out=st[:, 10:20], in_=sq2[xs].rearrange("p (h d) -> p h d", d=64),
                                                   axis=AX.X, op=ALU.add), r=[b_sq2[xs]], w=[bs])
            S_.op("dve", lambda e: e.tensor_copy(out=st[:, 8:10], in_=st[:, 4:6]), r=[bs], w=[bs])
            S_.op("act", lambda e: e.activation(out=st[:, 20:32], in_=st[:, 8:20], func=AF.Sqrt, bias=epsT[:, 0:1], scale=1.0),
                  r=[bs], w=[bs])
            S_.op("dve", lambda e: e.reciprocal(out=st[:, 8:20], in_=st[:, 20:32]), r=[bs], w=[bs])
            S_.op("dve", lambda e: e.tensor_scalar(out=lat[xs][:, 0:256], in0=z0[:, 0:256], scalar1=st[:, 8:9],
                                                   scalar2=None, op0=ALU.mult), r=[bz0, bs], w=[b_lat[xs]])
            S_.op("dve", lambda e: e.tensor_scalar(out=lat[xs][:, 256:384], in0=z0[:, 256:384], scalar1=st[:, 9:10],
                                                   scalar2=None, op0=ALU.mult), r=[bz0, bs], w=[b_lat[xs]])
            S_.op("dve", lambda e: e.tensor_copy(out=kr[xs], in_=z0[:, 384:416]), r=[bz0], w=[b_kr[xs]])
            S_.op("pool", lambda e: e.tensor_tensor(out=kt1[xs], in0=kr[xs], in1=cosm[:, t, :], op=ALU.mult),
                  r=[b_kr[xs], b_tab], w=[b_kt1[xs]])
            krv = kr[xs].rearrange("p (b h f) -> p b h f", b=2, h=2, f=8)
            kt2v = kt2[xs].rearrange("p (b h f) -> p b h f", b=2, h=2, f=8)
            for hh in range(2):
                S_.op("pool", lambda e, hh=hh: e.tensor_tensor(
                    out=kt2v[:, :, hh, :], in0=krv[:, :, 1 - hh, :],
                    in1=sinm[:, t, :].rearrange("p (b h f) -> p b h f", b=2, h=2, f=8)[:, :, hh, :], op=ALU.mult),
                    r=[b_kr[xs], b_tab], w=[b_kt2[xs]])
            S_.op("pool", lambda e: e.tensor_tensor(out=kpad[xs][:, 64:96], in0=kt1[xs], in1=kt2[xs], op=ALU.add),
                  r=[b_kt1[xs], b_kt2[xs]], w=[b_kpad[xs]])
            qk3 = qk2[xs].rearrange("p (h d) -> p h d", d=64)
            S_.op("dve", lambda e: e.tensor_tensor(out=qk3[:, 0:8, :], in0=z1.rearrange("p (h d) -> p h d", d=64),
                                                   in1=st[:, 10:18].unsqueeze(2).to_broadcast([128, 8, 64]), op=ALU.mult),
                  r=[bz1, bs], w=[b_qk2[xs]])
            S_.op("dve", lambda e: e.tensor_tensor(out=qk3[:, 8:10, :], in0=z2[:, 0:128].rearrange("p (h d) -> p h d", d=64),
                                                   in1=st[:, 18:20].unsqueeze(2).to_broadcast([128, 2, 64]), op=ALU.mult),
                  r=[bz2, bs], w=[b_qk2[xs]])
            S_.op("act", lambda e: e.activation(out=VAg[:, t, :, 0:64], in_=z2[:, 128:256].rearrange("p (h d) -> p h d", d=64),
                                                func=AF.Copy), r=[bz2], w=[b_res])

        def p2_back_b(t):
            xs = t % 2
            tsl = slice(t * 128, (t + 1) * 128)
            qk3 = qk2[xs].rearrange("p (h d) -> p h d", d=64)
            S_.op("dve", lambda e: e.tensor_tensor(out=qk2[xs], in0=qk2[xs], in1=gqk, op=ALU.mult), r=[b_gqk], w=[b_qk2[xs]])
            S_.op("pool", lambda e: e.tensor_tensor(out=t12[xs].rearrange("p (h d) -> p h d", d=64), in0=qk3,
                                                    in1=cosg[:, t, :].unsqueeze(1).to_broadcast([128, 10, 64]),
                                                    op=ALU.mult), r=[b_qk2[xs], b_tab], w=[b_t12[xs]])
            qk5 = qk2[xs].rearrange("p (h b t f) -> p h b t f", h=10, b=2, t=2, f=16)
            t25 = sq2[xs].rearrange("p (h b t f) -> p h b t f", h=10, b=2, t=2, f=16)
            for hh in range(2):
                S_.op("dve", lambda e, hh=hh: e.tensor_tensor(
                    out=t25[:, :, :, hh, :], in0=qk5[:, :, :, 1 - hh, :],
                    in1=sing[:, t, :].rearrange("p (b t f) -> p b t f", b=2, t=2, f=16)[:, :, hh, :]
                    .unsqueeze(1).to_broadcast([128, 10, 2, 16]), op=ALU.mult),
                    r=[b_qk2[xs], b_tab], w=[b_sq2[xs]])
            S_.op("pool", lambda e: e.tensor_tensor(out=qkb[xs], in0=t12[xs], in1=sq2[xs], op=ALU.add),
                  r=[b_t12[xs], b_sq2[xs]], w=[b_qkb[xs]])
            kz5 = kz[xs].rearrange("p (k v) d -> p k v d", k=2, v=2)
            kin = qkb[xs][:, 512:640].rearrange("p (k d) -> p k d", d=64)
            S_.op("pool", lambda e: e.tensor_copy(out=kz5[:, :, 0, 0:64], in_=kin), r=[b_qkb[xs]], w=[b_kz[xs]])
            S_.op("pool", lambda e: e.tensor_copy(out=kz5[:, :, 1, 64:128], in_=kin), r=[b_qkb[xs]], w=[b_kz[xs]])


        def p2_back_b2(t):
            xs = t % 2
            tsl = slice(t * 128, (t + 1) * 128)

            def tr2(e):
                e.transpose(out=pT2[:, 0, :], in_=lat[xs][:, 0:128], identity=ident)
                e.transpose(out=pT2[:, 1, :], in_=lat[xs][:, 128:256], identity=ident)
                e.transpose(out=pT2[:, 2, :], in_=lat[xs][:, 256:384], identity=ident)
                e.transpose(out=pT2[:, 3, :], in_=kpad[xs], identity=ident)
                for c in range(4):
                    ins = e.transpose(out=pT2[:, 4 + c, :], in_=qkb[xs][:, c * 128:(c + 1) * 128], identity=ident)
                return ins
            S_.op("pe", tr2, r=[b_lat[xs], b_kpad[xs], b_qkb[xs], b_const], w=[b_pT2])

            S_.op("act", lambda e: e.activation(out=qnT[:, :, tsl], in_=pT2[:, 0:2, :], func=AF.Copy), r=[b_pT2], w=[b_res])
            S_.op("dve", lambda e: e.tensor_copy(out=kvnT[:, tsl], in_=pT2[:, 2, :]), r=[b_pT2], w=[b_res])
            for i in range(2):
                S_.op("act", lambda e, i=i: e.activation(out=KTm[i][64:96, tsl], in_=pT2[64:96, 3, :], func=AF.Copy),
                      r=[b_pT2], w=[b_res])
            S_.op("act", lambda e: e.activation(out=QTg[:, :, tsl], in_=pT2[:, 4:8, :], func=AF.Copy), r=[b_pT2], w=[b_res])

            def tr3(e):
                for vv in range(4):
                    ins = e.transpose(out=pT2[:, vv, :], in_=kz[xs][:, vv, :], identity=ident)
                return ins
            S_.op("pe", tr3, r=[b_kz[xs], b_const], w=[b_pT2])
            S_.op("dve", lambda e: e.tensor_copy(out=KTg[:, :, tsl], in_=pT2[:, 0:4, :]), r=[b_pT2], w=[b_res])

        for t0 in range(min(2, NT)):
            p2_front_a(t0)
            p2_front_b(t0)
        for t0 in range(min(2, NT)):
            p2_front_b2(t0)
        if NT > 2:
            p2_front_a(2)
            p2_front_b(2)
        for t in range(NT):
            p2_back_a(t)
            if t + 2 < NT:
                p2_front_b2(t + 2)
            p2_back_b(t)
            if t + 3 < NT:
                p2_front_a(t + 3)
                p2_front_b(t + 3)
            p2_back_b2(t)
        if debug:
            dq = nc.dram_tensor("dbg_qnT", [128, 2, S], BF16, kind="ExternalOutput").ap()
            dk = nc.dram_tensor("dbg_kvnT", [128, S], BF16, kind="ExternalOutput").ap()
            dkt = nc.dram_tensor("dbg_KTm", [128, S], BF16, kind="ExternalOutput").ap()
            b_dbg = S_.buf("dbg")
            S_.dma("sp", lambda e: e.dma_start(out=dq, in_=qnT), b_dbg, r=[b_res])
            S_.dma("sp", lambda e: e.dma_start(out=dk, in_=kvnT), b_dbg, r=[b_res])
            S_.dma("sp", lambda e: e.dma_start(out=dkt, in_=KTm[0]), b_dbg, r=[b_res])
        S_.barrier()

        A = Arena(RES_END)
        AHH[0] = Arena(HOLE0)
        b_resm = S_.buf("resm")
        QTm = [V(A.take(QG * 2), BF16, [128, QG]) for _ in range(2)]
        PT = [V(A.take(1024 * 2), BF16, [128, 1024]) for _ in range(3)]
        rdn = V(A.take(QG * 4), F32, [128, QG])
        bcs = V(A.take(QG * 4), F32, [128, QG])
        ohT = [V(A.take(QG * 2), BF16, [128, QG]) for _ in range(2)]
        qt1 = V(A.take(QG * 4), F32, [128, QG])
        qt2 = V(A.take(QG * 4), F32, [128, QG])
        VAm = [V(hole_take(A, NT * 66 * 2), BF16, [128, NT, 66]) for _ in range(2)]
        rh = V(A.take(QG * 2), BF16, [128, QG])
        rl = V(A.take(QG * 2), BF16, [128, QG])
        b_rh = S_.buf("rh")
        cosf = V(hole_take(A, S * 4), F32, [128, S])
        sinf = V(hole_take(A, S * 4), F32, [128, S])
        b_qt1, b_qt2, b_tabf = S_.buf("qt1"), S_.buf("qt2"), S_.buf("tabf")
        WqA = V(hole_take(A, 2 * 768 * 2), BF16, [128, 2, 768])
        WqB = V(hole_take(A, 2 * 768 * 2), BF16, [128, 2, 768])
        Wk = V(hole_take(A, 512 * 2), BF16, [128, 512])
        Wv = V(hole_take(A, 512 * 2), BF16, [128, 512])
        gcol = V(A.take(64), F32, [128, 4])
        wq_st = V(A.take(2 * 768 * 4), F32, [128, 2, 768])
        b_wq, b_gcol, b_wstg3 = S_.buf("wq"), S_.buf("gcol"), S_.buf("wstg3")
        S_.dma("sp", lambda e: e.dma_start(out=gcol[:, 0:2], in_=gql_d), b_gcol, w=[b_gcol])
        S_.dma("sp", lambda e: e.dma_start(out=gcol[:, 2:3], in_=gkvl_d), b_gcol, w=[b_gcol])
        for (src, dst) in ((wuq_d, WqA), (wuqr_d, WqB)):
            S_.dma("sp", lambda e, src=src: e.dma_start(out=wq_st, in_=src.rearrange("(c p) n -> p c n", p=128)),
                   b_wstg3, w=[b_wstg3])
            for c in range(2):
                S_.op("dve", lambda e, c=c, dst=dst: e.tensor_scalar(out=dst[:, c, :], in0=wq_st[:, c, :],
                                                                     scalar1=gcol[:, c:c + 1], scalar2=None, op0=ALU.mult),
                      r=[b_wstg3, b_gcol], w=[b_wq])
        for (src, dst) in ((wuk_d, Wk), (wuv_d, Wv)):
            S_.dma("sp", lambda e, src=src: e.dma_start(out=wq_st[:, 0, 0:512], in_=src), b_wstg3, w=[b_wstg3])
            S_.op("dve", lambda e, dst=dst: e.tensor_scalar(out=dst, in0=wq_st[:, 0, 0:512], scalar1=gcol[:, 2:3],
                                                            scalar2=None, op0=ALU.mult), r=[b_wstg3, b_gcol], w=[b_wq])
        b_VAm = [S_.buf("VAm%d" % i) for i in range(2)]
        S_.dma("sp", lambda e: e.dma_start(out=cosf[64:96, :], in_=cosf_d), b_tabf, w=[b_tabf])
        S_.dma("sp", lambda e: e.dma_start(out=sinf[64:96, :], in_=sinf_d), b_tabf, w=[b_tabf])
        for i in range(2):
            S_.op("pool", lambda e, i=i: e.memset(VAm[i], 1.0), w=[b_VAm[i]])
        b_QTm = [S_.buf("QTm%d" % i) for i in range(2)]
        for i in range(2):
            S_.op("pool", lambda e, i=i: e.memset(QTm[i], 0.0), w=[b_QTm[i]])
        b_KTm = [S_.buf("KTm%d" % i) for i in range(2)]
        b_PT = [S_.buf("PT%d" % i) for i in range(3)]
        b_rdn, b_bcs = S_.buf("rdn"), S_.buf("bcs")
        b_ohT = [S_.buf("ohT%d" % i) for i in range(2)]
        b_pS = [S_.buf("pS%d" % i) for i in range(2)]
        b_pO = [S_.buf("pO%d" % i) for i in range(2)]
        b_pM = [S_.buf("pM%d" % i) for i in range(2)]
        pS = [pall[:, i * 1024:(i + 1) * 1024] for i in range(2)]
        pO = [PB(4 + i) for i in range(2)]
        pM = [PB(6 + i) for i in range(2)]
        items = [(h, qg) for h in range(HM + HG) for qg in range(NQG)]
        sc_m = 96.0 ** -0.5
        sc_g = 64.0 ** -0.5

        def prep_k_steps(h):
            kb = h % 2
            subs = []
            for qg in range(NQG):
                def sub(qg=qg):
                    m = qg % 2
                    cs = slice(qg * QG, (qg + 1) * QG)
                    S_.op("pe", lambda e: e.matmul(pM[m][0:64, :], lhsT=Wk[:, h * 64:(h + 1) * 64], rhs=kvnT[:, cs],
                                                   start=True, stop=True), r=[b_resm, b_wq], w=[b_pM[m]])
                    S_.op("dve", lambda e: e.tensor_copy(out=KTm[kb][0:64, cs], in_=pM[m][0:64, :]),
                          r=[b_pM[m]], w=[b_KTm[kb]])
                subs.append(sub)
            nb8 = min(8, NT)
            for g8 in range(NT // nb8):
                def sub(g8=g8):
                    m = g8 % 2

                    def vmm(e):
                        for u in range(nb8):
                            blk = g8 * nb8 + u
                            ins = e.matmul(pM[m][:, u * 64:(u + 1) * 64], lhsT=kvnT[:, blk * 128:(blk + 1) * 128],
                                           rhs=Wv[:, h * 64:(h + 1) * 64], start=True, stop=True)
                        return ins
                    S_.op("pe", vmm, r=[b_resm, b_wq], w=[b_pM[m]])
                    S_.op("dve", lambda e: e.tensor_copy(out=VAm[kb][:, g8 * nb8:(g8 + 1) * nb8, 0:64],
                                                         in_=pM[m][:, 0:nb8 * 64].rearrange("p (u d) -> p u d", d=64)),
                          r=[b_pM[m]], w=[b_VAm[kb]])
                subs.append(sub)
            return subs

        def prep_k(h):
            for sub in prep_k_steps(h):
                sub()

        def prep_q(idx):
            h, qg = items[idx]
            if h >= HM:
                return
            qb = idx % 2
            cs = slice(qg * QG, (qg + 1) * QG)

            def mm(e, h=h, cs=cs):
                for (m, W) in ((0, WqA), (1, WqB)):
                    for c in range(2):
                        ins = e.matmul(pM[m][0:96, :], lhsT=W[:, c, h * 96:(h + 1) * 96], rhs=qnT[:, c, cs],
                                       start=(c == 0), stop=(c == 1))
                return ins
            S_.op("pe", mm, r=[b_resm, b_wq], w=[b_pM[0], b_pM[1]])
            S_.op("dve", lambda e, qb=qb: e.tensor_copy(out=QTm[qb][0:64, :], in_=pM[0][0:64, :]),
                  r=[b_pM[0]], w=[b_QTm[qb]])
            S_.op("dve", lambda e, cs=cs: e.tensor_tensor(out=qt1[64:96, :], in0=pM[0][64:96, :], in1=cosf[64:96, cs], op=ALU.mult),
                  r=[b_pM[0], b_tabf], w=[b_qt1])
            S_.op("dve", lambda e, cs=cs: e.tensor_tensor(out=qt2[64:96, :], in0=pM[1][64:96, :], in1=sinf[64:96, cs], op=ALU.mult),
                  r=[b_pM[1], b_tabf], w=[b_qt2])
            S_.op("dve", lambda e, qb=qb: e.tensor_tensor(out=QTm[qb][64:96, :], in0=qt1[64:96, :], in1=qt2[64:96, :], op=ALU.add),
                  r=[b_qt1, b_qt2], w=[b_QTm[qb]])

        prep_k(0)
        prep_q(0)
        if debug:
            d1 = nc.dram_tensor("dbg_KTm0", [128, S], BF16, kind="ExternalOutput").ap()
            d2 = nc.dram_tensor("dbg_QTm0", [128, QG], BF16, kind="ExternalOutput").ap()
            d3 = nc.dram_tensor("dbg_VAm0", [128, NT, 66], BF16, kind="ExternalOutput").ap()
            S_.dma("sp", lambda e: e.dma_start(out=d1, in_=KTm[0]), b_dbg, r=[b_KTm[0], b_res])
            S_.dma("sp", lambda e: e.dma_start(out=d2, in_=QTm[0]), b_dbg, r=[b_QTm[0]])
            S_.dma("sp", lambda e: e.dma_start(out=d3, in_=VAm[0]), b_dbg, r=[b_VAm[0]])
            S_.barrier()
        gctr = 0
        pend_fin = None

        def fin_b(pf):
            fidx, fh, fcs, fpo = pf
            ob = fidx % 2

            def bmm(e):
                e.matmul(pM[0][0:64, :], lhsT=onesb[64:65, 0:64], rhs=rh[64:65, :], start=True, stop=False)
                return e.matmul(pM[0][0:64, :], lhsT=onesb[64:65, 0:64], rhs=rl[64:65, :], start=False, stop=True)
            S_.op("pe", bmm, r=[b_rh, b_const], w=[b_pM[0]])
            S_.op("dve", lambda e: e.tensor_copy(out=bcs[0:64, :], in_=pM[0][0:64, :]), r=[b_pM[0]], w=[b_bcs])
            S_.op("dve", lambda e, fpo=fpo, ob=ob: e.tensor_tensor(out=ohT[ob][0:64, :], in0=pO[fpo][0:64, :], in1=bcs[0:64, :],
                                                                   op=ALU.mult), r=[b_pO[fpo], b_bcs], w=[b_ohT[ob]])
            S_.dma("sp", lambda e, fh=fh, fcs=fcs, ob=ob: e.dma_start(out=otscr[fh * 64:(fh + 1) * 64, fcs], in_=ohT[ob][0:64, :]),
                   b_ohT[ob], r=[b_ohT[ob]])

        npair = NT // 2

        def item_cfg(idx):
            h, qg = items[idx]
            cs = slice(qg * QG, (qg + 1) * QG)
            c = dict(h=h, cs=cs, po=idx % 2, idx=idx)
            if h < HM:
                kb, qb = h % 2, idx % 2
                c.update(kt_ap=lambda blk: KTm[kb][:, blk * 128:(blk + 1) * 128], q_ap=QTm[qb][:, :],
                         v_ap=lambda blk: VAm[kb][:, blk, 0:65], rbufs=[b_KTm[kb], b_QTm[qb], b_resm],
                         vbuf=b_VAm[kb], scale=sc_m)
            else:
                hg = h - HM
                var = (hg // 4) * 2 + hg % 2
                kv = hg // 4
                c.update(kt_ap=lambda blk: KTg[:, var, blk * 128:(blk + 1) * 128], q_ap=QTg[:, hg // 2, cs],
                         v_ap=lambda blk: VAg[:, blk, kv, 0:65], rbufs=[b_res], vbuf=b_res, scale=sc_g)
            return c

        def emit_pv(c, pi, ppt):
            def pvmm(e):
                for u in range(2):
                    blk = 2 * pi + u
                    ins = e.matmul(pO[c["po"]][0:65, :], lhsT=c["v_ap"](blk), rhs=PT[ppt][:, u * 512:(u + 1) * 512],
                                   start=(blk == 0), stop=(blk == NT - 1))
                return ins
            S_.op("pe", pvmm, r=[b_PT[ppt], c["vbuf"]], w=[b_pO[c["po"]]])

        def fin_a(c):
            po = c["po"]
            S_.op("dve", lambda e: e.reciprocal(out=rdn[64:65, :], in_=pO[po][64:65, :]), r=[b_pO[po]], w=[b_rdn])
            S_.op("dve", lambda e: e.tensor_copy(out=rh[64:65, :], in_=rdn[64:65, :]), r=[b_rdn], w=[b_rh])
            S_.op("dve", lambda e: e.tensor_tensor(out=rl[64:65, :], in0=rdn[64:65, :], in1=rh[64:65, :], op=ALU.subtract),
                  r=[b_rdn, b_rh], w=[b_rh])
            return (c["idx"], c["h"], c["cs"], po)

        pending_prep = []
        cfgs = {}
        W2GU, W2D = ffn_weight_views(PERS)
        w2steps, w2bufs = ffn_weight_loads("f2", w2gu_d, w2d_d, W2GU, W2D)
        n_gu = 2 * DFF // 512
        mla_dead = [b_resm, b_KTm[0], b_KTm[1], b_VAm[0], b_VAm[1], b_tabf, b_wq]
        w2_gu_steps = []
        if HOLE_END == PERS + WGU_BYTES and HOLE0 <= HOLE_END:
            wgu_v2 = w2gu_d.rearrange("(k p) n -> p k n", p=128)
            for cg in range(n_gu):
                w2steps.pop(0)
                def st(cg=cg, b=w2bufs[cg % 4]):
                    S_.dma("pool", lambda e: e.dma_start(out=W2GU[:, :, cg * 512:(cg + 1) * 512],
                                                         in_=wgu_v2[:, :, cg * 512:(cg + 1) * 512]), b, w=[b] + mla_dead)
                w2_gu_steps.append(st)

        def cfg_of(idx):
            if idx not in cfgs:
                cfgs[idx] = item_cfg(idx)
            return cfgs[idx]

        nsteps = len(items) * npair

        def emit_S(k):
            idx, i = k // npair, k % npair
            c = cfg_of(idx)
            sb = k % 2

            def smm(e):
                for u in range(2):
                    ins = e.matmul(pS[sb][:, u * 512:(u + 1) * 512], lhsT=c["kt_ap"](2 * i + u), rhs=c["q_ap"],
                                   start=True, stop=True)
                return ins
            S_.op("pe", smm, r=c["rbufs"], w=[b_pS[sb]])

        def emit_exp(k):
            c = cfg_of(k // npair)
            sb, pt = k % 2, k % 3
            S_.op("act", lambda e: e.activation(out=PT[pt], in_=pS[sb], func=AF.Exp, scale=c["scale"]),
                  r=[b_pS[sb]], w=[b_PT[pt]])

        emit_S(0)
        for k in range(nsteps):
            idx, i = k // npair, k % npair
            h, qg = items[idx]
            if i == 0:
                if idx + 1 < len(items):
                    prep_q(idx + 1)
                if qg == 0 and h + 1 < HM:
                    pending_prep = prep_k_steps(h + 1)
            emit_exp(k)
            if k + 1 < nsteps:
                emit_S(k + 1)
            if k >= 1:
                pidx, pi = (k - 1) // npair, (k - 1) % npair
                emit_pv(cfg_of(pidx), pi, (k - 1) % 3)
                if pi == npair - 1:
                    pend_fin = fin_a(cfg_of(pidx))
            if i == min(8, npair - 1) and pend_fin is not None:
                fin_b(pend_fin)
                pend_fin = None
            if w2_gu_steps and h >= HM and (idx - HM * NQG) >= 1:
                w2_gu_steps.pop(0)()
            if pending_prep and (i >= 4 or i == npair - 1):
                n_emit = 1 if i < npair - 1 else (len(pending_prep) if qg == NQG - 1 else 1)
                for _ in range(n_emit):
                    pending_prep.pop(0)()
        emit_pv(cfg_of((nsteps - 1) // npair), (nsteps - 1) % npair, (nsteps - 1) % 3)
        pend_fin = fin_a(cfg_of((nsteps - 1) // npair))
        fin_b(pend_fin)
        while w2_gu_steps:
            w2_gu_steps.pop(0)()
        S_.barrier()

        A = Arena(PERS + W_BYTES)
        Wo = V(A.take(8 * D * 2), BF16, [128, 8, D])
        Gm_ = V(A.take(4096), F32, [128, D])
        goc = V(A.take(64), F32, [128, 8])
        wos = V(A.take(16384), F32, [128, 4, D])
        x3b = [V(A.take(4096), F32, [128, D]) for _ in range(2)]
        oTg = [V(A.take(8 * QG * 2), BF16, [128, 8, QG]) for _ in range(2)]
        tq = V(A.take(512 * 4), F32, [128, 512])
        dg = V(A.take(256 * 4), F32, [128, 256])
        st3 = [V(A.take(32), F32, [128, 8]) for _ in range(2)]
        b_Wo, b_Gm, b_goc, b_wos = S_.buf("Wo"), S_.buf("Gm"), S_.buf("goc"), S_.buf("wos")
        b_x3 = [S_.buf("x3_%d" % i) for i in range(2)]
        b_oTg = [S_.buf("oTg%d" % i) for i in range(2)]
        b_tq, b_dg = S_.buf("tq"), S_.buf("dg")
        b_st3 = [S_.buf("st3_%d" % i) for i in range(2)]
        b_pa = [S_.buf("pa%d" % i) for i in range(2)]
        b_pb = [S_.buf("pb%d" % i) for i in range(2)]
        b_pg = S_.buf("pgram")
        pa = [PB(i) for i in range(2)]
        pbb = [PB(2 + i) for i in range(2)]
        pg = PB(4)
        S_.dma("sp", lambda e: e.dma_start(out=Gm_, in_=modscr[5]), b_Gm, w=[b_Gm])
        S_.dma("sp", lambda e: e.dma_start(out=goc, in_=gout_d), b_goc, w=[b_goc])
        wout_v = wout_d.rearrange("(c p) n -> p c n", p=128)
        for c0 in (0, 4):
            S_.dma("sp", lambda e, c0=c0: e.dma_start(out=wos, in_=wout_v[:, c0:c0 + 4, :]), b_wos, w=[b_wos])
            for c in range(c0, c0 + 4):
                S_.op("dve",
                      lambda e, c=c, c0=c0: e.scalar_tensor_tensor(out=Wo[:, c, :], in0=wos[:, c - c0, :], scalar=goc[:, c:c + 1],
                                                                   in1=Gm_, op0=ALU.mult, op1=ALU.mult),
                      r=[b_wos, b_goc, b_Gm], w=[b_Wo])
        S_.barrier()
        x3b = x3b + [wos[:, 0, :], wos[:, 1, :]]
        b_x3 = b_x3 + [S_.buf("x3_2"), S_.buf("x3_3")]
        NX3 = len(x3b)
        otv = otscr.rearrange("(c p) n -> p c n", p=128)
        tq2 = [tq, V(A.take(512 * 4), F32, [128, 512])]
        tq4 = [[tq2[0], V(A.take(512 * 4), F32, [128, 512])], [tq2[1], V(A.take(512 * 4), F32, [128, 512])]]
        b_tq4 = [[S_.buf("tq4_%d%d" % (i, j)) for j in range(2)] for i in range(2)]
        dg2 = [dg, V(A.take(256 * 4), F32, [128, 256])]
        b_tq2 = [b_tq, S_.buf("tq1")]
        b_dg2 = [b_dg, S_.buf("dg1")]
        pg2 = [PB(4), PB(5)]
        b_pg2 = [b_pg, S_.buf("pgram1")]

        def p35_load(t):
            qg, tt = t // 4, t % 4
            og = qg % 2
            xq = t % NX3
            S_.dma("sp", lambda e: e.dma_start(out=x3b[xq], in_=x1scr[t * 128:(t + 1) * 128, :]), b_x3[xq], w=[b_x3[xq]])

        def p35_compute(t):
            qg, tt = t // 4, t % 4
            og = qg % 2
            xs = t % 2
            st = st3[xs]
            bs = b_st3[xs]
            tcs = slice(tt * 128, (tt + 1) * 128)
            pgx, tqx, dgx = pg2[xs], tq2[xs], dg2[xs]

            def gram(e):
                for grp in range(2):
                    for c in range(4):
                        cc = grp * 4 + c
                        ins = e.matmul(pgx[:, grp * 128:(grp + 1) * 128], lhsT=oTg[og][:, cc, tcs], rhs=oTg[og][:, cc, tcs],
                                       start=(c == 0), stop=(c == 3))
                return ins
            S_.op("pe", gram, r=[b_oTg[og]], w=[b_pg2[xs]])
            for grp in range(2):
                S_.op("dve", lambda e, grp=grp: e.tensor_tensor(out=dgx[:, grp * 128:(grp + 1) * 128],
                                                                in0=pgx[:, grp * 128:(grp + 1) * 128], in1=identf, op=ALU.mult),
                      r=[b_pg2[xs], b_const], w=[b_dg2[xs]])
            S_.op("dve", lambda e: e.tensor_reduce(out=st[:, 0:2], in_=dgx.rearrange("p (g d) -> p g d", d=128),
                                                   axis=AX.X, op=ALU.add), r=[b_dg2[xs]], w=[bs])
            S_.op("act", lambda e: e.activation(out=st[:, 2:4], in_=st[:, 0:2], func=AF.Sqrt, bias=epsT[:, 0:1],
                                                scale=1.0 / 512.0), r=[bs], w=[bs])
            S_.op("dve", lambda e: e.reciprocal(out=st[:, 4:6], in_=st[:, 2:4]), r=[bs], w=[bs])

        def p35_out(t):
            qg, tt = t // 4, t % 4
            og = qg % 2
            xs = t % 2
            st = st3[xs]
            bs = b_st3[xs]
            tcs = slice(tt * 128, (tt + 1) * 128)
            xq = t % NX3
            for hf in range(2):
                hs = slice(hf * 512, (hf + 1) * 512)

                def omm(e, hs=hs, hf=hf):
                    for c in range(4):
                        e.matmul(pa[hf], lhsT=oTg[og][:, c, tcs], rhs=Wo[:, c, hs], start=(c == 0), stop=(c == 3))
                    for c in range(4):
                        ins = e.matmul(pbb[hf], lhsT=oTg[og][:, 4 + c, tcs], rhs=Wo[:, 4 + c, hs], start=(c == 0), stop=(c == 3))
                    return ins
                S_.op("pe", omm, r=[b_oTg[og], b_Wo], w=[b_pa[hf], b_pb[hf]])
                tqh = tq4[xs][hf]
                btq = b_tq4[xs][hf]
                S_.op("act", lambda e, hf=hf, tqh=tqh: e.activation(out=tqh, in_=pa[hf], func=AF.Copy, scale=st[:, 4:5]),
                      r=[b_pa[hf], bs], w=[btq])
                S_.op("dve", lambda e, hf=hf, tqh=tqh: e.scalar_tensor_tensor(out=tqh, in0=pbb[hf], scalar=st[:, 5:6], in1=tqh,
                                                                              op0=ALU.mult, op1=ALU.add),
                      r=[b_pb[hf], bs], w=[btq])
                S_.op("pool", lambda e, hs=hs, tqh=tqh: e.tensor_tensor(out=x3b[xq][:, hs], in0=x3b[xq][:, hs], in1=tqh, op=ALU.add),
                      r=[btq], w=[b_x3[xq]])
            S_.dma("sp", lambda e: e.dma_start(out=x1scr[t * 128:(t + 1) * 128, :], in_=x3b[xq]), b_x3[xq], r=[b_x3[xq]])

        def otg_load(q2):
            if q2 < NQG:
                S_.dma("sp", lambda e: e.dma_start(out=oTg[q2 % 2], in_=otv[:, :, q2 * QG:(q2 + 1) * QG]),
                       b_oTg[q2 % 2], w=[b_oTg[q2 % 2]])

        otg_load(0)
        otg_load(1)
        p35_load(0)
        if NT > 1:
            p35_load(1)
        p35_compute(0)
        for t in range(NT):
            if t + 2 < NT:
                p35_load(t + 2)
            if t + 1 < NT:
                p35_compute(t + 1)
            p35_out(t)
            if t % 4 == 3 and t + 1 < NT:
                otg_load(t // 4 + 2)
            for _ in range(max(1, (len(w2steps) + NT - 1) // NT) if w2steps else 0):
                if w2steps:
                    w2steps.pop(0)()
        while w2steps:
            w2steps.pop(0)()
        S_.barrier()

        ffn_phase("f2", w2gu_d, w2d_d, 7, 6, 8, x1scr, True, preloaded=True)

        S_.emit(nc, stack)
    return nc


def _tables(S):
    tok = np.arange(S)
    row = (tok // GRID_W).astype(np.float64)
    col = (tok % GRID_W).astype(np.float64)

    def tab(dim):
        q = dim // 4
        axis_dim = dim // 2
        inv = THETA ** (-(np.arange(q, dtype=np.float64) * 2.0 / axis_dim))
        inv = inv.astype(np.float32).astype(np.float64)
        ar = (row[:, None].astype(np.float32) * inv[None, :].astype(np.float32)).astype(np.float64)
        ac = (col[:, None].astype(np.float32) * inv[None, :].astype(np.float32)).astype(np.float64)
        cos = np.concatenate([np.cos(ar), np.cos(ar), np.cos(ac), np.cos(ac)], axis=1)
        sin = np.concatenate([-np.sin(ar), np.sin(ar), -np.sin(ac), np.sin(ac)], axis=1)
        return cos.astype(np.float32), sin.astype(np.float32)

    cosm, sinm = tab(32)
    cosg, sing = tab(64)
    return cosm, sinm, cosg, sing


_ROT32 = np.concatenate([np.arange(8, 16), np.arange(0, 8), np.arange(24, 32), np.arange(16, 24)])


def make_in_maps(inp, S):
    B = inp["x"].shape[0]
    f = lambda a: np.ascontiguousarray(np.asarray(a, dtype=np.float32))
    cosm, sinm, cosg, sing = _tables(S)
    w_uq = f(inp["w_uq"][0])
    idx = np.arange(768).reshape(8, 96)
    idx_rot = idx.copy()
    idx_rot[:, 64:96] = idx[:, 64:96][:, _ROT32]
    w_uq_rot = np.ascontiguousarray(w_uq[:, idx_rot.reshape(-1)])
    w_ukv = f(inp["w_ukv"][0]).reshape(128, 8, 128)
    shared = {
        "w_ada": f(inp["w_ada"][0]), "b_ada": f(inp["b_ada"][0]).reshape(1, -1),
        "gvecs": f(np.stack([inp["g_ffn1"][0], inp["g_mix"][0], inp["g_ffn2"][0], inp["g_final"]])),
        "w1_gu": f(inp["w1_gu"][0]), "w1_down": f(inp["w1_down"][0]),
        "w2_gu": f(inp["w2_gu"][0]), "w2_down": f(inp["w2_down"][0]),
        "w_in": f(inp["w_in"][0]), "w_uq": w_uq, "w_uq_rot": w_uq_rot,
        "w_ukv_k": f(w_ukv[:, :, 0:64].reshape(128, 512)), "w_ukv_v": f(w_ukv[:, :, 64:128].reshape(128, 512)),
        "g_q_lat_col": f(np.asarray(inp["g_q_lat"][0]).reshape(2, 128).T),
        "g_kv_lat_col": f(np.asarray(inp["g_kv_lat"][0]).reshape(1, 128).T),
        "g_qk_row": f(np.concatenate([np.tile(np.asarray(inp["g_qhead"][0]), 8), np.tile(np.asarray(inp["g_khead"][0]), 2)]).reshape(1, 640)),
        "g_out_col": f(np.concatenate([np.asarray(inp["g_out_mla"][0]), np.asarray(inp["g_out_gqa"][0])]).reshape(8, 128).T),
        "w_out": f(inp["w_out"][0]),
        "cosm": cosm, "sinm": sinm, "cosg": cosg, "sing": sing,
        "cosf": f(cosm.T), "sinf": f(sinm.T),
    }
    maps = []
    x = np.asarray(inp["x"], dtype=np.float32)
    c = np.asarray(inp["c"], dtype=np.float32)
    for b in range(B):
        m = dict(shared)
        m["x"] = np.ascontiguousarray(x[b])
        m["c_col"] = np.ascontiguousarray(c[b].reshape(8, 128).T)
        maps.append(m)
    return maps


_CACHE = {}


def kernel(**inputs):
    x = np.asarray(inputs["x"])
    B, S, _ = x.shape
    if S not in _CACHE:
        _CACHE[S] = build_program(S)
    nc = _CACHE[S]
    in_maps = make_in_maps(inputs, S)
    res = run_bass_kernel_spmd(nc, in_maps, core_ids=list(range(B)))
    return np.stack([np.asarray(r["y"], dtype=np.float32) for r in res.results], axis=0)
```
